# Optimizing a Trainium2 kernel written in Bass

```python
import math
import jax, jax.numpy as jnp
from jax import lax
import numpy as np

D_MODEL = 2048
BATCH = 4
SEQ = 2048
DEPTH = 4

RET_HEADS = 4
RET_DIM = 128
FOX_HEADS = 4
FOX_DIM = 128
DSA_HEADS = 4
DSA_DIM = 128
DSA_Q_RANK = 512
IDX_HEADS = 16
IDX_DIM = 64
DSA_TOPK = 256
SSD_HEADS = 16
SSD_HEAD_DIM = 64
SSD_GROUPS = 2
SSD_STATE = 128
SSD_CONV = 4
SSD_INNER = SSD_HEADS * SSD_HEAD_DIM
SSD_XBC = SSD_INNER + 2 * SSD_GROUPS * SSD_STATE
D_FF = 4 * D_MODEL
N_BUCKETS = 32
MAX_DISTANCE = 128
Q_BLOCK = 128
CHUNK = 128
EPS = 1e-6
N_BRANCH = 4
BRANCH_WIDTHS = (RET_HEADS * RET_DIM, FOX_HEADS * FOX_DIM, DSA_HEADS * DSA_DIM, SSD_INNER)
IN_SIZES = (
    RET_HEADS * RET_DIM, RET_HEADS * RET_DIM, RET_HEADS * RET_DIM, RET_HEADS * RET_DIM,
    FOX_HEADS * FOX_DIM, FOX_HEADS * FOX_DIM, FOX_HEADS * FOX_DIM, FOX_HEADS,
    DSA_Q_RANK, DSA_DIM, DSA_DIM, IDX_DIM, IDX_HEADS,
    SSD_INNER, SSD_XBC, SSD_HEADS,
    N_BRANCH * D_MODEL,
)
IN_TOTAL = sum(IN_SIZES)
MIX_TOTAL = sum(BRANCH_WIDTHS)

kernel_name = "hybrid_gated_parallel_mixers"


def rms_norm(x, g):
    xf = x.astype(jnp.float32)
    y = xf * lax.rsqrt(jnp.mean(xf * xf, axis=-1, keepdims=True) + EPS)
    return (y * g.astype(jnp.float32)).astype(x.dtype)


def split_axis(t, sizes, axis):
    return jnp.split(t, np.cumsum(sizes)[:-1].tolist(), axis=axis)


def heads(t, n):
    return t.reshape(t.shape[0], t.shape[1], n, -1)


def rotary(x, pos):
    half = x.shape[-1] // 2
    inv = 1.0 / (10000.0 ** (jnp.arange(half, dtype=jnp.float32) / half))
    ang = pos.astype(jnp.float32)[:, None] * inv[None, :]
    cos, sin = jnp.cos(ang)[:, None, :], jnp.sin(ang)[:, None, :]
    x1, x2 = x[..., :half], x[..., half:]
    return jnp.concatenate([x1 * cos - x2 * sin, x1 * sin + x2 * cos], axis=-1)


def t5_bucket(dist):
    max_exact = N_BUCKETS // 2
    d = jnp.maximum(dist, 0)
    log_ratio = jnp.log(jnp.maximum(d, 1).astype(jnp.float32) / max_exact) / math.log(MAX_DISTANCE / max_exact)
    large = jnp.minimum(max_exact + (log_ratio * (N_BUCKETS - max_exact)).astype(jnp.int32), N_BUCKETS - 1)
    return jnp.where(d < max_exact, d, large)


def retention(q, k, v, g):
    B, S, H, Dh = q.shape
    n = S // CHUNK
    pos = jnp.arange(S)
    q = rotary(q.astype(jnp.float32), pos)
    k = rotary(k.astype(jnp.float32), pos) * (Dh ** -0.5)
    v = v.astype(jnp.float32)
    log_gamma = jnp.log1p(-jnp.exp2(-5.0 - jnp.arange(H, dtype=jnp.float32)))
    qc = q.reshape(B, n, CHUNK, H, Dh)
    kc = k.reshape(B, n, CHUNK, H, Dh)
    vc = v.reshape(B, n, CHUNK, H, Dh)
    i = jnp.arange(CHUNK, dtype=jnp.float32)
    rel = i[:, None] - i[None, :]
    decay = jnp.where(rel >= 0, jnp.exp(log_gamma[:, None, None] * jnp.maximum(rel, 0.0)), 0.0)
    scores = jnp.einsum('bnihd,bnjhd->bnhij', qc, kc) * decay
    y_inner = jnp.einsum('bnhij,bnjhd->bnihd', scores, vc)
    k_dec = jnp.exp(log_gamma[None, :] * (CHUNK - 1.0 - i)[:, None])
    q_dec = jnp.exp(log_gamma[None, :] * (i + 1.0)[:, None])
    chunk_kv = jnp.einsum('bnjhd,jh,bnjhe->bnhde', kc, k_dec, vc)
    chunk_decay = jnp.exp(log_gamma * CHUNK)[None, :, None, None]

    def step(state, kv):
        return chunk_decay * state + kv, state

    _, prev = lax.scan(step, jnp.zeros((B, H, Dh, Dh), jnp.float32), jnp.moveaxis(chunk_kv, 1, 0))
    prev = jnp.moveaxis(prev, 0, 1)
    y_cross = jnp.einsum('bnihd,ih,bnhde->bnihe', qc, q_dec, prev)
    y = (y_inner + y_cross).reshape(B, S, H, Dh)
    yc = y - jnp.mean(y, axis=-1, keepdims=True)
    y = yc * lax.rsqrt(jnp.mean(yc * yc, axis=-1, keepdims=True) + EPS)
    out = jax.nn.silu(g.astype(jnp.float32)) * y
    return out.reshape(B, S, H * Dh)


def forgetting_attention(q, k, v, f_logit, qn_g, kn_g):
    B, S, H, Dh = q.shape
    nb = S // Q_BLOCK
    q = rms_norm(q.astype(jnp.float32), qn_g) * (Dh ** -0.5)
    k = rms_norm(k.astype(jnp.float32), kn_g)
    v = v.astype(jnp.float32)
    F = jnp.cumsum(jax.nn.log_sigmoid(f_logit.astype(jnp.float32)), axis=1)
    FT = F.transpose(0, 2, 1)
    qb = q.reshape(B, nb, Q_BLOCK, H, Dh).transpose(1, 0, 2, 3, 4)
    Fb = F.reshape(B, nb, Q_BLOCK, H).transpose(1, 0, 3, 2)
    tb = jnp.arange(S).reshape(nb, Q_BLOCK)
    s_idx = jnp.arange(S)

    def block(args):
        qi, Fi, ti = args
        logits = jnp.einsum('bqhd,bshd->bhqs', qi, k) + (Fi[..., None] - FT[:, :, None, :])
        logits = jnp.where((s_idx[None, :] <= ti[:, None])[None, None], logits, -jnp.inf)
        p = jax.nn.softmax(logits, axis=-1)
        return jnp.einsum('bhqs,bshd->bqhd', p, v)

    out = lax.map(block, (qb, Fb, tb))
    return out.transpose(1, 0, 2, 3, 4).reshape(B, S, H * Dh)


def sparse_attention(c_q, k, v, idx_k, idx_w, cq_g, w_uq, w_qidx, qn_g, kn_g, rel_bias):
    B, S, _ = c_q.shape
    nb = S // Q_BLOCK
    topk = min(DSA_TOPK, S // 4)
    cq = rms_norm(c_q, cq_g)
    q = rms_norm((cq @ w_uq).reshape(B, S, DSA_HEADS, DSA_DIM).astype(jnp.float32), qn_g) * (DSA_DIM ** -0.5)
    q_idx = (cq @ w_qidx).reshape(B, S, IDX_HEADS, IDX_DIM).astype(jnp.float32) * (IDX_DIM ** -0.5)
    w_h = idx_w.astype(jnp.float32) * (IDX_HEADS ** -0.5)
    k_idx = idx_k.astype(jnp.float32)
    k = rms_norm(k.astype(jnp.float32), kn_g)
    v = v.astype(jnp.float32)
    bias_table = rel_bias.astype(jnp.float32)
    qb = q.reshape(B, nb, Q_BLOCK, DSA_HEADS, DSA_DIM).transpose(1, 0, 2, 3, 4)
    qib = q_idx.reshape(B, nb, Q_BLOCK, IDX_HEADS, IDX_DIM).transpose(1, 0, 2, 3, 4)
    wb = w_h.reshape(B, nb, Q_BLOCK, IDX_HEADS).transpose(1, 0, 2, 3)
    tb = jnp.arange(S).reshape(nb, Q_BLOCK)
    s_idx = jnp.arange(S)
    gather = jax.vmap(lambda kb, sb: kb[sb])

    def block(args):
        qi, qidx_i, wi, ti = args
        score = jax.nn.relu(jnp.einsum('bqhd,bsd->bqhs', qidx_i, k_idx))
        score = jnp.einsum('bqh,bqhs->bqs', wi, score)
        score = jnp.where((s_idx[None, :] <= ti[:, None])[None], score, -jnp.inf)
        _, sel = lax.top_k(score, topk)
        valid = sel <= ti[None, :, None]
        k_sel = gather(k, sel)
        v_sel = gather(v, sel)
        bias = bias_table[t5_bucket(ti[None, :, None] - sel)].transpose(0, 3, 1, 2)
        logits = jnp.einsum('bqhd,bqkd->bhqk', qi, k_sel) + bias
        logits = jnp.where(valid[:, None], logits, -jnp.inf)
        p = jax.nn.softmax(logits, axis=-1)
        return jnp.einsum('bhqk,bqkd->bqhd', p, v_sel)

    out = lax.map(block, (qb, qib, wb, tb))
    return out.transpose(1, 0, 2, 3, 4).reshape(B, S, DSA_HEADS * DSA_DIM)


def segsum_exp(a_cs):
    L = a_cs.shape[-1]
    mask = jnp.tril(jnp.ones((L, L), dtype=bool))
    diff = a_cs[..., :, None] - a_cs[..., None, :]
    return jnp.where(mask, jnp.exp(jnp.where(mask, diff, 0.0)), 0.0)


def ssd_mixer(z, xbc, dt_raw, conv_w, conv_b, dt_bias, a_log, d_skip, norm_g):
    B, S, _ = xbc.shape
    G, R, P, N = SSD_GROUPS, SSD_HEADS // SSD_GROUPS, SSD_HEAD_DIM, SSD_STATE
    n = S // CHUNK
    conv = lax.conv_general_dilated(
        xbc.astype(jnp.float32), conv_w.astype(jnp.float32)[:, None, :], window_strides=(1,),
        padding=[(SSD_CONV - 1, 0)], dimension_numbers=('NWC', 'WIO', 'NWC'),
        feature_group_count=SSD_XBC)
    xbc = jax.nn.silu(conv + conv_b.astype(jnp.float32))
    xs, Bm, Cm = split_axis(xbc, (SSD_INNER, G * N, G * N), axis=-1)
    xs = xs.reshape(B, S, G, R, P)
    dt = jax.nn.softplus(dt_raw.astype(jnp.float32) + dt_bias.astype(jnp.float32)).reshape(B, S, G, R)
    A = -jnp.exp(a_log.astype(jnp.float32)).reshape(G, R)
    xc = (xs * dt[..., None]).reshape(B, n, CHUNK, G, R, P)
    Bc = Bm.reshape(B, n, CHUNK, G, N)
    Cc = Cm.reshape(B, n, CHUNK, G, N)
    a = (dt * A).reshape(B, n, CHUNK, G, R).transpose(0, 3, 4, 1, 2)
    a_cs = jnp.cumsum(a, axis=-1)
    cb = jnp.einsum('bnlgk,bnsgk->bgnls', Cc, Bc)
    y_diag = jnp.einsum('bgnls,bgrnls,bnsgrp->bnlgrp', cb, segsum_exp(a_cs), xc)
    decay_states = jnp.exp(a_cs[..., -1:] - a_cs)
    states = jnp.einsum('bnlgk,bgrnl,bnlgrp->bngrpk', Bc, decay_states, xc)
    chunk_decay = jnp.exp(a_cs[..., -1])

    def step(state, inp):
        st, dec = inp
        return dec[..., None, None] * state + st, state

    _, prev = lax.scan(step, jnp.zeros((B, G, R, P, N), jnp.float32),
                       (jnp.moveaxis(states, 1, 0), jnp.moveaxis(chunk_decay, -1, 0)))
    prev = jnp.moveaxis(prev, 0, 1)
    y_off = jnp.einsum('bnlgk,bngrpk,bgrnl->bnlgrp', Cc, prev, jnp.exp(a_cs))
    y = (y_diag + y_off).reshape(B, S, G, R, P) + d_skip.astype(jnp.float32).reshape(G, R)[..., None] * xs
    gated = y.reshape(B, S, SSD_INNER) * jax.nn.silu(z.astype(jnp.float32))
    gated = rms_norm(gated.reshape(B, S, G, SSD_INNER // G), norm_g.reshape(G, SSD_INNER // G))
    return gated.reshape(B, S, SSD_INNER)


def setup_inputs(seed: int = 0) -> dict:
    key = jax.random.key(seed)
    ks = jax.random.split(key, 32)
    f32 = jnp.float32

    def nrm(k, shape, fan_in):
        return jax.random.normal(k, shape, f32) * (fan_in ** -0.5)

    def gain(k, shape):
        return 1.0 + 0.02 * jax.random.normal(k, shape, f32)

    dt0 = jnp.exp(jax.random.uniform(ks[16], (DEPTH, SSD_HEADS), f32, math.log(1e-3), math.log(1e-1)))
    br_keys = jax.random.split(ks[20], N_BRANCH)
    w_br = jnp.concatenate([nrm(br_keys[i], (DEPTH, w, D_MODEL), w) for i, w in enumerate(BRANCH_WIDTHS)], axis=1)
    return {
        "x": jax.random.normal(ks[0], (BATCH, SEQ, D_MODEL), f32),
        "norm1_g": gain(ks[1], (DEPTH, D_MODEL)),
        "w_in": nrm(ks[2], (DEPTH, D_MODEL, IN_TOTAL), D_MODEL),
        "gate_b": 0.02 * jax.random.normal(ks[3], (DEPTH, N_BRANCH * D_MODEL), f32),
        "fox_f_b": jax.random.uniform(ks[4], (DEPTH, FOX_HEADS), f32, 1.0, 5.0),
        "fox_qn_g": gain(ks[5], (DEPTH, FOX_DIM)),
        "fox_kn_g": gain(ks[6], (DEPTH, FOX_DIM)),
        "dsa_cq_g": gain(ks[7], (DEPTH, DSA_Q_RANK)),
        "dsa_w_uq": nrm(ks[8], (DEPTH, DSA_Q_RANK, DSA_HEADS * DSA_DIM), DSA_Q_RANK),
        "dsa_w_qidx": nrm(ks[9], (DEPTH, DSA_Q_RANK, IDX_HEADS * IDX_DIM), DSA_Q_RANK),
        "dsa_qn_g": gain(ks[10], (DEPTH, DSA_DIM)),
        "dsa_kn_g": gain(ks[11], (DEPTH, DSA_DIM)),
        "rel_bias": 0.5 * jax.random.normal(ks[12], (N_BUCKETS, DSA_HEADS), f32),
        "ssd_conv_w": jax.random.uniform(ks[13], (DEPTH, SSD_CONV, SSD_XBC), f32, -1.0, 1.0) * (SSD_CONV ** -0.5),
        "ssd_conv_b": 0.02 * jax.random.normal(ks[14], (DEPTH, SSD_XBC), f32),
        "ssd_dt_bias": dt0 + jnp.log(-jnp.expm1(-dt0)),
        "ssd_a_log": jnp.log(jax.random.uniform(ks[17], (DEPTH, SSD_HEADS), f32, 1.0, 16.0)),
        "ssd_d": 1.0 + 0.1 * jax.random.normal(ks[18], (DEPTH, SSD_HEADS), f32),
        "ssd_norm_g": gain(ks[19], (DEPTH, SSD_INNER)),
        "w_br": w_br,
        "w_out": nrm(ks[21], (DEPTH, D_MODEL, D_MODEL), D_MODEL),
        "norm2_g": gain(ks[22], (DEPTH, D_MODEL)),
        "w_ff1": nrm(ks[23], (DEPTH, D_MODEL, D_FF), D_MODEL),
        "w_ff2": nrm(ks[24], (DEPTH, D_FF, D_MODEL), D_FF),
    }


def reference(x, norm1_g, w_in, gate_b, fox_f_b, fox_qn_g, fox_kn_g, dsa_cq_g, dsa_w_uq, dsa_w_qidx,
              dsa_qn_g, dsa_kn_g, rel_bias, ssd_conv_w, ssd_conv_b, ssd_dt_bias, ssd_a_log, ssd_d,
              ssd_norm_g, w_br, w_out, norm2_g, w_ff1, w_ff2):
    B, S, _ = x.shape
    for l in range(DEPTH):
        h = rms_norm(x, norm1_g[l])
        (r_q, r_k, r_v, r_g, f_q, f_k, f_v, f_f, d_cq, d_k, d_v, i_k, i_w,
         s_z, s_xbc, s_dt, g_lin) = split_axis(h @ w_in[l], IN_SIZES, axis=-1)
        o_ret = retention(heads(r_q, RET_HEADS), heads(r_k, RET_HEADS), heads(r_v, RET_HEADS), heads(r_g, RET_HEADS))
        o_fox = forgetting_attention(heads(f_q, FOX_HEADS), heads(f_k, FOX_HEADS), heads(f_v, FOX_HEADS),
                                     f_f + fox_f_b[l], fox_qn_g[l], fox_kn_g[l])
        o_dsa = sparse_attention(d_cq, d_k, d_v, i_k, i_w, dsa_cq_g[l], dsa_w_uq[l], dsa_w_qidx[l],
                                 dsa_qn_g[l], dsa_kn_g[l], rel_bias)
        o_ssd = ssd_mixer(s_z, s_xbc, s_dt, ssd_conv_w[l], ssd_conv_b[l], ssd_dt_bias[l], ssd_a_log[l],
                          ssd_d[l], ssd_norm_g[l])
        gates = jax.nn.sigmoid((g_lin + gate_b[l]).astype(jnp.float32)).astype(x.dtype)
        gates = gates.reshape(B, S, N_BRANCH, D_MODEL)
        w_branch = split_axis(w_br[l], BRANCH_WIDTHS, axis=0)
        branch_out = (o_ret, o_fox, o_dsa, o_ssd)
        merged = sum(gates[:, :, i] * (branch_out[i].astype(x.dtype) @ w_branch[i]) for i in range(N_BRANCH))
        x = x + merged @ w_out[l]
        h2 = rms_norm(x, norm2_g[l])
        x = x + jnp.square(jax.nn.relu(h2 @ w_ff1[l])) @ w_ff2[l]
    return x
```

```python
import contextlib
import math
import numpy as np
import concourse.bass as bass
import concourse.mybir as mybir
from concourse.bass_utils import run_bass_kernel_spmd

F32 = mybir.dt.float32
BF16 = mybir.dt.bfloat16
AF = mybir.ActivationFunctionType
ALU = mybir.AluOpType
AX = mybir.AxisListType

D = 2048
S = 2048
L = 4
KC = 16
NT = 16
IN_TOTAL = 15204
EPS = 1e-6
NEG = -30000.0

C_RQ, C_RK, C_RV, C_RG = 0, 512, 1024, 1536
C_FQ, C_FK, C_FV, C_FF = 2048, 2560, 3072, 3584
C_DCQ, C_DK, C_DV, C_IK, C_IW = 3588, 4100, 4228, 4356, 4420
C_SZ, C_SX, C_SDT = 4436, 5460, 6996
C_G = 7012
TM_END = 5460

R_FB, R_FQG, R_FKG, R_CQG, R_DQG, R_DKG, R_DTB, R_ALOG, R_SD, R_SNG = 0, 4, 132, 260, 772, 900, 1028, 1044, 1060, 1076
NR = 2100
V_N1, V_N2, V_GB, V_CW, V_CB = 0, 16, 32, 96, 144
NV = 256

K_ID, K_ONES, K_TRI, K_MNEG, K_SU, K_DEC, K_QDEC, K_KDEC, K_CDEC, K_SEL, K_MBIG = 0, 128, 256, 384, 512, 640, 1152, 1156, 1160, 1164, 1676
NCST = 1804


class Buf:
    __slots__ = ("w", "r", "name")

    def __init__(self, name=""):
        self.w = None
        self.r = {}
        self.name = name


class Sched:
    EPOCH = 60000

    def __init__(self, nc, es, n_dma_sems=32):
        self.nc = nc
        self.es = es
        self.engs = {"pe": nc.tensor, "act": nc.scalar, "dve": nc.vector, "pool": nc.gpsimd, "sp": nc.sync}
        self.sem, self.cnt, self.semkey = {}, {}, {}
        self.nsem = 0
        for e in self.engs:
            self._new_eng_sem(e)
        self.seen = {e: {} for e in self.engs}
        self.dma_sems = []
        for i in range(n_dma_sems):
            s = es.enter_context(nc.semaphore(f"dma{i}"))
            self.dma_sems.append([f"dma{i}", s, 0])
        self.dma_next = 0
        self.n_ops = 0
        self.n_waits = 0

    def _new_eng_sem(self, e):
        self.nsem += 1
        key = f"{e}{self.nsem}"
        self.sem[e] = self.es.enter_context(self.nc.semaphore(key))
        self.semkey[e] = key
        self.cnt[e] = 0

    def _wait(self, eng, tok):
        key, sem, val, teng = tok
        if self.seen[eng].get(key, 0) >= val:
            return
        self.engs[eng].wait_ge(sem, val)
        self.seen[eng][key] = val
        self.n_waits += 1

    def _deps(self, eng, reads, writes, skip_same):
        for b in reads:
            if b.w is not None and not (skip_same and b.w[3] == eng):
                self._wait(eng, b.w)
        for b in writes:
            if b.w is not None and not (skip_same and b.w[3] == eng):
                self._wait(eng, b.w)
            for tok in b.r.values():
                if not (skip_same and tok[3] == eng):
                    self._wait(eng, tok)

    def _commit(self, tok, reads, writes):
        for b in reads:
            b.r[tok[0]] = tok
        for b in writes:
            b.w = tok
            b.r = {}

    def op(self, eng, fn, reads=(), writes=(), skip_same=False):
        self._deps(eng, reads, writes, skip_same)
        ins = fn(self.engs[eng])
        if self.cnt[eng] >= self.EPOCH:
            self._new_eng_sem(eng)
        self.cnt[eng] += 1
        ins.then_inc(self.sem[eng], 1)
        tok = (self.semkey[eng], self.sem[eng], self.cnt[eng], eng)
        self._commit(tok, reads, writes)
        self.n_ops += 1
        return tok

    def dma(self, q, out, in_, reads=(), writes=(), **kw):
        self._deps(q, reads, writes, False)
        ent = self.dma_sems[self.dma_next]
        self.dma_next = (self.dma_next + 1) % len(self.dma_sems)
        key, sem, val = ent
        if val > 0:
            self._wait(q, (key, sem, val, "dma"))
        self.engs[q].dma_start(out=out, in_=in_, **kw).then_inc(sem, 16)
        ent[2] = val + 16
        tok = (key, sem, val + 16, "dma")
        self._commit(tok, reads, writes)
        self.n_ops += 1
        return tok

    def barrier(self):
        toks = []
        for e in self.engs:
            if self.cnt[e] > 0:
                toks.append((self.semkey[e], self.sem[e], self.cnt[e], e))
        for key, sem, val in self.dma_sems:
            if val > 0:
                toks.append((key, sem, val, "dma"))
        for e in self.engs:
            for t in toks:
                self._wait(e, t)

    def final_wait(self, eng="sp"):
        for key, sem, val in self.dma_sems:
            if val > 0:
                self._wait(eng, (key, sem, val, "dma"))


class Rot:
    def __init__(self, items):
        self.items = items
        self.i = 0

    def next(self):
        it = self.items[self.i]
        self.i = (self.i + 1) % len(self.items)
        return it


def build(nl=L, dbg=(), LW=L):
    dbg = set(dbg)
    nc = bass.Bass("TRN2", target_bir_lowering=False)

    def din(name, shape, dt=F32):
        return nc.dram_tensor(name, list(shape), dt, kind="ExternalInput").ap()

    def dscr(name, shape, dt):
        if "dump" in dbg:
            return nc.dram_tensor(name, list(shape), dt, kind="ExternalOutput").ap()
        return nc.dram_tensor(name, list(shape), dt, kind="Internal").ap()

    xT_in = din("xT", [D, S])
    yT = nc.dram_tensor("yT", [D, S], F32, kind="ExternalOutput").ap()
    w_in = din("w_in", [LW, D, IN_TOTAL])
    w_br = din("w_br", [LW, 2560, D])
    w_out = din("w_out", [LW, D, D])
    w_ff1 = din("w_ff1", [LW, D, 4 * D])
    w_ff2 = din("w_ff2", [LW, 4 * D, D])
    w_uq = din("w_uq", [LW, 512, 512])
    w_qidx = din("w_qidx", [LW, 512, 1024])
    colv = din("colv", [LW, NV, 128])
    rowv = din("rowv", [LW, 1, NR])
    relb = din("relb", [1, 128])
    cst_d = din("cst", [128, NCST])
    oh_d = din("oh", [128, 8192])
    rot_d = din("rot", [S, 256])

    PT = dscr("PT", [S, TM_END], BF16)
    PFX = dscr("PFX", [1552, S], BF16)
    PFI = dscr("PFI", [128, S], BF16)
    PFG = dscr("PFG", [4 * D, S], BF16)
    if "ext_ot" in dbg:
        OT = din("OT", [2560, S], BF16)
    else:
        OT = dscr("OT", [2560, S], BF16)
    WB1 = nc.dram_tensor("WB1", [32, 128, KC * 256], BF16, kind="Internal").ap()
    WB2 = nc.dram_tensor("WB2", [32, 128, 32 * 128], BF16, kind="Internal").ap()
    XA = dscr("XA", [D, S], F32)
    XB = dscr("XB", [D, S], F32)

    ges = contextlib.ExitStack()
    with ges:
        sch = Sched(nc, ges)
        op, dma = sch.op, sch.dma

        uid = [0]

        def sb(es, name, shape, dt, nb=1):
            uid[0] += 1
            t = es.enter_context(nc.sbuf_tensor(f"s{uid[0]}_{name}", list(shape), dt))
            if nb == 1:
                return t, Buf(name)
            return t, [Buf(f"{name}{i}") for i in range(nb)]

        def rot(es, name, shape, dt, n):
            return Rot([sb(es, f"{name}{i}", shape, dt) for i in range(n)])

        PSB = []
        for i in range(8):
            t = ges.enter_context(nc.psum_tensor(f"ps{i}", [128, 512], F32))
            PSB.append((t, Buf(f"ps{i}")))

        def bf(ps_t):
            return ps_t[:, :].bitcast(BF16)

        cst, cstb = sb(ges, "cst", [128, NCST], F32)
        cstbf, cstbfb = sb(ges, "cstbf", [128, NCST], BF16)
        dma("sp", cst[:], cst_d, writes=[cstb])
        dma("pool", cstbf[:], cst_d, writes=[cstbfb])
        ident_f = cst[:, K_ID:K_ID + 128]
        ones_f = cst[:, K_ONES:K_ONES + 128]
        tri_f = cst[:, K_TRI:K_TRI + 128]
        su_f = cst[:, K_SU:K_SU + 128]
        ident_b = cstbf[:, K_ID:K_ID + 128]
        mneg_b = cstbf[:, K_MNEG:K_MNEG + 128]
        tri_b = cstbf[:, K_TRI:K_TRI + 128]
        biasT, biasTb = sb(ges, "biasT", [128, 3, 4, 128], BF16)
        with contextlib.ExitStack() as es:
            ohs, ohsb = sb(es, "ohs", [128, 64, 128], BF16)
            rb, rbb = sb(es, "rb", [128, 128], F32)
            acc, accb = sb(es, "bacc", [128, 2, 4, 128], F32)
            dma("pool", ohs[:].rearrange("p a b -> p (a b)"), oh_d, writes=[ohsb])
            dma("sp", rb[:], relb.partition_broadcast(128).rearrange("p a b -> p (a b)"), writes=[rbb])
            for ty in range(2):
                for h in range(4):
                    for b in range(32):
                        col = rb[:, b * 4 + h:b * 4 + h + 1]
                        if b == 0:
                            op("dve", lambda e, ty=ty, h=h, b=b, col=col: e.tensor_scalar(
                                acc[:, ty, h, :], ohs[:, ty * 32 + b, :], col, None, ALU.mult),
                               reads=[ohsb, rbb], writes=[accb])
                        else:
                            op("dve", lambda e, ty=ty, h=h, b=b, col=col: e.scalar_tensor_tensor(
                                out=acc[:, ty, h, :], in0=ohs[:, ty * 32 + b, :], scalar=col, in1=acc[:, ty, h, :],
                                op0=ALU.mult, op1=ALU.add), reads=[ohsb, rbb, accb], writes=[accb])
            op("dve", lambda e: e.tensor_copy(biasT[:, 0:2, :, :], acc[:]), reads=[accb], writes=[biasTb])
            for h in range(4):
                col = rb[:, 31 * 4 + h:31 * 4 + h + 1]
                op("dve", lambda e, h=h, col=col: e.tensor_scalar(
                    biasT[:, 2, h, :], cst[:, K_ONES:K_ONES + 128], col, None, ALU.mult),
                   reads=[cstb, rbb], writes=[biasTb])
            sch.barrier()

        evac_flip = [0]

        def evac_copy(out_ap, in_ap, reads, writes):
            evac_flip[0] ^= 1
            if evac_flip[0]:
                return op("act", lambda e: e.activation(out=out_ap, in_=in_ap, func=AF.Copy), reads=reads, writes=writes)
            return op("dve", lambda e: e.tensor_copy(out_ap, in_ap), reads=reads, writes=writes)

        def mm(out_ap, lhsT, rhs, start, stop, reads, writes):
            return op("pe", lambda e: e.matmul(out_ap, lhsT, rhs, start=start, stop=stop),
                      reads=reads, writes=writes, skip_same=True)

        def tr(out_ap, in_ap, ident, reads, writes):
            return op("pe", lambda e: e.transpose(out_ap, in_ap, ident), reads=reads, writes=writes, skip_same=True)

        def view_kc(ap2d):
            return ap2d.rearrange("(kc p) c -> p kc c", p=128)

        def load_params(es, l):
            cv, cvb = sb(es, "cv", [128, NV], F32)
            rv, rvb = sb(es, "rv", [128, NR], F32)
            with contextlib.ExitStack() as es2:
                raw, rawb = sb(es2, "cvraw", [128, 2, 128], F32)
                dma("sp", raw[:], colv[l].rearrange("(a p) c -> p a c", p=128), writes=[rawb])
                dma("sp", rv[:], rowv[l].partition_broadcast(128).rearrange("p a b -> p (a b)"), writes=[rvb])
                pt, ptb = PSB[0]
                for a in range(2):
                    tr(pt[:, a * 128:(a + 1) * 128], raw[:, a, :], ident_f, [rawb, cstb], [ptb])
                op("dve", lambda e: e.tensor_copy(cv[:], pt[:, 0:256]), reads=[ptb], writes=[cvb])
                op("dve", lambda e: e.tensor_scalar(cv[:, 0:32], cv[:, 0:32], math.sqrt(D), None, ALU.mult),
                   reads=[cvb], writes=[cvb])
                sch.barrier()
            return cv, cvb, rv, rvb

        def norm_group(xsrc, tcols, gcol0, cv, cvb, xt, xtb, hT, hcols, hbuf, sqr, rs, rsb):
            pA, pAb = PSB[7]
            xv = view_kc(xsrc)
            for kc in range(KC):
                dma("sp", xt[:, kc, :], xv[:, kc, tcols], writes=[xtb[kc]])
            for kc in range(KC):
                sq, sqb = sqr.next()
                op("act", lambda e, kc=kc, sq=sq: e.activation(out=sq[:], in_=xt[:, kc, :], func=AF.Square),
                   reads=[xtb[kc]], writes=[sqb])
                mm(pA[:, :], ones_f, sq[:], kc == 0, kc == KC - 1, [sqb, cstb], [pAb])
            op("act", lambda e: e.activation(out=rs[:], in_=pA[:, :], func=AF.Sqrt, bias=float(D * EPS), scale=1.0),
               reads=[pAb], writes=[rsb])
            op("dve", lambda e: e.reciprocal(rs[:], rs[:]), reads=[rsb], writes=[rsb])
            for kc in range(KC):
                op("dve", lambda e, kc=kc: e.scalar_tensor_tensor(
                    out=hT[:, kc, hcols], in0=xt[:, kc, :], scalar=cv[:, gcol0 + kc:gcol0 + kc + 1], in1=rs[:],
                    op0=ALU.mult, op1=ALU.mult), reads=[xtb[kc], cvb, rsb], writes=[hbuf])

        def phase1(l, xsrc, cv, cvb):
            with contextlib.ExitStack() as es:
                hT, hTb = sb(es, "hT", [128, KC, S], BF16, nb=4)
                xt, xtb = sb(es, "xt", [128, KC, 512], F32, nb=KC)
                sqr = rot(es, "sq", [128, 512], F32, 2)
                rs, rsb = sb(es, "rs", [128, 512], F32)
                for tg in range(4):
                    cols = slice(tg * 512, (tg + 1) * 512)
                    norm_group(xsrc, cols, V_N1, cv, cvb, xt, xtb, hT, cols, hTb[tg], sqr, rs, rsb)
                wr = rot(es, "w1t", [128, KC, 528], BF16, 3)
                stg = rot(es, "stg", [128, 512], BF16, 4)
                banks = Rot(PSB[0:6])
                wv = view_kc(w_in[l])
                jobs = []
                c = 0
                while c < TM_END:
                    wd = min(512, TM_END - c)
                    jobs.append(("tm", c, wd))
                    c += wd
                jobs.append(("ik", C_IK, 64))
                jobs += [("fx", 5460, 512, 0), ("fx", 5972, 512, 512), ("fx", 6484, 528, 1024)]
                for i in range(16):
                    jobs.append(("fg", C_G + i * 512, 512, i * 512))

                def load(j):
                    wt, wtb = wr.items[j % 3]
                    job = jobs[j]
                    if job[0] == "ik":
                        dma("pool", wt[:, :, 0:64], wv[:, :, C_IK:C_IK + 64], writes=[wtb])
                        dma("pool", wt[:, :, 64:128], wv[:, :, C_IK:C_IK + 64], writes=[wtb])
                    else:
                        dma("pool", wt[:, :, 0:job[2]], wv[:, :, job[1]:job[1] + job[2]], writes=[wtb])

                load(0)
                load(1)
                for j, job in enumerate(jobs):
                    wt, wtb = wr.items[j % 3]
                    if job[0] == "tm":
                        _, c0, wd = job
                        for tt in range(NT):
                            pb, pbb = banks.next()
                            for kc in range(KC):
                                mm(pb[:, 0:wd], hT[:, kc, tt * 128:(tt + 1) * 128], wt[:, kc, 0:wd], kc == 0, kc == KC - 1,
                                   [hTb[tt // 4], wtb], [pbb])
                            st, stb = stg.next()
                            evac_copy(st[:, 0:wd], pb[:, 0:wd], [pbb], [stb])
                            dma("sp", PT[tt * 128:(tt + 1) * 128, c0:c0 + wd], st[:, 0:wd], reads=[stb])
                    else:
                        if job[0] == "ik":
                            slices = [(0, 128, PFI, 0)]
                        elif job[0] == "fx":
                            slices = []
                            s0 = 0
                            while s0 < job[2]:
                                m = min(128, job[2] - s0)
                                slices.append((s0, m, PFX, job[3] + s0))
                                s0 += m
                        else:
                            slices = [(s0, 128, PFG, job[3] + s0) for s0 in range(0, 512, 128)]
                        for (s0, m, dst, r0) in slices:
                            for tg in range(4):
                                pb, pbb = banks.next()
                                for kc in range(KC):
                                    mm(pb[0:m, :], wt[:, kc, s0:s0 + m], hT[:, kc, tg * 512:(tg + 1) * 512], kc == 0, kc == KC - 1,
                                       [hTb[tg], wtb], [pbb])
                                st, stb = stg.next()
                                if job[0] == "fg":
                                    gch = r0 // 128
                                    op("act", lambda e, st=st, pb=pb, gch=gch: e.activation(
                                        out=st[:, :], in_=pb[:, :], func=AF.Sigmoid,
                                        bias=cv[:, V_GB + gch:V_GB + gch + 1], scale=1.0), reads=[pbb, cvb], writes=[stb])
                                else:
                                    evac_copy(st[0:m, :], pb[0:m, :], [pbb], [stb])
                                dma("sp", dst[r0:r0 + m, tg * 512:(tg + 1) * 512], st[0:m, :], reads=[stb])
                    if j + 2 < len(jobs):
                        load(j + 2)
                sch.barrier()

        def phase3(l, xsrc, xdst):
            with contextlib.ExitStack() as es:
                oT, oTb = sb(es, "oT", [128, 20, 1024], BF16)
                mT, mTb = sb(es, "mT", [128, KC, 1024], BF16, nb=KC)
                wr = rot(es, "w3t", [128, 20, 128], BF16, 3)
                gr = rot(es, "g3t", [128, 4, 1024], BF16, 2)
                accr = rot(es, "acc3", [128, 512], F32, 2)
                tmpr = rot(es, "tmp3", [128, 512], F32, 2)
                xr_ = rot(es, "xr3", [128, 1024], F32, 2)
                xo_ = rot(es, "xo3", [128, 512], F32, 3)
                banks = Rot(PSB[0:6])
                wbv = view_kc(w_br[l])
                wov = view_kc(w_out[l])
                gv = PFG.rearrange("(i dc p) t -> p i dc t", p=128, i=4)
                otv = view_kc(OT)
                xv = view_kc(xsrc)
                xdv = view_kc(xdst)
                brk = [(0, 4), (4, 8), (8, 12), (12, 20)]
                for hf in range(2):
                    hcols = slice(hf * 1024, (hf + 1) * 1024)
                    for kc in range(20):
                        dma("sp", oT[:, kc, :], otv[:, kc, hcols], writes=[oTb])
                    for dc in range(KC):
                        wt, wtb = wr.next()
                        dma("pool", wt[:], wbv[:, :, dc * 128:(dc + 1) * 128], writes=[wtb])
                        gt, gtb = gr.next()
                        dma("sp", gt[:], gv[:, :, dc, hcols], writes=[gtb])
                        for t2 in range(2):
                            cols = slice(t2 * 512, (t2 + 1) * 512)
                            ac, acb = accr.next()
                            for i in range(4):
                                pb, pbb = banks.next()
                                k0, k1 = brk[i]
                                for kc in range(k0, k1):
                                    mm(pb[:, :], wt[:, kc, :], oT[:, kc, cols], kc == k0, kc == k1 - 1, [wtb, oTb], [pbb])
                                if i == 0:
                                    op("dve", lambda e, ac=ac, pb=pb, gt=gt, cols=cols: e.tensor_tensor(
                                        ac[:], pb[:, :], gt[:, 0, cols], ALU.mult), reads=[pbb, gtb], writes=[acb])
                                else:
                                    tp, tpb = tmpr.next()
                                    op("dve", lambda e, tp=tp, pb=pb, gt=gt, cols=cols, i=i: e.tensor_tensor(
                                        tp[:], pb[:, :], gt[:, i, cols], ALU.mult), reads=[pbb, gtb], writes=[tpb])
                                    if i < 3:
                                        op("dve", lambda e, ac=ac, tp=tp: e.tensor_tensor(ac[:], ac[:], tp[:], ALU.add),
                                           reads=[acb, tpb], writes=[acb])
                                    else:
                                        op("dve", lambda e, ac=ac, tp=tp, dc=dc, cols=cols: e.tensor_tensor(
                                            mT[:, dc, cols], ac[:], tp[:], ALU.add), reads=[acb, tpb], writes=[mTb[dc]])
                    for dd in range(KC):
                        wt, wtb = wr.next()
                        dma("pool", wt[:, 0:KC, :], wov[:, :, dd * 128:(dd + 1) * 128], writes=[wtb])
                        xr, xrb = xr_.next()
                        dma("sp", xr[:], xv[:, dd, hcols], writes=[xrb])
                        for t2 in range(2):
                            cols = slice(t2 * 512, (t2 + 1) * 512)
                            pb, pbb = banks.next()
                            for kc in range(KC):
                                mm(pb[:, :], wt[:, kc, :], mT[:, kc, cols], kc == 0, kc == KC - 1, [wtb, mTb[kc]], [pbb])
                            xo, xob = xo_.next()
                            op("dve", lambda e, xo=xo, pb=pb, xr=xr, cols=cols: e.tensor_tensor(
                                xo[:], pb[:, :], xr[:, cols], ALU.add), reads=[pbb, xrb], writes=[xob])
                            dma("sp", xdv[:, dd, hf * 1024 + t2 * 512:hf * 1024 + (t2 + 1) * 512], xo[:], reads=[xob])
                sch.barrier()

        def phase4(l, xsrc, xdst, cv, cvb):
            with contextlib.ExitStack() as es:
                xt, xtb = sb(es, "xt4", [128, KC, 512], F32, nb=KC)
                sqr = rot(es, "sq4", [128, 512], F32, 2)
                rs, rsb = sb(es, "rs4", [128, 512], F32)
                h2, h2b = sb(es, "h2T", [128, KC, 512], BF16)
                aT, aTb = sb(es, "aT", [128, 64, 512], BF16, nb=64)
                w1r = rot(es, "w41", [128, KC, 256], BF16, 3)
                w2r = rot(es, "w42", [128, 32, 128], BF16, 3)
                rl_ = rot(es, "rl4", [128, 512], BF16, 2)
                xo_ = rot(es, "xo4", [128, 512], F32, 2)
                banks = Rot(PSB[0:6])
                w1v = view_kc(w_ff1[l])
                w2v = view_kc(w_ff2[l])
                xdv = view_kc(xdst)
                wb1b = [Buf(f"wb1_{i}") for i in range(32)]
                wb2b = [Buf(f"wb2_{i}") for i in range(32)]
                for tg in range(4):
                    cols = slice(tg * 512, (tg + 1) * 512)
                    norm_group(xsrc, cols, V_N2, cv, cvb, xt, xtb, h2, slice(0, 512), h2b, sqr, rs, rsb)
                    for fg in range(32):
                        wt, wtb = w1r.next()
                        if tg == 0:
                            dma("pool", wt[:], w1v[:, :, fg * 256:(fg + 1) * 256], writes=[wtb])
                            dma("sp", WB1[fg], wt[:].rearrange("p a b -> p (a b)"), reads=[wtb], writes=[wb1b[fg]])
                        else:
                            dma("pool", wt[:].rearrange("p a b -> p (a b)"), WB1[fg], reads=[wb1b[fg]], writes=[wtb])
                        for fs in range(2):
                            fc = fg * 2 + fs
                            pb, pbb = banks.next()
                            for kc in range(KC):
                                mm(pb[:, :], wt[:, kc, fs * 128:(fs + 1) * 128], h2[:, kc, :], kc == 0, kc == KC - 1, [wtb, h2b], [pbb])
                            rl, rlb = rl_.next()
                            op("act", lambda e, rl=rl, pb=pb: e.activation(out=rl[:], in_=pb[:, :], func=AF.Relu),
                               reads=[pbb], writes=[rlb])
                            op("dve", lambda e, rl=rl, fc=fc: e.tensor_tensor(aT[:, fc, :], rl[:], rl[:], ALU.mult),
                               reads=[rlb], writes=[aTb[fc]])
                    for dd in range(KC):
                        pb, pbb = banks.next()
                        for hf in range(2):
                            wt, wtb = w2r.next()
                            wi = dd * 2 + hf
                            if tg == 0:
                                dma("pool", wt[:], w2v[:, hf * 32:(hf + 1) * 32, dd * 128:(dd + 1) * 128], writes=[wtb])
                                dma("sp", WB2[wi], wt[:].rearrange("p a b -> p (a b)"), reads=[wtb], writes=[wb2b[wi]])
                            else:
                                dma("pool", wt[:].rearrange("p a b -> p (a b)"), WB2[wi], reads=[wb2b[wi]], writes=[wtb])
                            for f in range(32):
                                fc = hf * 32 + f
                                mm(pb[:, :], wt[:, f, :], aT[:, fc, :], fc == 0, fc == 63, [wtb, aTb[fc]], [pbb])
                        xo, xob = xo_.next()
                        op("dve", lambda e, xo=xo, pb=pb, dd=dd: e.tensor_tensor(xo[:], pb[:, :], xt[:, dd, :], ALU.add),
                           reads=[pbb, xtb[dd]], writes=[xob])
                        dma("sp", xdv[:, dd, cols], xo[:], reads=[xob])
                sch.barrier()

        MIXERS = {}
        ctx = dict(nc=nc, sch=sch, op=op, dma=dma, sb=sb, rot=rot, PSB=PSB, bf=bf, cst=cst, cstb=cstb, cstbf=cstbf,
                   cstbfb=cstbfb, biasT=biasT, biasTb=biasTb, mm=mm, tr=tr, evac_copy=evac_copy, PT=PT, PFX=PFX, PFI=PFI,
                   OT=OT, rot_d=rot_d, w_uq=w_uq, w_qidx=w_qidx, dbg=dbg)

        xcur = xT_in
        for l in range(nl):
            with contextlib.ExitStack() as les:
                cv, cvb, rv, rvb = load_params(les, l)
                if "skip1" not in dbg:
                    phase1(l, xcur, cv, cvb)
                if "ext_ot" not in dbg:
                    mixers(ctx, l, cv, cvb, rv, rvb)
                if "only_mix" in dbg:
                    continue
                if "no_p3" not in dbg:
                    phase3(l, xcur, XA)
                xdst = yT if l == nl - 1 else XB
                if "no_p4" not in dbg:
                    phase4(l, XA, xdst, cv, cvb)
                xcur = XB
                sch.barrier()
        sch.final_wait("sp")
        print("built: ops", sch.n_ops, "waits", sch.n_waits, "sems", sch.nsem)
    return nc


def mixers(ctx, l, cv, cvb, rv, rvb):
    dbg = ctx["dbg"]
    if "no_ret" not in dbg:
        mix_ret(ctx, l, rv, rvb)
    if "no_fox" not in dbg:
        mix_fox(ctx, l, rv, rvb)
    if "no_dsa" not in dbg:
        mix_dsa(ctx, l, rv, rvb)
    if "no_ssd" not in dbg:
        mix_ssd(ctx, l, cv, cvb, rv, rvb)


def _store_oT(ctx, o_ap, obuf, nch, row0, tt, bank, stgr):
    op, dma, tr, bf, evac_copy = ctx["op"], ctx["dma"], ctx["tr"], ctx["bf"], ctx["evac_copy"]
    ident_b = ctx["cstbf"][:, K_ID:K_ID + 128]
    pb, pbb = bank
    pbv = bf(pb)
    for c in range(nch):
        tr(pbv[:, c * 128:(c + 1) * 128], o_ap[:, c * 128:(c + 1) * 128], ident_b, [obuf, ctx["cstbfb"]], [pbb])
    st, stb = stgr.next()
    evac_copy(st[:, 0:nch * 128], pbv[:, 0:nch * 128], [pbb], [stb])
    dma("sp", ctx["OT"][row0:row0 + nch * 128, tt * 128:(tt + 1) * 128].rearrange("(c p) t -> p c t", p=128),
        st[:, 0:nch * 128].rearrange("p (c t) -> p c t", c=nch), reads=[stb])


def _interleave(gens):
    gens = list(gens)
    while gens:
        for g in list(gens):
            try:
                next(g)
            except StopIteration:
                gens.remove(g)


def _bc_last(ap2, n):
    return ap2.unsqueeze(2).to_broadcast([ap2.shape[0], ap2.shape[1], n])


def _bc_mid(ap2, n):
    return ap2.unsqueeze(1).to_broadcast([ap2.shape[0], n, ap2.shape[1]])


def _v3(ap2, a):
    return ap2.rearrange("p (a b) -> p a b", a=a)


def mix_ret(ctx, l, rv, rvb):
    nc, sch, op, dma, sb, rot, PSB, bf = (ctx[k] for k in ("nc", "sch", "op", "dma", "sb", "rot", "PSB", "bf"))
    mm, tr, evac_copy = ctx["mm"], ctx["tr"], ctx["evac_copy"]
    cst, cstb, cstbf, cstbfb = ctx["cst"], ctx["cstb"], ctx["cstbf"], ctx["cstbfb"]
    ident_b = cstbf[:, K_ID:K_ID + 128]
    PT = ctx["PT"]
    with contextlib.ExitStack() as es:
        rt, rtb = sb(es, "rt", [128, NT, 256], F32)
        dma("sp", rt[:], ctx["rot_d"].rearrange("(t p) c -> p t c", p=128), writes=[rtb])
        Sf, Sfb = sb(es, "Sf", [128, 4, 128], F32)
        Sb, Sbb = sb(es, "Sb", [128, 4, 128], BF16)
        op("dve", lambda e: e.memset(Sf[:], 0.0), writes=[Sfb])
        op("dve", lambda e: e.memset(Sb[:], 0.0), writes=[Sbb])
        ptr = rot(es, "rpt", [128, 2048], BF16, 2)
        tmp = [sb(es, f"rtmp{i}", [128, 4, 64], F32) for i in range(4)]
        qr, qrb = sb(es, "qr", [128, 4, 128], BF16)
        kr, krb = sb(es, "kr", [128, 4, 128], BF16)
        qd, qdb = sb(es, "qd", [128, 4, 128], BF16)
        vd, vdb = sb(es, "vd", [128, 4, 128], BF16)
        qkT, qkTb = sb(es, "qkT", [128, 8, 128], BF16)
        qdT, qdTb = sb(es, "qdT", [128, 4, 128], BF16)
        sm, smb = sb(es, "sm", [128, 512], BF16)
        ysb, ysbb = sb(es, "ysb", [128, 4, 128], F32)
        ysq, ysqb = sb(es, "ysq", [128, 4, 128], F32)
        st4 = [sb(es, f"rst{i}", [128, 4], F32) for i in range(5)]
        sg, sgb = sb(es, "sg", [128, 512], F32)
        o, ob = sb(es, "oret", [128, 512], BF16)
        stgr = rot(es, "rstg", [128, 1024], BF16, 2)
        for c in range(NT):
            pt, ptb = ptr.next()
            dma("sp", pt[:], PT[c * 128:(c + 1) * 128, 0:2048], writes=[ptb])
            for (c0, ct, dst, dstb) in ((0, 0, qr, qrb), (512, 128, kr, krb)):
                x = _v3(pt[:, c0:c0 + 512], 4)
                x1, x2 = x[:, :, 0:64], x[:, :, 64:128]
                cosb = _bc_mid(rt[:, c, ct:ct + 64], 4)
                sinb = _bc_mid(rt[:, c, ct + 64:ct + 128], 4)
                (t0, t0b), (t1, t1b), (t2, t2b), (t3, t3b) = tmp
                op("dve", lambda e, t0=t0, x1=x1, cosb=cosb: e.tensor_tensor(t0[:], x1, cosb, ALU.mult), reads=[ptb, rtb], writes=[t0b])
                op("dve", lambda e, t1=t1, x2=x2, sinb=sinb: e.tensor_tensor(t1[:], x2, sinb, ALU.mult), reads=[ptb, rtb], writes=[t1b])
                op("dve", lambda e, t2=t2, x1=x1, sinb=sinb: e.tensor_tensor(t2[:], x1, sinb, ALU.mult), reads=[ptb, rtb], writes=[t2b])
                op("dve", lambda e, t3=t3, x2=x2, cosb=cosb: e.tensor_tensor(t3[:], x2, cosb, ALU.mult), reads=[ptb, rtb], writes=[t3b])
                op("dve", lambda e, dst=dst, t0=t0, t1=t1: e.tensor_tensor(dst[:, :, 0:64], t0[:], t1[:], ALU.subtract), reads=[t0b, t1b], writes=[dstb])
                op("dve", lambda e, dst=dst, t2=t2, t3=t3: e.tensor_tensor(dst[:, :, 64:128], t2[:], t3[:], ALU.add), reads=[t2b, t3b], writes=[dstb])
            op("dve", lambda e: e.tensor_tensor(qd[:], qr[:], _bc_last(cst[:, K_QDEC:K_QDEC + 4], 128), ALU.mult),
               reads=[qrb, cstb], writes=[qdb])
            op("dve", lambda e, pt=pt: e.tensor_tensor(vd[:], _v3(pt[:, 1024:1536], 4), _bc_last(cst[:, K_KDEC:K_KDEC + 4], 128), ALU.mult),
               reads=[ptb, cstb], writes=[vdb])
            pA, pAb = PSB[0]
            pB, pBb = PSB[1]
            pAv, pBv = bf(pA), bf(pB)
            for h in range(4):
                tr(pAv[:, h * 128:(h + 1) * 128], qr[:, h, :], ident_b, [qrb, cstbfb], [pAb])
                tr(pAv[:, (4 + h) * 128:(5 + h) * 128], kr[:, h, :], ident_b, [krb, cstbfb], [pAb])
                tr(pBv[:, h * 128:(h + 1) * 128], qd[:, h, :], ident_b, [qdb, cstbfb], [pBb])
            op("act", lambda e: e.activation(out=qkT[:].rearrange("p a b -> p (a b)"), in_=pAv[:, :], func=AF.Copy),
               reads=[pAb], writes=[qkTb])
            op("dve", lambda e: e.tensor_copy(qdT[:].rearrange("p a b -> p (a b)"), pBv[:, 0:512]), reads=[pBb], writes=[qdTb])
            pC, pCb = PSB[2]
            for h in range(4):
                mm(pC[:, h * 128:(h + 1) * 128], qkT[:, 4 + h, :], qkT[:, h, :], True, True, [qkTb], [pCb])
            op("dve", lambda e: e.tensor_tensor(sm[:], pC[:, :], cst[:, K_DEC:K_DEC + 512], ALU.mult), reads=[pCb, cstb], writes=[smb])
            pD, pDb = PSB[3]
            for h in range(4):
                hs = slice(h * 128, (h + 1) * 128)
                mm(pD[:, hs], sm[:, hs], pt[:, 1024 + h * 128:1024 + (h + 1) * 128], True, False, [smb, ptb], [pDb])
                mm(pD[:, hs], qdT[:, h, :], Sb[:, h, :], False, True, [qdTb, Sbb], [pDb])
            pE, pEb = PSB[4]
            for h in range(4):
                mm(pE[:, h * 128:(h + 1) * 128], kr[:, h, :], vd[:, h, :], True, True, [krb, vdb], [pEb])
            op("dve", lambda e: e.tensor_tensor(Sf[:], Sf[:], _bc_last(cst[:, K_CDEC:K_CDEC + 4], 128), ALU.mult), reads=[Sfb, cstb], writes=[Sfb])
            op("dve", lambda e: e.tensor_tensor(Sf[:], Sf[:], _v3(pE[:, :], 4), ALU.add), reads=[Sfb, pEb], writes=[Sfb])
            op("act", lambda e: e.activation(out=Sb[:], in_=Sf[:], func=AF.Copy), reads=[Sfb], writes=[Sbb])
            (s1, s1b), (s2, s2b), (mean, meanb), (msq, msqb), (rstd, rstdb) = st4
            op("act", lambda e: e.activation(out=ysb[:], in_=_v3(pD[:, :], 4), func=AF.Copy), reads=[pDb], writes=[ysbb])
            op("dve", lambda e: e.tensor_reduce(out=s1[:], in_=ysb[:], axis=AX.X, op=ALU.add), reads=[ysbb], writes=[s1b])
            op("dve", lambda e: e.tensor_tensor(ysq[:], ysb[:], ysb[:], ALU.mult), reads=[ysbb], writes=[ysqb])
            op("dve", lambda e: e.tensor_reduce(out=s2[:], in_=ysq[:], axis=AX.X, op=ALU.add), reads=[ysqb], writes=[s2b])
            op("dve", lambda e: e.tensor_scalar(mean[:], s1[:], 1.0 / 128, None, ALU.mult), reads=[s1b], writes=[meanb])
            op("dve", lambda e: e.tensor_tensor(msq[:], mean[:], mean[:], ALU.mult), reads=[meanb], writes=[msqb])
            op("dve", lambda e: e.scalar_tensor_tensor(out=rstd[:], in0=s2[:], scalar=1.0 / 128, in1=msq[:], op0=ALU.mult, op1=ALU.subtract),
               reads=[s2b, msqb], writes=[rstdb])
            op("act", lambda e: e.activation(out=rstd[:], in_=rstd[:], func=AF.Sqrt, bias=EPS, scale=1.0), reads=[rstdb], writes=[rstdb])
            op("dve", lambda e: e.reciprocal(rstd[:], rstd[:]), reads=[rstdb], writes=[rstdb])
            op("dve", lambda e: e.tensor_tensor(ysb[:], ysb[:], _bc_last(mean[:], 128), ALU.subtract), reads=[ysbb, meanb], writes=[ysbb])
            op("dve", lambda e: e.tensor_tensor(ysb[:], ysb[:], _bc_last(rstd[:], 128), ALU.mult), reads=[ysbb, rstdb], writes=[ysbb])
            op("act", lambda e, pt=pt: e.activation(out=sg[:], in_=pt[:, 1536:2048], func=AF.Silu), reads=[ptb], writes=[sgb])
            op("dve", lambda e: e.tensor_tensor(o[:], ysb[:].rearrange("p a b -> p (a b)"), sg[:], ALU.mult), reads=[ysbb, sgb], writes=[ob])
            _store_oT(ctx, o[:], ob, 4, 0, c, PSB[5], stgr)
        sch.barrier()


def mix_fox(ctx, l, rv, rvb):
    nc, sch, op, dma, sb, rot, PSB, bf = (ctx[k] for k in ("nc", "sch", "op", "dma", "sb", "rot", "PSB", "bf"))
    mm, tr, evac_copy = ctx["mm"], ctx["tr"], ctx["evac_copy"]
    cst, cstb, cstbf, cstbfb = ctx["cst"], ctx["cstb"], ctx["cstbf"], ctx["cstbfb"]
    ident_b = cstbf[:, K_ID:K_ID + 128]
    ident_f = cst[:, K_ID:K_ID + 128]
    mneg_b = cstbf[:, K_MNEG:K_MNEG + 128]
    tri_f = cst[:, K_TRI:K_TRI + 128]
    ones_f = cst[:, K_ONES:K_ONES + 128]
    PT = ctx["PT"]
    with contextlib.ExitStack() as es:
        qT, qTb = sb(es, "fqT", [128, 4, S], BF16)
        kT, kTb = sb(es, "fkT", [128, 4, S], BF16)
        Vp, Vpb = sb(es, "fVp", [128, NT, 4, 129], BF16)
        nlf, nlfb = sb(es, "nlf", [128, NT, 4], F32)
        G, Gb = sb(es, "fG", [128, NT, 4], F32)
        nGT, nGTb = sb(es, "nGT", [128, S], F32)
        of, ofb = sb(es, "ofox", [128, NT, 512], BF16)
        gq, gqb = sb(es, "fgq", [128, 128], F32)
        carry, carryb = sb(es, "fcarry", [128, 4], F32)
        op("dve", lambda e: e.tensor_scalar(gq[:], rv[:, R_FQG:R_FQG + 128], 128 ** -0.5, None, ALU.mult), reads=[rvb], writes=[gqb])
        op("dve", lambda e: e.memset(Vp[:, :, :, 128:129], 1.0), writes=[Vpb])
        op("dve", lambda e: e.memset(carry[:], 0.0), writes=[carryb])
        W = 4
        lanes = [dict(pt=sb(es, f"fpt{k}", [128, 1540], BF16), sq=sb(es, f"fsq{k}", [128, 512], F32),
                      xn=sb(es, f"fxn{k}", [128, 512], BF16), ss=sb(es, f"fss{k}", [128, 4], F32),
                      z=sb(es, f"fz{k}", [128, 4], F32), pb=PSB[k]) for k in range(W)]
        pbanks = Rot(PSB[0:2])

        def stepA(tt, ln):
            (pt, ptb), (sq, sqb), (xn, xnb), (ss, ssb), (z, zb), (pb, pbb) = ln["pt"], ln["sq"], ln["xn"], ln["ss"], ln["z"], ln["pb"]
            dma("sp", pt[:], PT[tt * 128:(tt + 1) * 128, C_FQ:C_FQ + 1540], writes=[ptb])
            yield
            for (c0, gain, gainb, dstT, dstTb) in ((0, gq[:], gqb, qT, qTb), (512, rv[:, R_FKG:R_FKG + 128], rvb, kT, kTb)):
                x = pt[:, c0:c0 + 512]
                op("dve", lambda e: e.tensor_tensor(sq[:], x, x, ALU.mult), reads=[ptb], writes=[sqb])
                yield
                op("dve", lambda e: e.tensor_reduce(out=ss[:], in_=_v3(sq[:], 4), axis=AX.X, op=ALU.add), reads=[sqb], writes=[ssb])
                yield
                op("act", lambda e: e.activation(out=ss[:], in_=ss[:], func=AF.Sqrt, bias=EPS, scale=1.0 / 128), reads=[ssb], writes=[ssb])
                yield
                op("dve", lambda e: e.reciprocal(ss[:], ss[:]), reads=[ssb], writes=[ssb])
                yield
                op("dve", lambda e: e.tensor_tensor(_v3(sq[:], 4), _v3(x, 4), _bc_last(ss[:], 128), ALU.mult), reads=[ptb, ssb], writes=[sqb])
                yield
                op("dve", lambda e: e.tensor_tensor(_v3(xn[:], 4), _v3(sq[:], 4), _bc_mid(gain, 4), ALU.mult),
                   reads=[sqb, gainb], writes=[xnb])
                yield
                pbv = bf(pb)
                for h in range(4):
                    tr(pbv[:, h * 128:(h + 1) * 128], xn[:, h * 128:(h + 1) * 128], ident_b, [xnb, cstbfb], [pbb])
                yield
                evac_copy(dstT[:, :, tt * 128:(tt + 1) * 128], _v3(pbv[:, 0:512], 4), [pbb], [dstTb])
                yield
            op("act", lambda e: e.activation(out=Vp[:, tt, :, 0:128], in_=_v3(pt[:, 1024:1536], 4), func=AF.Copy),
               reads=[ptb], writes=[Vpb])
            yield
            op("dve", lambda e: e.tensor_tensor(z[:], pt[:, 1536:1540], rv[:, R_FB:R_FB + 4], ALU.add), reads=[ptb, rvb], writes=[zb])
            yield
            op("act", lambda e: e.activation(out=z[:], in_=z[:], func=AF.Exp, scale=-1.0), reads=[zb], writes=[zb])
            yield
            op("act", lambda e: e.activation(out=nlf[:, tt, :], in_=z[:], func=AF.Ln, bias=1.0, scale=1.0), reads=[zb], writes=[nlfb])
            yield
        for t0 in range(0, NT, W):
            _interleave([stepA(t0 + k, lanes[k]) for k in range(W)])
        for tt in range(NT):
            pb, pbb = pbanks.next()
            mm(pb[:, 0:4], tri_f, nlf[:, tt, :], True, True, [cstb, nlfb], [pbb])
            mm(pb[:, 4:8], ones_f, nlf[:, tt, :], True, True, [cstb, nlfb], [pbb])
            op("dve", lambda e, pb=pb, tt=tt: e.tensor_tensor(G[:, tt, :], pb[:, 0:4], carry[:], ALU.add), reads=[pbb, carryb], writes=[Gb])
            op("dve", lambda e, pb=pb: e.tensor_tensor(carry[:], pb[:, 4:8], carry[:], ALU.add), reads=[pbb, carryb], writes=[carryb])
            pb2, pb2b = pbanks.next()
            tr(pb2[0:4, 0:128], G[:, tt, :], ident_f, [Gb, cstb], [pb2b])
            op("dve", lambda e, pb2=pb2, tt=tt: e.tensor_scalar(nGT[0:4, tt * 128:(tt + 1) * 128], pb2[0:4, 0:128], -1.0, None, ALU.mult),
               reads=[pb2b], writes=[nGTb])
        sbanks = Rot(PSB[0:3])
        abanks = Rot([(PSB[3], PSB[4]), (PSB[5], PSB[6])])
        pTr = rot(es, "fpT", [128, 512], BF16, 3)
        rc_ = rot(es, "frc", [128, 1], F32, 2)
        for h in range(4):
            selh = cst[0:4, K_SEL + h * 128:K_SEL + (h + 1) * 128]
            for qg in range(4):
                ab = abanks.next()
                nj = 4 * qg + 4

                def acc(r):
                    t, b = ab[r // 2]
                    return t[:, (r % 2) * 256:(r % 2) * 256 + 129], b
                for j in range(nj):
                    r0 = max(0, j - 4 * qg)
                    ncol = (4 - r0) * 128
                    qc0 = qg * 512 + r0 * 128
                    sk, skb = sbanks.next()
                    mm(sk[:, 0:ncol], kT[:, h, j * 128:(j + 1) * 128], qT[:, h, qc0:qc0 + ncol], True, False, [kTb, qTb], [skb])
                    if j >= 4 * qg:
                        mm(sk[:, 0:128], ident_b, mneg_b, False, False, [cstbfb], [skb])
                    mm(sk[:, 0:ncol], selh, nGT[0:4, qc0:qc0 + ncol], False, True, [cstb, nGTb], [skb])
                    pT, pTb = pTr.next()
                    op("act", lambda e, pT=pT, sk=sk, ncol=ncol, j=j, h=h: e.activation(
                        out=pT[:, 0:ncol], in_=sk[:, 0:ncol], func=AF.Exp, bias=G[:, j, h:h + 1], scale=1.0),
                       reads=[skb, Gb], writes=[pTb])
                    for r in range(r0, 4):
                        i = 4 * qg + r
                        a_ap, a_b = acc(r)
                        mm(a_ap, pT[:, (r - r0) * 128:(r - r0 + 1) * 128], Vp[:, j, h, :], j == 0 and r % 2 == 0, j == i, [pTb, Vpb], [a_b])
                for r in range(4):
                    i = 4 * qg + r
                    a_ap, a_b = acc(r)
                    rc, rcb = rc_.next()
                    op("dve", lambda e, rc=rc, a_ap=a_ap: e.reciprocal(rc[:], a_ap[:, 128:129]), reads=[a_b], writes=[rcb])
                    op("dve", lambda e, rc=rc, a_ap=a_ap, i=i, h=h: e.tensor_scalar(
                        of[:, i, h * 128:(h + 1) * 128], a_ap[:, 0:128], rc[:, 0:1], None, ALU.mult), reads=[a_b, rcb], writes=[ofb])
        stgr = rot(es, "fstg", [128, 1024], BF16, 2)
        for tt in range(NT):
            _store_oT(ctx, of[:, tt, :], ofb, 4, 512, tt, PSB[7], stgr)
        sch.barrier()


def mix_dsa(ctx, l, rv, rvb):
    nc, sch, op, dma, sb, rot, PSB, bf = (ctx[k] for k in ("nc", "sch", "op", "dma", "sb", "rot", "PSB", "bf"))
    mm, tr, evac_copy = ctx["mm"], ctx["tr"], ctx["evac_copy"]
    cst, cstb, cstbf, cstbfb = ctx["cst"], ctx["cstb"], ctx["cstbf"], ctx["cstbfb"]
    biasT, biasTb = ctx["biasT"], ctx["biasTb"]
    ident_b = cstbf[:, K_ID:K_ID + 128]
    PT = ctx["PT"]
    TOPK = 256
    with contextlib.ExitStack() as es:
        cqT, cqTb = sb(es, "cqT", [128, 4, S], BF16)
        kT, kTb = sb(es, "dkT", [128, S], BF16)
        Vp, Vpb = sb(es, "dVp", [128, NT, 129], BF16)
        wh, whb = sb(es, "dwh", [128, NT, 16], F32)
        qT, qTb = sb(es, "dqT", [128, NT, 4, 128], BF16)
        qiT, qiTb = sb(es, "qiT", [128, 8, S], BF16)
        kiT, kiTb = sb(es, "kiT", [128, S], BF16)
        wuq, wuqb = sb(es, "wuq", [128, 4, 512], BF16)
        wqi, wqib = sb(es, "wqi", [128, 4, 1024], BF16)
        od, odb = sb(es, "odsa", [128, NT, 512], BF16)
        gqd, gqdb = sb(es, "dgq", [128, 128], F32)
        thr0, thr0b = sb(es, "thr0", [128, 1], F32)
        dma("pool", wuq[:], ctx["w_uq"][l].rearrange("(rc p) c -> p rc c", p=128), writes=[wuqb])
        dma("pool", wqi[:], ctx["w_qidx"][l].rearrange("(rc p) c -> p rc c", p=128), writes=[wqib])
        dma("sp", kiT[:], ctx["PFI"], writes=[kiTb])
        op("dve", lambda e: e.tensor_scalar(gqd[:], rv[:, R_DQG:R_DQG + 128], 128 ** -0.5, None, ALU.mult), reads=[rvb], writes=[gqdb])
        op("dve", lambda e: e.memset(Vp[:, :, 128:129], 1.0), writes=[Vpb])
        op("dve", lambda e: e.memset(thr0[:], -1e29), writes=[thr0b])
        ptr = rot(es, "dpt", [128, 848], BF16, 2)
        sq, sqb = sb(es, "dsq", [128, 512], F32)
        xn, xnb = sb(es, "dxn", [128, 512], BF16)
        ss, ssb = sb(es, "dss", [128, 4], F32)
        pbanks = Rot(PSB[0:4])
        for tt in range(NT):
            pt, ptb = ptr.next()
            dma("sp", pt[:], PT[tt * 128:(tt + 1) * 128, C_DCQ:C_DCQ + 848], writes=[ptb])
            for (c0, w, gain, dstf) in ((0, 512, rv[:, R_CQG:R_CQG + 512], "cq"), (512, 128, rv[:, R_DKG:R_DKG + 128], "k")):
                x = pt[:, c0:c0 + w]
                op("dve", lambda e, x=x, w=w: e.tensor_tensor(sq[:, 0:w], x, x, ALU.mult), reads=[ptb], writes=[sqb])
                op("dve", lambda e, w=w: e.tensor_reduce(out=ss[:, 0:1], in_=sq[:, 0:w], axis=AX.X, op=ALU.add), reads=[sqb], writes=[ssb])
                op("act", lambda e, w=w: e.activation(out=ss[:, 0:1], in_=ss[:, 0:1], func=AF.Sqrt, bias=EPS, scale=1.0 / w), reads=[ssb], writes=[ssb])
                op("dve", lambda e: e.reciprocal(ss[:, 0:1], ss[:, 0:1]), reads=[ssb], writes=[ssb])
                op("dve", lambda e, x=x, w=w: e.tensor_scalar(sq[:, 0:w], x, ss[:, 0:1], None, ALU.mult), reads=[ptb, ssb], writes=[sqb])
                op("dve", lambda e, w=w, gain=gain: e.tensor_tensor(xn[:, 0:w], sq[:, 0:w], gain, ALU.mult), reads=[sqb, rvb], writes=[xnb])
                pb, pbb = pbanks.next()
                pbv = bf(pb)
                nch = w // 128
                for c in range(nch):
                    tr(pbv[:, c * 128:(c + 1) * 128], xn[:, c * 128:(c + 1) * 128], ident_b, [xnb, cstbfb], [pbb])
                if dstf == "cq":
                    evac_copy(cqT[:, :, tt * 128:(tt + 1) * 128], _v3(pbv[:, 0:512], 4), [pbb], [cqTb])
                else:
                    evac_copy(kT[:, tt * 128:(tt + 1) * 128], pbv[:, 0:128], [pbb], [kTb])
            op("act", lambda e, pt=pt, tt=tt: e.activation(out=Vp[:, tt, 0:128], in_=pt[:, 640:768], func=AF.Copy), reads=[ptb], writes=[Vpb])
            op("dve", lambda e, pt=pt, tt=tt: e.tensor_scalar(wh[:, tt, :], pt[:, 832:848], 0.25 * 0.125, None, ALU.mult), reads=[ptb], writes=[whb])
        qs, qsb = sb(es, "dqs", [128, 512], F32)
        for tt in range(NT):
            pb, pbb = pbanks.next()
            for rc in range(4):
                mm(pb[:, :], cqT[:, rc, tt * 128:(tt + 1) * 128], wuq[:, rc, :], rc == 0, rc == 3, [cqTb, wuqb], [pbb])
            op("act", lambda e, pb=pb: e.activation(out=qs[:], in_=pb[:, :], func=AF.Copy), reads=[pbb], writes=[qsb])
            op("dve", lambda e: e.tensor_tensor(sq[:], qs[:], qs[:], ALU.mult), reads=[qsb], writes=[sqb])
            op("dve", lambda e: e.tensor_reduce(out=ss[:], in_=_v3(sq[:], 4), axis=AX.X, op=ALU.add), reads=[sqb], writes=[ssb])
            op("act", lambda e: e.activation(out=ss[:], in_=ss[:], func=AF.Sqrt, bias=EPS, scale=1.0 / 128), reads=[ssb], writes=[ssb])
            op("dve", lambda e: e.reciprocal(ss[:], ss[:]), reads=[ssb], writes=[ssb])
            op("dve", lambda e: e.tensor_tensor(_v3(sq[:], 4), _v3(qs[:], 4), _bc_last(ss[:], 128), ALU.mult), reads=[qsb, ssb], writes=[sqb])
            op("dve", lambda e: e.tensor_tensor(_v3(xn[:], 4), _v3(sq[:], 4), _bc_mid(gqd[:], 4), ALU.mult), reads=[sqb, gqdb], writes=[xnb])
            pb2, pb2b = pbanks.next()
            pbv = bf(pb2)
            for h in range(4):
                tr(pbv[:, h * 128:(h + 1) * 128], xn[:, h * 128:(h + 1) * 128], ident_b, [xnb, cstbfb], [pb2b])
            evac_copy(qT[:, tt, :, :].rearrange("p a b -> p (a b)"), pbv[:, 0:512], [pb2b], [qTb])
        for ch in range(8):
            for tg in range(4):
                pb, pbb = pbanks.next()
                for rc in range(4):
                    mm(pb[:, :], wqi[:, rc, ch * 128:(ch + 1) * 128], cqT[:, rc, tg * 512:(tg + 1) * 512], rc == 0, rc == 3, [wqib, cqTb], [pbb])
                evac_copy(qiT[:, ch, tg * 512:(tg + 1) * 512], pb[:, :], [pbb], [qiTb])
        I2 = [sb(es, f"dI{k}", [128, S], F32, nb=4) for k in range(2)]
        work, workb = sb(es, "dwork", [128, S], F32)
        m8, m8b = sb(es, "dm8", [128, 8], F32)
        thr, thrb = sb(es, "dthr", [128, 1], F32)
        selm, selmb = sb(es, "dsel", [128, S], BF16)
        mT, mTb = sb(es, "dmT", [128, NT, 128], BF16)
        rr = rot(es, "drr", [128, 512], BF16, 3)
        er = rot(es, "der", [128, 512], BF16, 2)
        pr = rot(es, "dpr", [128, 4, 128], BF16, 2)
        rc_ = rot(es, "drc", [128, 1], F32, 2)
        ibanks = Rot(PSB[0:2])
        iacc = [PSB[2], PSB[3]]
        tbanks = [PSB[4], PSB[4]]
        sbanks = Rot(PSB[5:6])
        A0, A1 = PSB[6], PSB[7]
        Dg2 = [sb(es, f"dDg{k}", [128, 16, 128], BF16) for k in range(2)]

        def acc(h):
            t, b = (A0, A1)[h // 2]
            return t[:, (h % 2) * 256:(h % 2) * 256 + 129], b
        def idx_scores(i):
            I, Ib = I2[i % 2]
            Dg, Dgb = Dg2[i % 2]
            nk = (i + 1) * 128
            ng = (nk + 511) // 512
            op("dve", lambda e, i=i, Dg=Dg: e.tensor_tensor(Dg[:], _bc_mid(cst[:, K_ID:K_ID + 128], 16), _bc_last(wh[:, i, :], 128), ALU.mult),
               reads=[cstb, whb], writes=[Dgb])
            for gp in range(0, ng, 2):
                gs = list(range(gp, min(gp + 2, ng)))
                items = [(hh, g) for hh in range(16) for g in gs]

                def score(k):
                    hh, g = items[k]
                    pp = (hh % 2) * 64
                    ncol = min(512, nk - g * 512)
                    ib, ibb = ibanks.next()
                    mm(ib[:, 0:ncol], qiT[pp:pp + 64, hh // 2, i * 128:(i + 1) * 128], kiT[pp:pp + 64, g * 512:g * 512 + ncol], True, True, [qiTb, kiTb], [ibb])
                    return ib, ibb
                pend = [score(0)]
                for k, (hh, g) in enumerate(items):
                    ncol = min(512, nk - g * 512)
                    ib, ibb = pend.pop(0)
                    r, rb_ = rr.next()
                    op("act", lambda e, r=r, ib=ib, ncol=ncol: e.activation(out=r[:, 0:ncol], in_=ib[:, 0:ncol], func=AF.Relu), reads=[ibb], writes=[rb_])
                    if k + 1 < len(items):
                        pend.append(score(k + 1))
                    ab_, abb_ = iacc[g - gp]
                    last = hh == 15 and g != i // 4
                    mm(ab_[:, 0:ncol], Dg[:, hh, :], r[:, 0:ncol], hh == 0, last, [Dgb, rb_], [abb_])
                    if hh == 15 and g == i // 4:
                        dc0 = i * 128 - g * 512
                        mm(ab_[:, dc0:dc0 + 128], ident_b, cstbf[:, K_MBIG:K_MBIG + 128], False, True, [cstbfb], [abb_])
                for g in gs:
                    ncol = min(512, nk - g * 512)
                    cs = slice(g * 512, g * 512 + ncol)
                    ab_, abb_ = iacc[g - gp]
                    op("act", lambda e, ab_=ab_, ncol=ncol, cs=cs, I=I: e.activation(out=I[:, cs], in_=ab_[:, 0:ncol], func=AF.Copy), reads=[abb_], writes=[Ib[g]])

        def select_attend(i):
            I, Ibl = I2[i % 2]
            nk = (i + 1) * 128
            Ib = Ibl[0:(nk + 511) // 512]
            if i >= 2 and "dsa_notopk" not in ctx["dbg"]:
                nround = TOPK // 8
                for rd in range(nround):
                    src = I if rd == 0 else work
                    srcb = Ib if rd == 0 else [workb]
                    op("dve", lambda e, src=src, nk=nk: e.max(out=m8[:], in_=src[:, 0:nk]), reads=srcb, writes=[m8b])
                    if rd < nround - 1:
                        op("dve", lambda e, src=src, nk=nk: e.match_replace(out=work[:, 0:nk], in_to_replace=m8[:], in_values=src[:, 0:nk], imm_value=-1e30),
                           reads=srcb + [m8b], writes=[workb])
                op("dve", lambda e: e.tensor_reduce(out=thr[:], in_=m8[:], axis=AX.X, op=ALU.min), reads=[m8b], writes=[thrb])
                th, thb = thr, thrb
            else:
                th, thb = thr0, thr0b
            op("dve", lambda e, nk=nk, th=th: e.tensor_scalar(selm[:, 0:nk], I[:, 0:nk], th[:, 0:1], None, ALU.is_ge), reads=Ib + [thb], writes=[selmb])
            for j0 in range(0, i + 1, 8):
                tb, tbb = tbanks[j0 // 8]
                tbv = bf(tb)
                nb = min(8, i + 1 - j0)
                for j in range(j0, j0 + nb):
                    tr(tbv[:, (j - j0) * 128:(j - j0 + 1) * 128], selm[:, j * 128:(j + 1) * 128], ident_b, [selmb, cstbfb], [tbb])
                evac_copy(mT[:, j0:j0 + nb, :].rearrange("p a b -> p (a b)"), tbv[:, 0:nb * 128], [tbb], [mTb])
            for j in range(i + 1):
                ty = 0 if j == i else (1 if j == i - 1 else 2)
                sk, skb = sbanks.next()
                mm(sk[:, :], kT[:, j * 128:(j + 1) * 128], qT[:, i, :, :].rearrange("p a b -> p (a b)"), True, False, [kTb, qTb], [skb])
                mm(sk[:, :], ident_b, biasT[:, ty, :, :].rearrange("p a b -> p (a b)"), False, True, [cstbfb, biasTb], [skb])
                ee, eeb = er.next()
                op("act", lambda e, ee=ee, sk=sk: e.activation(out=ee[:], in_=sk[:, :], func=AF.Exp), reads=[skb], writes=[eeb])
                pT, pTb = pr.next()
                op("dve", lambda e, pT=pT, ee=ee, j=j: e.tensor_tensor(pT[:], _v3(ee[:], 4), _bc_mid(mT[:, j, :], 4), ALU.mult),
                   reads=[eeb, mTb], writes=[pTb])
                for h in range(4):
                    a_ap, a_b = acc(h)
                    mm(a_ap, pT[:, h, :], Vp[:, j, :], j == 0 and h % 2 == 0, j == i, [pTb, Vpb], [a_b])
            for h in range(4):
                a_ap, a_b = acc(h)
                rc, rcb = rc_.next()
                op("dve", lambda e, rc=rc, a_ap=a_ap: e.reciprocal(rc[:], a_ap[:, 128:129]), reads=[a_b], writes=[rcb])
                op("dve", lambda e, rc=rc, a_ap=a_ap, i=i, h=h: e.tensor_scalar(
                    od[:, i, h * 128:(h + 1) * 128], a_ap[:, 0:128], rc[:, 0:1], None, ALU.mult), reads=[a_b, rcb], writes=[odb])
        if "dsa_noC" not in ctx["dbg"]:
            idx_scores(0)
        for i in range(NT):
            if "dsa_noC" in ctx["dbg"]:
                break
            if i + 1 < NT:
                idx_scores(i + 1)
            if "dsa_noattn" not in ctx["dbg"]:
                select_attend(i)
        stgr = rot(es, "dstg", [128, 1024], BF16, 2)
        for tt in range(NT):
            _store_oT(ctx, od[:, tt, :], odb, 4, 1024, tt, PSB[2], stgr)
        sch.barrier()


def mix_ssd(ctx, l, cv, cvb, rv, rvb):
    nc, sch, op, dma, sb, rot, PSB, bf = (ctx[k] for k in ("nc", "sch", "op", "dma", "sb", "rot", "PSB", "bf"))
    mm, tr, evac_copy = ctx["mm"], ctx["tr"], ctx["evac_copy"]
    cst, cstb, cstbf, cstbfb = ctx["cst"], ctx["cstb"], ctx["cstbf"], ctx["cstbfb"]
    ident_b = cstbf[:, K_ID:K_ID + 128]
    tri_f = cst[:, K_TRI:K_TRI + 128]
    su_f = cst[:, K_SU:K_SU + 128]
    ones_f = cst[:, K_ONES:K_ONES + 128]
    PT, PFX = ctx["PT"], ctx["PFX"]
    with contextlib.ExitStack() as es:
        xsT, xsTb = sb(es, "xsT", [128, NT, 1024], BF16)
        BT, BTb = sb(es, "sBT", [128, 2, S], BF16)
        CT, CTb = sb(es, "sCT", [128, 2, S], BF16)
        Btm, Btmb = sb(es, "sBtm", [128, NT, 2, 128], BF16)
        dt, dtb = sb(es, "sdt", [128, NT, 16], F32)
        aa, aab = sb(es, "saa", [128, NT, 16], F32)
        Abc, Abcb = sb(es, "sAbc", [128, 16], F32)
        op("act", lambda e: e.activation(out=Abc[:], in_=rv[:, R_ALOG:R_ALOG + 16], func=AF.Exp), reads=[rvb], writes=[Abcb])
        op("dve", lambda e: e.tensor_scalar(Abc[:], Abc[:], -1.0, None, ALU.mult), reads=[Abcb], writes=[Abcb])
        tbanks = Rot(PSB[0:4])
        with contextlib.ExitStack() as es2:
            xin_ = rot(es2, "sxin", [128, 3 + S], BF16, 2)
            for xin, xinb in xin_.items:
                op("dve", lambda e, xin=xin: e.memset(xin[:, 0:3], 0.0), writes=[xinb])
            acc_ = rot(es2, "sacc", [128, S], F32, 2)
            cvd_ = rot(es2, "scvd", [128, S], BF16, 2)
            for ch in range(12):
                xin, xinb = xin_.next()
                dma("sp", xin[:, 3:3 + S], PFX[ch * 128:(ch + 1) * 128, :], writes=[xinb])
                ac, acb = acc_.next()
                for k in range(4):
                    wcol = cv[:, V_CW + k * 12 + ch:V_CW + k * 12 + ch + 1]
                    if k == 0:
                        op("dve", lambda e, ac=ac, xin=xin, wcol=wcol: e.tensor_scalar(ac[:], xin[:, 0:S], wcol, None, ALU.mult),
                           reads=[xinb, cvb], writes=[acb])
                    else:
                        op("dve", lambda e, ac=ac, xin=xin, wcol=wcol, k=k: e.scalar_tensor_tensor(
                            out=ac[:], in0=xin[:, k:k + S], scalar=wcol, in1=ac[:], op0=ALU.mult, op1=ALU.add),
                           reads=[xinb, cvb, acb], writes=[acb])
                bcol = cv[:, V_CB + ch:V_CB + ch + 1]
                if ch < 8 or ch in (8, 9):
                    cd, cdb = cvd_.next()
                    if ch < 8:
                        op("act", lambda e, cd=cd, ac=ac, bcol=bcol: e.activation(out=cd[:], in_=ac[:], func=AF.Silu, bias=bcol, scale=1.0),
                           reads=[acb, cvb], writes=[cdb])
                        src, srcb = cd[:], cdb
                    else:
                        g = ch - 8
                        op("act", lambda e, g=g, ac=ac, bcol=bcol: e.activation(out=BT[:, g, :], in_=ac[:], func=AF.Silu, bias=bcol, scale=1.0),
                           reads=[acb, cvb], writes=[BTb])
                        src, srcb = BT[:, g, :], BTb
                    for t0 in range(0, NT, 8):
                        pb, pbb = tbanks.next()
                        pbv = bf(pb)
                        for tt in range(t0, t0 + 8):
                            tr(pbv[:, (tt - t0) * 128:(tt - t0 + 1) * 128], src[:, tt * 128:(tt + 1) * 128], ident_b, [srcb, cstbfb], [pbb])
                        if ch < 8:
                            evac_copy(xsT[:, t0:t0 + 8, ch * 128:(ch + 1) * 128], _v3(pbv[:, 0:1024], 8), [pbb], [xsTb])
                        else:
                            evac_copy(Btm[:, t0:t0 + 8, ch - 8, :], _v3(pbv[:, 0:1024], 8), [pbb], [Btmb])
                else:
                    g = ch - 10
                    op("act", lambda e, g=g, ac=ac, bcol=bcol: e.activation(out=CT[:, g, :], in_=ac[:], func=AF.Silu, bias=bcol, scale=1.0),
                       reads=[acb, cvb], writes=[CTb])
            dtT, dtTb = sb(es2, "sdtT", [128, S], BF16)
            dma("sp", dtT[0:16, :], PFX[1536:1552, :], writes=[dtTb])
            zz, zzb = sb(es2, "szz", [128, 16], F32)
            for tt in range(NT):
                pb, pbb = tbanks.next()
                pbv = bf(pb)
                tr(pbv[:, 0:16], dtT[0:16, tt * 128:(tt + 1) * 128], cstbf[0:16, K_ID:K_ID + 16], [dtTb, cstbfb], [pbb])
                op("dve", lambda e, pbv=pbv: e.tensor_tensor(zz[:], pbv[:, 0:16], rv[:, R_DTB:R_DTB + 16], ALU.add), reads=[pbb, rvb], writes=[zzb])
                op("act", lambda e: e.activation(out=zz[:], in_=zz[:], func=AF.Exp), reads=[zzb], writes=[zzb])
                op("act", lambda e, tt=tt: e.activation(out=dt[:, tt, :], in_=zz[:], func=AF.Ln, bias=1.0, scale=1.0), reads=[zzb], writes=[dtb])
            op("dve", lambda e: e.tensor_tensor(aa[:], dt[:], _bc_mid(Abc[:], NT), ALU.mult), reads=[dtb, Abcb], writes=[aab])
            sch.barrier()
        Sf, Sfb = sb(es, "sSf", [128, 2, 512], F32)
        Sb, Sbb = sb(es, "sSb", [128, 2, 512], BF16)
        op("dve", lambda e: e.memset(Sf[:], 0.0), writes=[Sfb])
        op("dve", lambda e: e.memset(Sb[:], 0.0), writes=[Sbb])
        ex3, ex3b = sb(es, "sex3", [128, 48], F32)
        R, Rb = sb(es, "sR", [128, 16, 128], F32)
        Lx, Lxb = sb(es, "sLx", [128, 16, 128], BF16)
        CBm, CBmb = sb(es, "sCBm", [128, 2, 128], BF16)
        MT, MTb = sb(es, "sMT", [128, 16, 128], BF16)
        xdt, xdtb = sb(es, "sxdt", [128, 16, 64], BF16)
        xdd, xddb = sb(es, "sxdd", [128, 16, 64], BF16)
        ysc, yscb = sb(es, "sysc", [128, 16, 64], F32)
        yy, yyb = sb(es, "syy", [128, 16, 64], F32)
        tmp, tmpb = sb(es, "stmp", [128, 16, 64], F32)
        ptz_ = rot(es, "sptz", [128, 1024], BF16, 2)
        sz, szb = sb(es, "ssz", [128, 1024], F32)
        ss, ssb = sb(es, "sss", [128, 2], F32)
        oo, oob = sb(es, "sso", [128, 1024], BF16)
        stgr = rot(es, "sstg", [128, 1024], BF16, 2)
        P = PSB
        for c in range(NT):
            cs = slice(c * 128, (c + 1) * 128)
            ptz, ptzb = ptz_.next()
            dma("sp", ptz[:], PT[cs, C_SZ:C_SZ + 1024], writes=[ptzb])
            p0, p0b = P[0]
            mm(p0[:, 0:16], tri_f, aa[:, c, :], True, True, [cstb, aab], [p0b])
            mm(p0[:, 16:32], su_f, aa[:, c, :], True, True, [cstb, aab], [p0b])
            mm(p0[:, 32:48], ones_f, aa[:, c, :], True, True, [cstb, aab], [p0b])
            op("act", lambda e: e.activation(out=ex3[:], in_=p0[:, 0:48], func=AF.Exp), reads=[p0b], writes=[ex3b])
            ea, edec, etot = ex3[:, 0:16], ex3[:, 16:32], ex3[:, 32:48]
            op("dve", lambda e, c=c: e.tensor_tensor(R[:], _bc_last(aa[:, c, :], 128), _bc_mid(tri_f, 16), ALU.mult), reads=[aab, cstb], writes=[Rb])
            for q in range(4):
                pq, pqb = P[1 + q]
                mm(pq[:, :], su_f, R[:, 4 * q:4 * q + 4, :].rearrange("p a b -> p (a b)"), True, True, [cstb, Rb], [pqb])
                op("act", lambda e, q=q, pq=pq: e.activation(out=Lx[:, 4 * q:4 * q + 4, :].rearrange("p a b -> p (a b)"), in_=pq[:, :], func=AF.Exp),
                   reads=[pqb], writes=[Lxb])
            p5, p5b = P[5]
            for g in range(2):
                mm(p5[:, g * 128:(g + 1) * 128], BT[:, g, cs], CT[:, g, cs], True, True, [BTb, CTb], [p5b])
            op("dve", lambda e: e.tensor_tensor(CBm[:], _v3(p5[:, 0:256], 2), _bc_mid(tri_f, 2), ALU.mult), reads=[p5b, cstb], writes=[CBmb])
            for g in range(2):
                op("dve", lambda e, g=g: e.tensor_tensor(MT[:, 8 * g:8 * g + 8, :], Lx[:, 8 * g:8 * g + 8, :], _bc_mid(CBm[:, g, :], 8), ALU.mult),
                   reads=[Lxb, CBmb], writes=[MTb])
            xs3 = _v3(xsT[:, c, :], 16)
            op("dve", lambda e, xs3=xs3, c=c: e.tensor_tensor(xdt[:], xs3, _bc_last(dt[:, c, :], 64), ALU.mult), reads=[xsTb, dtb], writes=[xdtb])
            op("dve", lambda e: e.tensor_tensor(xdd[:], xdt[:], _bc_last(edec, 64), ALU.mult), reads=[xdtb, ex3b], writes=[xddb])
            for h in range(16):
                py, pyb = P[6 + h // 8]
                mm(py[:, (h % 8) * 64:(h % 8 + 1) * 64], MT[:, h, :], xdt[:, h, :], True, True, [MTb, xdtb], [pyb])
            for g in range(2):
                pq, pqb = P[1 + g]
                mm(pq[:, :], CT[:, g, cs], Sb[:, g, :], True, True, [CTb, Sbb], [pqb])
                op("dve", lambda e, g=g, pq=pq: e.tensor_tensor(ysc[:, 8 * g:8 * g + 8, :], _v3(pq[:, :], 8), _bc_last(ea[:, 8 * g:8 * g + 8], 64), ALU.mult),
                   reads=[pqb, ex3b], writes=[yscb])
                py, pyb = P[6 + g]
                op("dve", lambda e, g=g, py=py: e.tensor_tensor(yy[:, 8 * g:8 * g + 8, :], ysc[:, 8 * g:8 * g + 8, :], _v3(py[:, :], 8), ALU.add),
                   reads=[yscb, pyb], writes=[yyb])
            op("dve", lambda e, xs3=xs3: e.tensor_tensor(tmp[:], xs3, _bc_last(rv[:, R_SD:R_SD + 16], 64), ALU.mult), reads=[xsTb, rvb], writes=[tmpb])
            op("dve", lambda e: e.tensor_tensor(yy[:], yy[:], tmp[:], ALU.add), reads=[yyb, tmpb], writes=[yyb])
            for g in range(2):
                pu, pub = P[3 + g]
                mm(pu[:, :], Btm[:, c, g, :], xdd[:, 8 * g:8 * g + 8, :].rearrange("p a b -> p (a b)"), True, True, [Btmb, xddb], [pub])
                op("dve", lambda e, g=g: e.tensor_tensor(_v3(Sf[:, g, :], 8), _v3(Sf[:, g, :], 8), _bc_last(etot[:, 8 * g:8 * g + 8], 64), ALU.mult),
                   reads=[Sfb, ex3b], writes=[Sfb])
                op("dve", lambda e, g=g, pu=pu: e.tensor_tensor(Sf[:, g, :], Sf[:, g, :], pu[:, :], ALU.add), reads=[Sfb, pub], writes=[Sfb])
            op("act", lambda e: e.activation(out=Sb[:], in_=Sf[:], func=AF.Copy), reads=[Sfb], writes=[Sbb])
            yf = yy[:].rearrange("p a b -> p (a b)")
            op("act", lambda e, ptz=ptz: e.activation(out=sz[:], in_=ptz[:], func=AF.Silu), reads=[ptzb], writes=[szb])
            op("dve", lambda e, yf=yf: e.tensor_tensor(yf, yf, sz[:], ALU.mult), reads=[yyb, szb], writes=[yyb])
            op("dve", lambda e, yf=yf: e.tensor_tensor(sz[:], yf, yf, ALU.mult), reads=[yyb], writes=[szb])
            op("dve", lambda e: e.tensor_reduce(out=ss[:], in_=_v3(sz[:], 2), axis=AX.X, op=ALU.add), reads=[szb], writes=[ssb])
            op("act", lambda e: e.activation(out=ss[:], in_=ss[:], func=AF.Sqrt, bias=EPS, scale=1.0 / 512), reads=[ssb], writes=[ssb])
            op("dve", lambda e: e.reciprocal(ss[:], ss[:]), reads=[ssb], writes=[ssb])
            op("dve", lambda e, yf=yf: e.tensor_tensor(_v3(sz[:], 2), _v3(yf, 2), _bc_last(ss[:], 512), ALU.mult), reads=[yyb, ssb], writes=[szb])
            op("dve", lambda e: e.tensor_tensor(oo[:], sz[:], rv[:, R_SNG:R_SNG + 1024], ALU.mult), reads=[szb, rvb], writes=[oob])
            _store_oT(ctx, oo[:], oob, 8, 1536, c, P[0], stgr)
        sch.barrier()


def _t5_bucket(d):
    d = np.maximum(d, 0)
    max_exact = 16
    lr = np.log(np.maximum(d, 1).astype(np.float32) / max_exact) / math.log(128 / max_exact)
    large = np.minimum(max_exact + (lr * 16).astype(np.int32), 31)
    return np.where(d < max_exact, d, large)


def host_consts():
    i = np.arange(128)
    cst = np.zeros((128, NCST), np.float32)
    cst[:, K_ID:K_ID + 128] = np.eye(128)
    cst[:, K_ONES:K_ONES + 128] = 1.0
    cst[:, K_TRI:K_TRI + 128] = (i[None, :] >= i[:, None])
    cst[:, K_MNEG:K_MNEG + 128] = np.where(i[None, :] >= i[:, None], 0.0, NEG)
    cst[:, K_SU:K_SU + 128] = (i[:, None] > i[None, :])
    lg = np.log1p(-np.exp2(-5.0 - np.arange(4, dtype=np.float64)))
    rel = (i[None, :] - i[:, None]).astype(np.float64)
    for h in range(4):
        cst[:, K_DEC + h * 128:K_DEC + (h + 1) * 128] = np.where(rel >= 0, np.exp(lg[h] * np.maximum(rel, 0)), 0.0)
        cst[:, K_QDEC + h] = np.exp(lg[h] * (i + 1.0))
        cst[:, K_KDEC + h] = np.exp(lg[h] * (127.0 - i))
        cst[:, K_CDEC + h] = np.exp(lg[h] * 128.0)
        cst[h, K_SEL + h * 128:K_SEL + (h + 1) * 128] = 1.0
    cst[:, K_MBIG:K_MBIG + 128] = np.where(i[None, :] <= i[:, None], 0.0, -1e30)
    oh = np.zeros((128, 2, 32, 128), np.float32)
    for ty in range(2):
        dist = i[None, :] - i[:, None] + 128 * ty
        bk = _t5_bucket(dist)
        for b in range(32):
            oh[:, ty, b, :] = ((bk == b) & (dist >= 0))
    oh = oh.reshape(128, 8192)
    pos = np.arange(S, dtype=np.float32)
    inv = (1.0 / (10000.0 ** (np.arange(64, dtype=np.float32) / 64))).astype(np.float32)
    ang = pos[:, None] * inv[None, :]
    rot = np.concatenate([np.cos(ang), np.sin(ang), np.cos(ang) * (128 ** -0.5), np.sin(ang) * (128 ** -0.5)], axis=1).astype(np.float32)
    return cst, oh, rot


def host_params(inp):
    colv = np.zeros((L, NV, 128), np.float32)
    rowv = np.zeros((L, 1, NR), np.float32)
    for l in range(L):
        colv[l, V_N1:V_N1 + 16] = inp["norm1_g"][l].reshape(16, 128)
        colv[l, V_N2:V_N2 + 16] = inp["norm2_g"][l].reshape(16, 128)
        colv[l, V_GB:V_GB + 64] = inp["gate_b"][l].reshape(64, 128)
        colv[l, V_CW:V_CW + 48] = inp["ssd_conv_w"][l].reshape(48, 128)
        colv[l, V_CB:V_CB + 12] = inp["ssd_conv_b"][l].reshape(12, 128)
        r = rowv[l, 0]
        r[R_FB:R_FB + 4] = inp["fox_f_b"][l]
        r[R_FQG:R_FQG + 128] = inp["fox_qn_g"][l]
        r[R_FKG:R_FKG + 128] = inp["fox_kn_g"][l]
        r[R_CQG:R_CQG + 512] = inp["dsa_cq_g"][l]
        r[R_DQG:R_DQG + 128] = inp["dsa_qn_g"][l]
        r[R_DKG:R_DKG + 128] = inp["dsa_kn_g"][l]
        r[R_DTB:R_DTB + 16] = inp["ssd_dt_bias"][l]
        r[R_ALOG:R_ALOG + 16] = inp["ssd_a_log"][l]
        r[R_SD:R_SD + 16] = inp["ssd_d"][l]
        r[R_SNG:R_SNG + 1024] = inp["ssd_norm_g"][l]
    relb = np.ascontiguousarray(inp["rel_bias"].reshape(1, 128)).astype(np.float32)
    return colv, rowv, relb


CORE_OF_BATCH = [0, 1, 4, 5]


def make_in_maps(inp, n_cores=8):
    cst, oh, rot = host_consts()
    colv, rowv, relb = host_params(inp)
    shared = dict(
        w_in=np.ascontiguousarray(inp["w_in"]), w_br=np.ascontiguousarray(inp["w_br"]),
        w_out=np.ascontiguousarray(inp["w_out"]), w_ff1=np.ascontiguousarray(inp["w_ff1"]),
        w_ff2=np.ascontiguousarray(inp["w_ff2"]), w_uq=np.ascontiguousarray(inp["dsa_w_uq"]),
        w_qidx=np.ascontiguousarray(inp["dsa_w_qidx"]), colv=colv, rowv=rowv, relb=relb, cst=cst, oh=oh, rot=rot)
    maps = []
    if n_cores == 8:
        zeros = {k: np.zeros_like(v) for k, v in shared.items()}
        zx = np.zeros((D, S), np.float32)
        for c in range(n_cores):
            if c in CORE_OF_BATCH:
                m = dict(shared)
                m["xT"] = np.ascontiguousarray(inp["x"][CORE_OF_BATCH.index(c)].T)
            else:
                m = dict(zeros)
                m["xT"] = zx
            maps.append(m)
        return maps
    for c in range(n_cores):
        m = dict(shared)
        m["xT"] = np.ascontiguousarray(inp["x"][c % 4].T)
        maps.append(m)
    return maps


def kernel(**inputs):
    inp = {k: np.asarray(v) for k, v in inputs.items()}
    nc = build(L)
    maps = make_in_maps(inp, 8)
    res = run_bass_kernel_spmd(nc, maps, core_ids=list(range(8)))
    out = np.stack([np.ascontiguousarray(res.results[CORE_OF_BATCH[b]]["yT"].T) for b in range(4)], axis=0)
    return out.astype(np.float32)
```

```python
import contextlib
import math
import numpy as np
import concourse.bass as bass
import concourse.mybir as mybir
from concourse.bass_utils import run_bass_kernel_spmd

F32 = mybir.dt.float32
BF16 = mybir.dt.bfloat16
AF = mybir.ActivationFunctionType
ALU = mybir.AluOpType
AX = mybir.AxisListType

D = 2048
S = 2048
L = 4
KC = 16
NT = 16
IN_TOTAL = 15204
EPS = 1e-6
NEG = -30000.0

C_RQ, C_RK, C_RV, C_RG = 0, 512, 1024, 1536
C_FQ, C_FK, C_FV, C_FF = 2048, 2560, 3072, 3584
C_DCQ, C_DK, C_DV, C_IK, C_IW = 3588, 4100, 4228, 4356, 4420
C_SZ, C_SX, C_SDT = 4436, 5460, 6996
C_G = 7012
TM_END = 5460

R_FB, R_FQG, R_FKG, R_CQG, R_DQG, R_DKG, R_DTB, R_ALOG, R_SD, R_SNG = 0, 4, 132, 260, 772, 900, 1028, 1044, 1060, 1076
NR = 2100
V_N1, V_N2, V_GB, V_CW, V_CB = 0, 16, 32, 96, 144
NV = 256

K_ID, K_ONES, K_TRI, K_MNEG, K_SU, K_DEC, K_QDEC, K_KDEC, K_CDEC, K_SEL, K_MBIG = 0, 128, 256, 384, 512, 640, 1152, 1156, 1160, 1164, 1676
NCST = 1804


class Buf:
    __slots__ = ("w", "r", "name")

    def __init__(self, name=""):
        self.w = None
        self.r = {}
        self.name = name


class Sched:
    EPOCH = 60000

    def __init__(self, nc, es, n_dma_sems=32):
        self.nc = nc
        self.es = es
        self.engs = {"pe": nc.tensor, "act": nc.scalar, "dve": nc.vector, "pool": nc.gpsimd, "sp": nc.sync}
        self.sem, self.cnt, self.semkey = {}, {}, {}
        self.nsem = 0
        for e in self.engs:
            self._new_eng_sem(e)
        self.seen = {e: {} for e in self.engs}
        self.dma_sems = []
        for i in range(n_dma_sems):
            s = es.enter_context(nc.semaphore(f"dma{i}"))
            self.dma_sems.append([f"dma{i}", s, 0])
        self.dma_next = 0
        self.n_ops = 0
        self.n_waits = 0

    def _new_eng_sem(self, e):
        self.nsem += 1
        key = f"{e}{self.nsem}"
        self.sem[e] = self.es.enter_context(self.nc.semaphore(key))
        self.semkey[e] = key
        self.cnt[e] = 0

    def _wait(self, eng, tok):
        key, sem, val, teng = tok
        if self.seen[eng].get(key, 0) >= val:
            return
        self.engs[eng].wait_ge(sem, val)
        self.seen[eng][key] = val
        self.n_waits += 1

    def _deps(self, eng, reads, writes, skip_same):
        for b in reads:
            if b.w is not None and not (skip_same and b.w[3] == eng):
                self._wait(eng, b.w)
        for b in writes:
            if b.w is not None and not (skip_same and b.w[3] == eng):
                self._wait(eng, b.w)
            for tok in b.r.values():
                if not (skip_same and tok[3] == eng):
                    self._wait(eng, tok)

    def _commit(self, tok, reads, writes):
        for b in reads:
            b.r[tok[0]] = tok
        for b in writes:
            b.w = tok
            b.r = {}

    def op(self, eng, fn, reads=(), writes=(), skip_same=False):
        self._deps(eng, reads, writes, skip_same)
        ins = fn(self.engs[eng])
        if self.cnt[eng] >= self.EPOCH:
            self._new_eng_sem(eng)
        self.cnt[eng] += 1
        ins.then_inc(self.sem[eng], 1)
        tok = (self.semkey[eng], self.sem[eng], self.cnt[eng], eng)
        self._commit(tok, reads, writes)
        self.n_ops += 1
        return tok

    def dma(self, q, out, in_, reads=(), writes=(), **kw):
        self._deps(q, reads, writes, False)
        ent = self.dma_sems[self.dma_next]
        self.dma_next = (self.dma_next + 1) % len(self.dma_sems)
        key, sem, val = ent
        if val > 0:
            self._wait(q, (key, sem, val, "dma"))
        self.engs[q].dma_start(out=out, in_=in_, **kw).then_inc(sem, 16)
        ent[2] = val + 16
        tok = (key, sem, val + 16, "dma")
        self._commit(tok, reads, writes)
        self.n_ops += 1
        return tok

    def barrier(self):
        toks = []
        for e in self.engs:
            if self.cnt[e] > 0:
                toks.append((self.semkey[e], self.sem[e], self.cnt[e], e))
        for key, sem, val in self.dma_sems:
            if val > 0:
                toks.append((key, sem, val, "dma"))
        for e in self.engs:
            for t in toks:
                self._wait(e, t)

    def final_wait(self, eng="sp"):
        for key, sem, val in self.dma_sems:
            if val > 0:
                self._wait(eng, (key, sem, val, "dma"))


class Rot:
    def __init__(self, items):
        self.items = items
        self.i = 0

    def next(self):
        it = self.items[self.i]
        self.i = (self.i + 1) % len(self.items)
        return it


def build(nl=L, dbg=(), LW=L):
    dbg = set(dbg)
    nc = bass.Bass("TRN2", target_bir_lowering=False)

    def din(name, shape, dt=F32):
        return nc.dram_tensor(name, list(shape), dt, kind="ExternalInput").ap()

    def dscr(name, shape, dt):
        if "dump" in dbg:
            return nc.dram_tensor(name, list(shape), dt, kind="ExternalOutput").ap()
        return nc.dram_tensor(name, list(shape), dt, kind="Internal").ap()

    xT_in = din("xT", [D, S])
    yT = nc.dram_tensor("yT", [D, S], F32, kind="ExternalOutput").ap()
    w_in = din("w_in", [LW, D, IN_TOTAL])
    w_br = din("w_br", [LW, 2560, D])
    w_out = din("w_out", [LW, D, D])
    w_ff1 = din("w_ff1", [LW, D, 4 * D])
    w_ff2 = din("w_ff2", [LW, 4 * D, D])
    w_uq = din("w_uq", [LW, 512, 512])
    w_qidx = din("w_qidx", [LW, 512, 1024])
    colv = din("colv", [LW, NV, 128])
    rowv = din("rowv", [LW, 1, NR])
    relb = din("relb", [1, 128])
    cst_d = din("cst", [128, NCST])
    oh_d = din("oh", [128, 8192])
    rot_d = din("rot", [S, 256])

    PT = dscr("PT", [S, TM_END], BF16)
    PFX = dscr("PFX", [1552, S], BF16)
    PFI = dscr("PFI", [128, S], BF16)
    PFG = dscr("PFG", [4 * D, S], BF16)
    if "ext_ot" in dbg:
        OT = din("OT", [2560, S], BF16)
    else:
        OT = dscr("OT", [2560, S], BF16)
    WB1 = nc.dram_tensor("WB1", [32, 128, KC * 256], BF16, kind="Internal").ap()
    WB2 = nc.dram_tensor("WB2", [32, 128, 32 * 128], BF16, kind="Internal").ap()
    XA = dscr("XA", [D, S], F32)
    XB = dscr("XB", [D, S], F32)

    ges = contextlib.ExitStack()
    with ges:
        sch = Sched(nc, ges)
        op, dma = sch.op, sch.dma

        uid = [0]

        def sb(es, name, shape, dt, nb=1):
            uid[0] += 1
            t = es.enter_context(nc.sbuf_tensor(f"s{uid[0]}_{name}", list(shape), dt))
            if nb == 1:
                return t, Buf(name)
            return t, [Buf(f"{name}{i}") for i in range(nb)]

        def rot(es, name, shape, dt, n):
            return Rot([sb(es, f"{name}{i}", shape, dt) for i in range(n)])

        PSB = []
        for i in range(8):
            t = ges.enter_context(nc.psum_tensor(f"ps{i}", [128, 512], F32))
            PSB.append((t, Buf(f"ps{i}")))

        def bf(ps_t):
            return ps_t[:, :].bitcast(BF16)

        cst, cstb = sb(ges, "cst", [128, NCST], F32)
        cstbf, cstbfb = sb(ges, "cstbf", [128, NCST], BF16)
        dma("sp", cst[:], cst_d, writes=[cstb])
        dma("pool", cstbf[:], cst_d, writes=[cstbfb])
        ident_f = cst[:, K_ID:K_ID + 128]
        ones_f = cst[:, K_ONES:K_ONES + 128]
        tri_f = cst[:, K_TRI:K_TRI + 128]
        su_f = cst[:, K_SU:K_SU + 128]
        ident_b = cstbf[:, K_ID:K_ID + 128]
        mneg_b = cstbf[:, K_MNEG:K_MNEG + 128]
        tri_b = cstbf[:, K_TRI:K_TRI + 128]
        biasT, biasTb = sb(ges, "biasT", [128, 3, 4, 128], BF16)
        with contextlib.ExitStack() as es:
            ohs, ohsb = sb(es, "ohs", [128, 64, 128], BF16)
            rb, rbb = sb(es, "rb", [128, 128], F32)
            acc, accb = sb(es, "bacc", [128, 2, 4, 128], F32)
            dma("pool", ohs[:].rearrange("p a b -> p (a b)"), oh_d, writes=[ohsb])
            dma("sp", rb[:], relb.partition_broadcast(128).rearrange("p a b -> p (a b)"), writes=[rbb])
            for ty in range(2):
                for h in range(4):
                    for b in range(32):
                        col = rb[:, b * 4 + h:b * 4 + h + 1]
                        if b == 0:
                            op("dve", lambda e, ty=ty, h=h, b=b, col=col: e.tensor_scalar(
                                acc[:, ty, h, :], ohs[:, ty * 32 + b, :], col, None, ALU.mult),
                               reads=[ohsb, rbb], writes=[accb])
                        else:
                            op("dve", lambda e, ty=ty, h=h, b=b, col=col: e.scalar_tensor_tensor(
                                out=acc[:, ty, h, :], in0=ohs[:, ty * 32 + b, :], scalar=col, in1=acc[:, ty, h, :],
                                op0=ALU.mult, op1=ALU.add), reads=[ohsb, rbb, accb], writes=[accb])
            op("dve", lambda e: e.tensor_copy(biasT[:, 0:2, :, :], acc[:]), reads=[accb], writes=[biasTb])
            for h in range(4):
                col = rb[:, 31 * 4 + h:31 * 4 + h + 1]
                op("dve", lambda e, h=h, col=col: e.tensor_scalar(
                    biasT[:, 2, h, :], cst[:, K_ONES:K_ONES + 128], col, None, ALU.mult),
                   reads=[cstb, rbb], writes=[biasTb])
            sch.barrier()

        evac_flip = [0]

        def evac_copy(out_ap, in_ap, reads, writes):
            evac_flip[0] ^= 1
            if evac_flip[0]:
                return op("act", lambda e: e.activation(out=out_ap, in_=in_ap, func=AF.Copy), reads=reads, writes=writes)
            return op("dve", lambda e: e.tensor_copy(out_ap, in_ap), reads=reads, writes=writes)

        def mm(out_ap, lhsT, rhs, start, stop, reads, writes):
            return op("pe", lambda e: e.matmul(out_ap, lhsT, rhs, start=start, stop=stop),
                      reads=reads, writes=writes, skip_same=True)

        def tr(out_ap, in_ap, ident, reads, writes):
            return op("pe", lambda e: e.transpose(out_ap, in_ap, ident), reads=reads, writes=writes, skip_same=True)

        def view_kc(ap2d):
            return ap2d.rearrange("(kc p) c -> p kc c", p=128)

        def load_params(es, l):
            cv, cvb = sb(es, "cv", [128, NV], F32)
            rv, rvb = sb(es, "rv", [128, NR], F32)
            with contextlib.ExitStack() as es2:
                raw, rawb = sb(es2, "cvraw", [128, 2, 128], F32)
                dma("sp", raw[:], colv[l].rearrange("(a p) c -> p a c", p=128), writes=[rawb])
                dma("sp", rv[:], rowv[l].partition_broadcast(128).rearrange("p a b -> p (a b)"), writes=[rvb])
                pt, ptb = PSB[0]
                for a in range(2):
                    tr(pt[:, a * 128:(a + 1) * 128], raw[:, a, :], ident_f, [rawb, cstb], [ptb])
                op("dve", lambda e: e.tensor_copy(cv[:], pt[:, 0:256]), reads=[ptb], writes=[cvb])
                op("dve", lambda e: e.tensor_scalar(cv[:, 0:32], cv[:, 0:32], math.sqrt(D), None, ALU.mult),
                   reads=[cvb], writes=[cvb])
                sch.barrier()
            return cv, cvb, rv, rvb

        def norm_group(xsrc, tcols, gcol0, cv, cvb, xt, xtb, hT, hcols, hbuf, sqr, rs, rsb):
            pA, pAb = PSB[7]
            xv = view_kc(xsrc)
            for kc in range(KC):
                dma("sp", xt[:, kc, :], xv[:, kc, tcols], writes=[xtb[kc]])
            for kc in range(KC):
                sq, sqb = sqr.next()
                op("act", lambda e, kc=kc, sq=sq: e.activation(out=sq[:], in_=xt[:, kc, :], func=AF.Square),
                   reads=[xtb[kc]], writes=[sqb])
                mm(pA[:, :], ones_f, sq[:], kc == 0, kc == KC - 1, [sqb, cstb], [pAb])
            op("act", lambda e: e.activation(out=rs[:], in_=pA[:, :], func=AF.Sqrt, bias=float(D * EPS), scale=1.0),
               reads=[pAb], writes=[rsb])
            op("dve", lambda e: e.reciprocal(rs[:], rs[:]), reads=[rsb], writes=[rsb])
            for kc in range(KC):
                op("dve", lambda e, kc=kc: e.scalar_tensor_tensor(
                    out=hT[:, kc, hcols], in0=xt[:, kc, :], scalar=cv[:, gcol0 + kc:gcol0 + kc + 1], in1=rs[:],
                    op0=ALU.mult, op1=ALU.mult), reads=[xtb[kc], cvb, rsb], writes=[hbuf])

        def phase1(l, xsrc, cv, cvb):
            with contextlib.ExitStack() as es:
                hT, hTb = sb(es, "hT", [128, KC, S], BF16, nb=4)
                with contextlib.ExitStack() as es2:
                    xt, xtb = sb(es2, "xt", [128, KC, 512], F32, nb=KC)
                    sqr = rot(es2, "sq", [128, 512], F32, 2)
                    rs, rsb = sb(es2, "rs", [128, 512], F32)
                    for tg in range(4):
                        cols = slice(tg * 512, (tg + 1) * 512)
                        norm_group(xsrc, cols, V_N1, cv, cvb, xt, xtb, hT, cols, hTb[tg], sqr, rs, rsb)
                    sch.barrier()
                wr = rot(es, "w1t", [128, KC, 528], BF16, 3)
                stg = rot(es, "stg", [128, 512], BF16, 4)
                banks = Rot(PSB[0:6])
                wv = view_kc(w_in[l])
                jobs = []
                c = 0
                while c < TM_END:
                    wd = min(512, TM_END - c)
                    jobs.append(("tm", c, wd))
                    c += wd
                jobs.append(("ik", C_IK, 64))
                jobs += [("fx", 5460, 512, 0), ("fx", 5972, 512, 512), ("fx", 6484, 528, 1024)]
                for i in range(16):
                    jobs.append(("fg", C_G + i * 512, 512, i * 512))

                def load(j):
                    wt, wtb = wr.items[j % 3]
                    job = jobs[j]
                    if job[0] == "ik":
                        dma("pool", wt[:, :, 0:64], wv[:, :, C_IK:C_IK + 64], writes=[wtb])
                        dma("pool", wt[:, :, 64:128], wv[:, :, C_IK:C_IK + 64], writes=[wtb])
                    else:
                        dma("pool", wt[:, :, 0:job[2]], wv[:, :, job[1]:job[1] + job[2]], writes=[wtb])

                load(0)
                load(1)
                for j, job in enumerate(jobs):
                    wt, wtb = wr.items[j % 3]
                    if job[0] == "tm":
                        _, c0, wd = job
                        for tt in range(NT):
                            pb, pbb = banks.next()
                            for kc in range(KC):
                                mm(pb[:, 0:wd], hT[:, kc, tt * 128:(tt + 1) * 128], wt[:, kc, 0:wd], kc == 0, kc == KC - 1,
                                   [hTb[tt // 4], wtb], [pbb])
                            st, stb = stg.next()
                            evac_copy(st[:, 0:wd], pb[:, 0:wd], [pbb], [stb])
                            dma("sp", PT[tt * 128:(tt + 1) * 128, c0:c0 + wd], st[:, 0:wd], reads=[stb])
                    else:
                        if job[0] == "ik":
                            slices = [(0, 128, PFI, 0)]
                        elif job[0] == "fx":
                            slices = []
                            s0 = 0
                            while s0 < job[2]:
                                m = min(128, job[2] - s0)
                                slices.append((s0, m, PFX, job[3] + s0))
                                s0 += m
                        else:
                            slices = [(s0, 128, PFG, job[3] + s0) for s0 in range(0, 512, 128)]
                        for (s0, m, dst, r0) in slices:
                            for tg in range(4):
                                pb, pbb = banks.next()
                                for kc in range(KC):
                                    mm(pb[0:m, :], wt[:, kc, s0:s0 + m], hT[:, kc, tg * 512:(tg + 1) * 512], kc == 0, kc == KC - 1,
                                       [hTb[tg], wtb], [pbb])
                                st, stb = stg.next()
                                if job[0] == "fg":
                                    gch = r0 // 128
                                    op("act", lambda e, st=st, pb=pb, gch=gch: e.activation(
                                        out=st[:, :], in_=pb[:, :], func=AF.Sigmoid,
                                        bias=cv[:, V_GB + gch:V_GB + gch + 1], scale=1.0), reads=[pbb, cvb], writes=[stb])
                                else:
                                    evac_copy(st[0:m, :], pb[0:m, :], [pbb], [stb])
                                dma("sp", dst[r0:r0 + m, tg * 512:(tg + 1) * 512], st[0:m, :], reads=[stb])
                    if j + 2 < len(jobs):
                        load(j + 2)
                sch.barrier()

        def phase3(l, xsrc, xdst):
            with contextlib.ExitStack() as es:
                oT, oTb = sb(es, "oT", [128, 20, 1024], BF16)
                mT, mTb = sb(es, "mT", [128, KC, 1024], BF16, nb=KC)
                wr = rot(es, "w3t", [128, 20, 128], BF16, 3)
                gr = rot(es, "g3t", [128, 4, 1024], BF16, 2)
                accr = rot(es, "acc3", [128, 512], F32, 2)
                tmpr = rot(es, "tmp3", [128, 512], F32, 2)
                xr_ = rot(es, "xr3", [128, 1024], F32, 2)
                xo_ = rot(es, "xo3", [128, 512], F32, 3)
                banks = Rot(PSB[0:6])
                wbv = view_kc(w_br[l])
                wov = view_kc(w_out[l])
                gv = PFG.rearrange("(i dc p) t -> p i dc t", p=128, i=4)
                otv = view_kc(OT)
                xv = view_kc(xsrc)
                xdv = view_kc(xdst)
                brk = [(0, 4), (4, 8), (8, 12), (12, 20)]
                for hf in range(2):
                    hcols = slice(hf * 1024, (hf + 1) * 1024)
                    for kc in range(20):
                        dma("sp", oT[:, kc, :], otv[:, kc, hcols], writes=[oTb])
                    for dc in range(KC):
                        wt, wtb = wr.next()
                        dma("pool", wt[:], wbv[:, :, dc * 128:(dc + 1) * 128], writes=[wtb])
                        gt, gtb = gr.next()
                        dma("sp", gt[:], gv[:, :, dc, hcols], writes=[gtb])
                        for t2 in range(2):
                            cols = slice(t2 * 512, (t2 + 1) * 512)
                            ac, acb = accr.next()
                            for i in range(4):
                                pb, pbb = banks.next()
                                k0, k1 = brk[i]
                                for kc in range(k0, k1):
                                    mm(pb[:, :], wt[:, kc, :], oT[:, kc, cols], kc == k0, kc == k1 - 1, [wtb, oTb], [pbb])
                                if i == 0:
                                    op("dve", lambda e, ac=ac, pb=pb, gt=gt, cols=cols: e.tensor_tensor(
                                        ac[:], pb[:, :], gt[:, 0, cols], ALU.mult), reads=[pbb, gtb], writes=[acb])
                                else:
                                    tp, tpb = tmpr.next()
                                    op("dve", lambda e, tp=tp, pb=pb, gt=gt, cols=cols, i=i: e.tensor_tensor(
                                        tp[:], pb[:, :], gt[:, i, cols], ALU.mult), reads=[pbb, gtb], writes=[tpb])
                                    if i < 3:
                                        op("dve", lambda e, ac=ac, tp=tp: e.tensor_tensor(ac[:], ac[:], tp[:], ALU.add),
                                           reads=[acb, tpb], writes=[acb])
                                    else:
                                        op("dve", lambda e, ac=ac, tp=tp, dc=dc, cols=cols: e.tensor_tensor(
                                            mT[:, dc, cols], ac[:], tp[:], ALU.add), reads=[acb, tpb], writes=[mTb[dc]])
                    for dd in range(KC):
                        wt, wtb = wr.next()
                        dma("pool", wt[:, 0:KC, :], wov[:, :, dd * 128:(dd + 1) * 128], writes=[wtb])
                        xr, xrb = xr_.next()
                        dma("sp", xr[:], xv[:, dd, hcols], writes=[xrb])
                        for t2 in range(2):
                            cols = slice(t2 * 512, (t2 + 1) * 512)
                            pb, pbb = banks.next()
                            for kc in range(KC):
                                mm(pb[:, :], wt[:, kc, :], mT[:, kc, cols], kc == 0, kc == KC - 1, [wtb, mTb[kc]], [pbb])
                            xo, xob = xo_.next()
                            op("dve", lambda e, xo=xo, pb=pb, xr=xr, cols=cols: e.tensor_tensor(
                                xo[:], pb[:, :], xr[:, cols], ALU.add), reads=[pbb, xrb], writes=[xob])
                            dma("sp", xdv[:, dd, hf * 1024 + t2 * 512:hf * 1024 + (t2 + 1) * 512], xo[:], reads=[xob])
                sch.barrier()

        def phase4(l, xsrc, xdst, cv, cvb):
            with contextlib.ExitStack() as es:
                xt, xtb = sb(es, "xt4", [128, KC, 512], F32, nb=KC)
                sqr = rot(es, "sq4", [128, 512], F32, 2)
                rs, rsb = sb(es, "rs4", [128, 512], F32)
                h2, h2b = sb(es, "h2T", [128, KC, 512], BF16)
                aT, aTb = sb(es, "aT", [128, 64, 512], BF16, nb=64)
                w1r = rot(es, "w41", [128, KC, 256], BF16, 3)
                w2r = rot(es, "w42", [128, 32, 128], BF16, 3)
                rl_ = rot(es, "rl4", [128, 512], BF16, 2)
                xo_ = rot(es, "xo4", [128, 512], F32, 2)
                banks = Rot(PSB[0:6])
                w1v = view_kc(w_ff1[l])
                w2v = view_kc(w_ff2[l])
                xdv = view_kc(xdst)
                wb1b = [Buf(f"wb1_{i}") for i in range(32)]
                wb2b = [Buf(f"wb2_{i}") for i in range(32)]
                for tg in range(4):
                    cols = slice(tg * 512, (tg + 1) * 512)
                    norm_group(xsrc, cols, V_N2, cv, cvb, xt, xtb, h2, slice(0, 512), h2b, sqr, rs, rsb)
                    for fg in range(32):
                        wt, wtb = w1r.next()
                        if tg == 0:
                            dma("pool", wt[:], w1v[:, :, fg * 256:(fg + 1) * 256], writes=[wtb])
                            dma("sp", WB1[fg], wt[:].rearrange("p a b -> p (a b)"), reads=[wtb], writes=[wb1b[fg]])
                        else:
                            dma("pool", wt[:].rearrange("p a b -> p (a b)"), WB1[fg], reads=[wb1b[fg]], writes=[wtb])
                        for fs in range(2):
                            fc = fg * 2 + fs
                            pb, pbb = banks.next()
                            for kc in range(KC):
                                mm(pb[:, :], wt[:, kc, fs * 128:(fs + 1) * 128], h2[:, kc, :], kc == 0, kc == KC - 1, [wtb, h2b], [pbb])
                            rl, rlb = rl_.next()
                            op("act", lambda e, rl=rl, pb=pb: e.activation(out=rl[:], in_=pb[:, :], func=AF.Relu),
                               reads=[pbb], writes=[rlb])
                            op("dve", lambda e, rl=rl, fc=fc: e.tensor_tensor(aT[:, fc, :], rl[:], rl[:], ALU.mult),
                               reads=[rlb], writes=[aTb[fc]])
                    for dd in range(KC):
                        pb, pbb = banks.next()
                        for hf in range(2):
                            wt, wtb = w2r.next()
                            wi = dd * 2 + hf
                            if tg == 0:
                                dma("pool", wt[:], w2v[:, hf * 32:(hf + 1) * 32, dd * 128:(dd + 1) * 128], writes=[wtb])
                                dma("sp", WB2[wi], wt[:].rearrange("p a b -> p (a b)"), reads=[wtb], writes=[wb2b[wi]])
                            else:
                                dma("pool", wt[:].rearrange("p a b -> p (a b)"), WB2[wi], reads=[wb2b[wi]], writes=[wtb])
                            for f in range(32):
                                fc = hf * 32 + f
                                mm(pb[:, :], wt[:, f, :], aT[:, fc, :], fc == 0, fc == 63, [wtb, aTb[fc]], [pbb])
                        xo, xob = xo_.next()
                        op("dve", lambda e, xo=xo, pb=pb, dd=dd: e.tensor_tensor(xo[:], pb[:, :], xt[:, dd, :], ALU.add),
                           reads=[pbb, xtb[dd]], writes=[xob])
                        dma("sp", xdv[:, dd, cols], xo[:], reads=[xob])
                sch.barrier()

        MIXERS = {}
        ctx = dict(nc=nc, sch=sch, op=op, dma=dma, sb=sb, rot=rot, PSB=PSB, bf=bf, cst=cst, cstb=cstb, cstbf=cstbf,
                   cstbfb=cstbfb, biasT=biasT, biasTb=biasTb, mm=mm, tr=tr, evac_copy=evac_copy, PT=PT, PFX=PFX, PFI=PFI,
                   OT=OT, rot_d=rot_d, w_uq=w_uq, w_qidx=w_qidx, dbg=dbg)

        xcur = xT_in
        for l in range(nl):
            with contextlib.ExitStack() as les:
                cv, cvb, rv, rvb = load_params(les, l)
                if "skip1" not in dbg:
                    phase1(l, xcur, cv, cvb)
                if "ext_ot" not in dbg:
                    mixers(ctx, l, cv, cvb, rv, rvb)
                if "only_mix" in dbg:
                    continue
                if "no_p3" not in dbg:
                    phase3(l, xcur, XA)
                xdst = yT if l == nl - 1 else XB
                if "no_p4" not in dbg:
                    phase4(l, XA, xdst, cv, cvb)
                xcur = XB
                sch.barrier()
        sch.final_wait("sp")
        print("built: ops", sch.n_ops, "waits", sch.n_waits, "sems", sch.nsem)
    return nc


def mixers(ctx, l, cv, cvb, rv, rvb):
    dbg = ctx["dbg"]
    if "no_ret" not in dbg:
        mix_ret(ctx, l, rv, rvb)
    if "no_fox" not in dbg:
        mix_fox(ctx, l, rv, rvb)
    if "no_dsa" not in dbg:
        mix_dsa(ctx, l, rv, rvb)
    if "no_ssd" not in dbg:
        mix_ssd(ctx, l, cv, cvb, rv, rvb)


def _store_oT(ctx, o_ap, obuf, nch, row0, tt, bank, stgr):
    op, dma, tr, bf, evac_copy = ctx["op"], ctx["dma"], ctx["tr"], ctx["bf"], ctx["evac_copy"]
    ident_b = ctx["cstbf"][:, K_ID:K_ID + 128]
    pb, pbb = bank
    pbv = bf(pb)
    for c in range(nch):
        tr(pbv[:, c * 128:(c + 1) * 128], o_ap[:, c * 128:(c + 1) * 128], ident_b, [obuf, ctx["cstbfb"]], [pbb])
    st, stb = stgr.next()
    evac_copy(st[:, 0:nch * 128], pbv[:, 0:nch * 128], [pbb], [stb])
    dma("sp", ctx["OT"][row0:row0 + nch * 128, tt * 128:(tt + 1) * 128].rearrange("(c p) t -> p c t", p=128),
        st[:, 0:nch * 128].rearrange("p (c t) -> p c t", c=nch), reads=[stb])


def _interleave(gens):
    gens = list(gens)
    while gens:
        for g in list(gens):
            try:
                next(g)
            except StopIteration:
                gens.remove(g)


def _bc_last(ap2, n):
    return ap2.unsqueeze(2).to_broadcast([ap2.shape[0], ap2.shape[1], n])


def _bc_mid(ap2, n):
    return ap2.unsqueeze(1).to_broadcast([ap2.shape[0], n, ap2.shape[1]])


def _v3(ap2, a):
    return ap2.rearrange("p (a b) -> p a b", a=a)


def mix_ret(ctx, l, rv, rvb):
    nc, sch, op, dma, sb, rot, PSB, bf = (ctx[k] for k in ("nc", "sch", "op", "dma", "sb", "rot", "PSB", "bf"))
    mm, tr, evac_copy = ctx["mm"], ctx["tr"], ctx["evac_copy"]
    cst, cstb, cstbf, cstbfb = ctx["cst"], ctx["cstb"], ctx["cstbf"], ctx["cstbfb"]
    ident_b = cstbf[:, K_ID:K_ID + 128]
    PT = ctx["PT"]
    with contextlib.ExitStack() as es:
        rt, rtb = sb(es, "rt", [128, NT, 256], F32)
        dma("sp", rt[:], ctx["rot_d"].rearrange("(t p) c -> p t c", p=128), writes=[rtb])
        Sf, Sfb = sb(es, "Sf", [128, 4, 128], F32)
        Sb, Sbb = sb(es, "Sb", [128, 4, 128], BF16)
        op("dve", lambda e: e.memset(Sf[:], 0.0), writes=[Sfb])
        op("dve", lambda e: e.memset(Sb[:], 0.0), writes=[Sbb])
        ptr = rot(es, "rpt", [128, 2048], BF16, 2)
        tmp = [sb(es, f"rtmp{i}", [128, 4, 64], F32) for i in range(4)]
        qr, qrb = sb(es, "qr", [128, 4, 128], BF16)
        kr, krb = sb(es, "kr", [128, 4, 128], BF16)
        qd, qdb = sb(es, "qd", [128, 4, 128], BF16)
        vd, vdb = sb(es, "vd", [128, 4, 128], BF16)
        qkT, qkTb = sb(es, "qkT", [128, 8, 128], BF16)
        qdT, qdTb = sb(es, "qdT", [128, 4, 128], BF16)
        sm, smb = sb(es, "sm", [128, 512], BF16)
        ysb, ysbb = sb(es, "ysb", [128, 4, 128], F32)
        ysq, ysqb = sb(es, "ysq", [128, 4, 128], F32)
        st4 = [sb(es, f"rst{i}", [128, 4], F32) for i in range(5)]
        sg, sgb = sb(es, "sg", [128, 512], F32)
        o, ob = sb(es, "oret", [128, 512], BF16)
        stgr = rot(es, "rstg", [128, 1024], BF16, 2)
        for c in range(NT):
            pt, ptb = ptr.next()
            dma("sp", pt[:], PT[c * 128:(c + 1) * 128, 0:2048], writes=[ptb])
            for (c0, ct, dst, dstb) in ((0, 0, qr, qrb), (512, 128, kr, krb)):
                x = _v3(pt[:, c0:c0 + 512], 4)
                x1, x2 = x[:, :, 0:64], x[:, :, 64:128]
                cosb = _bc_mid(rt[:, c, ct:ct + 64], 4)
                sinb = _bc_mid(rt[:, c, ct + 64:ct + 128], 4)
                (t0, t0b), (t1, t1b), (t2, t2b), (t3, t3b) = tmp
                op("dve", lambda e, t0=t0, x1=x1, cosb=cosb: e.tensor_tensor(t0[:], x1, cosb, ALU.mult), reads=[ptb, rtb], writes=[t0b])
                op("dve", lambda e, t1=t1, x2=x2, sinb=sinb: e.tensor_tensor(t1[:], x2, sinb, ALU.mult), reads=[ptb, rtb], writes=[t1b])
                op("dve", lambda e, t2=t2, x1=x1, sinb=sinb: e.tensor_tensor(t2[:], x1, sinb, ALU.mult), reads=[ptb, rtb], writes=[t2b])
                op("dve", lambda e, t3=t3, x2=x2, cosb=cosb: e.tensor_tensor(t3[:], x2, cosb, ALU.mult), reads=[ptb, rtb], writes=[t3b])
                op("dve", lambda e, dst=dst, t0=t0, t1=t1: e.tensor_tensor(dst[:, :, 0:64], t0[:], t1[:], ALU.subtract), reads=[t0b, t1b], writes=[dstb])
                op("dve", lambda e, dst=dst, t2=t2, t3=t3: e.tensor_tensor(dst[:, :, 64:128], t2[:], t3[:], ALU.add), reads=[t2b, t3b], writes=[dstb])
            op("dve", lambda e: e.tensor_tensor(qd[:], qr[:], _bc_last(cst[:, K_QDEC:K_QDEC + 4], 128), ALU.mult),
               reads=[qrb, cstb], writes=[qdb])
            op("dve", lambda e, pt=pt: e.tensor_tensor(vd[:], _v3(pt[:, 1024:1536], 4), _bc_last(cst[:, K_KDEC:K_KDEC + 4], 128), ALU.mult),
               reads=[ptb, cstb], writes=[vdb])
            pA, pAb = PSB[0]
            pB, pBb = PSB[1]
            pAv, pBv = bf(pA), bf(pB)
            for h in range(4):
                tr(pAv[:, h * 128:(h + 1) * 128], qr[:, h, :], ident_b, [qrb, cstbfb], [pAb])
                tr(pAv[:, (4 + h) * 128:(5 + h) * 128], kr[:, h, :], ident_b, [krb, cstbfb], [pAb])
                tr(pBv[:, h * 128:(h + 1) * 128], qd[:, h, :], ident_b, [qdb, cstbfb], [pBb])
            op("act", lambda e: e.activation(out=qkT[:].rearrange("p a b -> p (a b)"), in_=pAv[:, :], func=AF.Copy),
               reads=[pAb], writes=[qkTb])
            op("dve", lambda e: e.tensor_copy(qdT[:].rearrange("p a b -> p (a b)"), pBv[:, 0:512]), reads=[pBb], writes=[qdTb])
            pC, pCb = PSB[2]
            for h in range(4):
                mm(pC[:, h * 128:(h + 1) * 128], qkT[:, 4 + h, :], qkT[:, h, :], True, True, [qkTb], [pCb])
            op("dve", lambda e: e.tensor_tensor(sm[:], pC[:, :], cst[:, K_DEC:K_DEC + 512], ALU.mult), reads=[pCb, cstb], writes=[smb])
            pD, pDb = PSB[3]
            for h in range(4):
                hs = slice(h * 128, (h + 1) * 128)
                mm(pD[:, hs], sm[:, hs], pt[:, 1024 + h * 128:1024 + (h + 1) * 128], True, False, [smb, ptb], [pDb])
                mm(pD[:, hs], qdT[:, h, :], Sb[:, h, :], False, True, [qdTb, Sbb], [pDb])
            pE, pEb = PSB[4]
            for h in range(4):
                mm(pE[:, h * 128:(h + 1) * 128], kr[:, h, :], vd[:, h, :], True, True, [krb, vdb], [pEb])
            op("dve", lambda e: e.tensor_tensor(Sf[:], Sf[:], _bc_last(cst[:, K_CDEC:K_CDEC + 4], 128), ALU.mult), reads=[Sfb, cstb], writes=[Sfb])
            op("dve", lambda e: e.tensor_tensor(Sf[:], Sf[:], _v3(pE[:, :], 4), ALU.add), reads=[Sfb, pEb], writes=[Sfb])
            op("act", lambda e: e.activation(out=Sb[:], in_=Sf[:], func=AF.Copy), reads=[Sfb], writes=[Sbb])
            (s1, s1b), (s2, s2b), (mean, meanb), (msq, msqb), (rstd, rstdb) = st4
            op("act", lambda e: e.activation(out=ysb[:], in_=_v3(pD[:, :], 4), func=AF.Copy), reads=[pDb], writes=[ysbb])
            op("dve", lambda e: e.tensor_reduce(out=s1[:], in_=ysb[:], axis=AX.X, op=ALU.add), reads=[ysbb], writes=[s1b])
            op("dve", lambda e: e.tensor_tensor(ysq[:], ysb[:], ysb[:], ALU.mult), reads=[ysbb], writes=[ysqb])
            op("dve", lambda e: e.tensor_reduce(out=s2[:], in_=ysq[:], axis=AX.X, op=ALU.add), reads=[ysqb], writes=[s2b])
            op("dve", lambda e: e.tensor_scalar(mean[:], s1[:], 1.0 / 128, None, ALU.mult), reads=[s1b], writes=[meanb])
            op("dve", lambda e: e.tensor_tensor(msq[:], mean[:], mean[:], ALU.mult), reads=[meanb], writes=[msqb])
            op("dve", lambda e: e.scalar_tensor_tensor(out=rstd[:], in0=s2[:], scalar=1.0 / 128, in1=msq[:], op0=ALU.mult, op1=ALU.subtract),
               reads=[s2b, msqb], writes=[rstdb])
            op("act", lambda e: e.activation(out=rstd[:], in_=rstd[:], func=AF.Sqrt, bias=EPS, scale=1.0), reads=[rstdb], writes=[rstdb])
            op("dve", lambda e: e.reciprocal(rstd[:], rstd[:]), reads=[rstdb], writes=[rstdb])
            op("dve", lambda e: e.tensor_tensor(ysb[:], ysb[:], _bc_last(mean[:], 128), ALU.subtract), reads=[ysbb, meanb], writes=[ysbb])
            op("dve", lambda e: e.tensor_tensor(ysb[:], ysb[:], _bc_last(rstd[:], 128), ALU.mult), reads=[ysbb, rstdb], writes=[ysbb])
            op("act", lambda e, pt=pt: e.activation(out=sg[:], in_=pt[:, 1536:2048], func=AF.Silu), reads=[ptb], writes=[sgb])
            op("dve", lambda e: e.tensor_tensor(o[:], ysb[:].rearrange("p a b -> p (a b)"), sg[:], ALU.mult), reads=[ysbb, sgb], writes=[ob])
            _store_oT(ctx, o[:], ob, 4, 0, c, PSB[5], stgr)
        sch.barrier()


def mix_fox(ctx, l, rv, rvb):
    nc, sch, op, dma, sb, rot, PSB, bf = (ctx[k] for k in ("nc", "sch", "op", "dma", "sb", "rot", "PSB", "bf"))
    mm, tr, evac_copy = ctx["mm"], ctx["tr"], ctx["evac_copy"]
    cst, cstb, cstbf, cstbfb = ctx["cst"], ctx["cstb"], ctx["cstbf"], ctx["cstbfb"]
    ident_b = cstbf[:, K_ID:K_ID + 128]
    ident_f = cst[:, K_ID:K_ID + 128]
    mneg_b = cstbf[:, K_MNEG:K_MNEG + 128]
    tri_f = cst[:, K_TRI:K_TRI + 128]
    ones_f = cst[:, K_ONES:K_ONES + 128]
    PT = ctx["PT"]
    with contextlib.ExitStack() as es:
        qT, qTb = sb(es, "fqT", [128, 4, S], BF16)
        kT, kTb = sb(es, "fkT", [128, 4, S], BF16)
        Vp, Vpb = sb(es, "fVp", [128, NT, 4, 129], BF16)
        nlf, nlfb = sb(es, "nlf", [128, NT, 4], F32)
        G, Gb = sb(es, "fG", [128, NT, 4], F32)
        nGT, nGTb = sb(es, "nGT", [128, S], F32)
        of, ofb = sb(es, "ofox", [128, NT, 512], BF16)
        gq, gqb = sb(es, "fgq", [128, 128], F32)
        carry, carryb = sb(es, "fcarry", [128, 4], F32)
        op("dve", lambda e: e.tensor_scalar(gq[:], rv[:, R_FQG:R_FQG + 128], 128 ** -0.5, None, ALU.mult), reads=[rvb], writes=[gqb])
        op("dve", lambda e: e.memset(Vp[:, :, :, 128:129], 1.0), writes=[Vpb])
        op("dve", lambda e: e.memset(carry[:], 0.0), writes=[carryb])
        W = 4
        lanes = [dict(pt=sb(es, f"fpt{k}", [128, 1540], BF16), sq=sb(es, f"fsq{k}", [128, 512], F32),
                      xn=sb(es, f"fxn{k}", [128, 512], BF16), ss=sb(es, f"fss{k}", [128, 4], F32),
                      z=sb(es, f"fz{k}", [128, 4], F32), pb=PSB[k]) for k in range(W)]
        pbanks = Rot(PSB[0:2])

        def stepA(tt, ln):
            (pt, ptb), (sq, sqb), (xn, xnb), (ss, ssb), (z, zb), (pb, pbb) = ln["pt"], ln["sq"], ln["xn"], ln["ss"], ln["z"], ln["pb"]
            dma("sp", pt[:], PT[tt * 128:(tt + 1) * 128, C_FQ:C_FQ + 1540], writes=[ptb])
            yield
            for (c0, gain, gainb, dstT, dstTb) in ((0, gq[:], gqb, qT, qTb), (512, rv[:, R_FKG:R_FKG + 128], rvb, kT, kTb)):
                x = pt[:, c0:c0 + 512]
                op("dve", lambda e: e.tensor_tensor(sq[:], x, x, ALU.mult), reads=[ptb], writes=[sqb])
                yield
                op("dve", lambda e: e.tensor_reduce(out=ss[:], in_=_v3(sq[:], 4), axis=AX.X, op=ALU.add), reads=[sqb], writes=[ssb])
                yield
                op("act", lambda e: e.activation(out=ss[:], in_=ss[:], func=AF.Sqrt, bias=EPS, scale=1.0 / 128), reads=[ssb], writes=[ssb])
                yield
                op("dve", lambda e: e.reciprocal(ss[:], ss[:]), reads=[ssb], writes=[ssb])
                yield
                op("dve", lambda e: e.tensor_tensor(_v3(sq[:], 4), _v3(x, 4), _bc_last(ss[:], 128), ALU.mult), reads=[ptb, ssb], writes=[sqb])
                yield
                op("dve", lambda e: e.tensor_tensor(_v3(xn[:], 4), _v3(sq[:], 4), _bc_mid(gain, 4), ALU.mult),
                   reads=[sqb, gainb], writes=[xnb])
                yield
                pbv = bf(pb)
                for h in range(4):
                    tr(pbv[:, h * 128:(h + 1) * 128], xn[:, h * 128:(h + 1) * 128], ident_b, [xnb, cstbfb], [pbb])
                yield
                evac_copy(dstT[:, :, tt * 128:(tt + 1) * 128], _v3(pbv[:, 0:512], 4), [pbb], [dstTb])
                yield
            op("act", lambda e: e.activation(out=Vp[:, tt, :, 0:128], in_=_v3(pt[:, 1024:1536], 4), func=AF.Copy),
               reads=[ptb], writes=[Vpb])
            yield
            op("dve", lambda e: e.tensor_tensor(z[:], pt[:, 1536:1540], rv[:, R_FB:R_FB + 4], ALU.add), reads=[ptb, rvb], writes=[zb])
            yield
            op("act", lambda e: e.activation(out=z[:], in_=z[:], func=AF.Exp, scale=-1.0), reads=[zb], writes=[zb])
            yield
            op("act", lambda e: e.activation(out=nlf[:, tt, :], in_=z[:], func=AF.Ln, bias=1.0, scale=1.0), reads=[zb], writes=[nlfb])
            yield
        for t0 in range(0, NT, W):
            _interleave([stepA(t0 + k, lanes[k]) for k in range(W)])
        for tt in range(NT):
            pb, pbb = pbanks.next()
            mm(pb[:, 0:4], tri_f, nlf[:, tt, :], True, True, [cstb, nlfb], [pbb])
            mm(pb[:, 4:8], ones_f, nlf[:, tt, :], True, True, [cstb, nlfb], [pbb])
            op("dve", lambda e, pb=pb, tt=tt: e.tensor_tensor(G[:, tt, :], pb[:, 0:4], carry[:], ALU.add), reads=[pbb, carryb], writes=[Gb])
            op("dve", lambda e, pb=pb: e.tensor_tensor(carry[:], pb[:, 4:8], carry[:], ALU.add), reads=[pbb, carryb], writes=[carryb])
            pb2, pb2b = pbanks.next()
            tr(pb2[0:4, 0:128], G[:, tt, :], ident_f, [Gb, cstb], [pb2b])
            op("dve", lambda e, pb2=pb2, tt=tt: e.tensor_scalar(nGT[0:4, tt * 128:(tt + 1) * 128], pb2[0:4, 0:128], -1.0, None, ALU.mult),
               reads=[pb2b], writes=[nGTb])
        sbanks = Rot(PSB[0:3])
        abanks = Rot([(PSB[3], PSB[4]), (PSB[5], PSB[6])])
        pTr = rot(es, "fpT", [128, 512], BF16, 3)
        rc_ = rot(es, "frc", [128, 1], F32, 2)
        for h in range(4):
            selh = cst[0:4, K_SEL + h * 128:K_SEL + (h + 1) * 128]
            for qg in range(4):
                ab = abanks.next()
                nj = 4 * qg + 4

                def acc(r):
                    t, b = ab[r // 2]
                    return t[:, (r % 2) * 256:(r % 2) * 256 + 129], b
                for j in range(nj):
                    r0 = max(0, j - 4 * qg)
                    ncol = (4 - r0) * 128
                    qc0 = qg * 512 + r0 * 128
                    sk, skb = sbanks.next()
                    mm(sk[:, 0:ncol], kT[:, h, j * 128:(j + 1) * 128], qT[:, h, qc0:qc0 + ncol], True, False, [kTb, qTb], [skb])
                    if j >= 4 * qg:
                        mm(sk[:, 0:128], ident_b, mneg_b, False, False, [cstbfb], [skb])
                    mm(sk[:, 0:ncol], selh, nGT[0:4, qc0:qc0 + ncol], False, True, [cstb, nGTb], [skb])
                    pT, pTb = pTr.next()
                    op("act", lambda e, pT=pT, sk=sk, ncol=ncol, j=j, h=h: e.activation(
                        out=pT[:, 0:ncol], in_=sk[:, 0:ncol], func=AF.Exp, bias=G[:, j, h:h + 1], scale=1.0),
                       reads=[skb, Gb], writes=[pTb])
                    for r in range(r0, 4):
                        i = 4 * qg + r
                        a_ap, a_b = acc(r)
                        mm(a_ap, pT[:, (r - r0) * 128:(r - r0 + 1) * 128], Vp[:, j, h, :], j == 0 and r % 2 == 0, j == i, [pTb, Vpb], [a_b])
                for r in range(4):
                    i = 4 * qg + r
                    a_ap, a_b = acc(r)
                    rc, rcb = rc_.next()
                    op("dve", lambda e, rc=rc, a_ap=a_ap: e.reciprocal(rc[:], a_ap[:, 128:129]), reads=[a_b], writes=[rcb])
                    op("dve", lambda e, rc=rc, a_ap=a_ap, i=i, h=h: e.tensor_scalar(
                        of[:, i, h * 128:(h + 1) * 128], a_ap[:, 0:128], rc[:, 0:1], None, ALU.mult), reads=[a_b, rcb], writes=[ofb])
        stgr = rot(es, "fstg", [128, 1024], BF16, 2)
        for tt in range(NT):
            _store_oT(ctx, of[:, tt, :], ofb, 4, 512, tt, PSB[7], stgr)
        sch.barrier()


def mix_dsa(ctx, l, rv, rvb):
    nc, sch, op, dma, sb, rot, PSB, bf = (ctx[k] for k in ("nc", "sch", "op", "dma", "sb", "rot", "PSB", "bf"))
    mm, tr, evac_copy = ctx["mm"], ctx["tr"], ctx["evac_copy"]
    cst, cstb, cstbf, cstbfb = ctx["cst"], ctx["cstb"], ctx["cstbf"], ctx["cstbfb"]
    biasT, biasTb = ctx["biasT"], ctx["biasTb"]
    ident_b = cstbf[:, K_ID:K_ID + 128]
    PT = ctx["PT"]
    TOPK = 256
    with contextlib.ExitStack() as es:
        kT, kTb = sb(es, "dkT", [128, S], BF16)
        Vp, Vpb = sb(es, "dVp", [128, NT, 129], BF16)
        wh, whb = sb(es, "dwh", [128, NT, 16], F32)
        qT, qTb = sb(es, "dqT", [128, NT, 4, 128], BF16)
        qiT, qiTb = sb(es, "qiT", [128, 8, S], BF16)
        kiT, kiTb = sb(es, "kiT", [128, S], BF16)
        od, odb = sb(es, "odsa", [128, NT, 512], BF16)
        gqd, gqdb = sb(es, "dgq", [128, 128], F32)
        thr0, thr0b = sb(es, "thr0", [128, 1], F32)
        id30k, id30kb = sb(es, "id30k", [128, 128], BF16)
        op("dve", lambda e: e.tensor_scalar(id30k[:], cst[:, K_ID:K_ID + 128], 30000.0, None, ALU.mult), reads=[cstb], writes=[id30kb])
        esA = contextlib.ExitStack()
        cqT, cqTb = sb(esA, "cqT", [128, 4, S], BF16)
        wuq, wuqb = sb(esA, "wuq", [128, 4, 512], BF16)
        wqi, wqib = sb(esA, "wqi", [128, 4, 1024], BF16)
        dma("pool", wuq[:], ctx["w_uq"][l].rearrange("(rc p) c -> p rc c", p=128), writes=[wuqb])
        dma("pool", wqi[:], ctx["w_qidx"][l].rearrange("(rc p) c -> p rc c", p=128), writes=[wqib])
        dma("sp", kiT[:], ctx["PFI"], writes=[kiTb])
        op("dve", lambda e: e.tensor_scalar(gqd[:], rv[:, R_DQG:R_DQG + 128], 128 ** -0.5, None, ALU.mult), reads=[rvb], writes=[gqdb])
        op("dve", lambda e: e.memset(Vp[:, :, 128:129], 1.0), writes=[Vpb])
        op("dve", lambda e: e.memset(thr0[:], -1e29), writes=[thr0b])
        ptr = rot(esA, "dpt", [128, 848], BF16, 2)
        sq, sqb = sb(esA, "dsq", [128, 512], F32)
        xn, xnb = sb(esA, "dxn", [128, 512], BF16)
        ss, ssb = sb(esA, "dss", [128, 4], F32)
        pbanks = Rot(PSB[0:4])
        for tt in range(NT):
            pt, ptb = ptr.next()
            dma("sp", pt[:], PT[tt * 128:(tt + 1) * 128, C_DCQ:C_DCQ + 848], writes=[ptb])
            for (c0, w, gain, dstf) in ((0, 512, rv[:, R_CQG:R_CQG + 512], "cq"), (512, 128, rv[:, R_DKG:R_DKG + 128], "k")):
                x = pt[:, c0:c0 + w]
                op("dve", lambda e, x=x, w=w: e.tensor_tensor(sq[:, 0:w], x, x, ALU.mult), reads=[ptb], writes=[sqb])
                op("dve", lambda e, w=w: e.tensor_reduce(out=ss[:, 0:1], in_=sq[:, 0:w], axis=AX.X, op=ALU.add), reads=[sqb], writes=[ssb])
                op("act", lambda e, w=w: e.activation(out=ss[:, 0:1], in_=ss[:, 0:1], func=AF.Sqrt, bias=EPS, scale=1.0 / w), reads=[ssb], writes=[ssb])
                op("dve", lambda e: e.reciprocal(ss[:, 0:1], ss[:, 0:1]), reads=[ssb], writes=[ssb])
                op("dve", lambda e, x=x, w=w: e.tensor_scalar(sq[:, 0:w], x, ss[:, 0:1], None, ALU.mult), reads=[ptb, ssb], writes=[sqb])
                op("dve", lambda e, w=w, gain=gain: e.tensor_tensor(xn[:, 0:w], sq[:, 0:w], gain, ALU.mult), reads=[sqb, rvb], writes=[xnb])
                pb, pbb = pbanks.next()
                pbv = bf(pb)
                nch = w // 128
                for c in range(nch):
                    tr(pbv[:, c * 128:(c + 1) * 128], xn[:, c * 128:(c + 1) * 128], ident_b, [xnb, cstbfb], [pbb])
                if dstf == "cq":
                    evac_copy(cqT[:, :, tt * 128:(tt + 1) * 128], _v3(pbv[:, 0:512], 4), [pbb], [cqTb])
                else:
                    evac_copy(kT[:, tt * 128:(tt + 1) * 128], pbv[:, 0:128], [pbb], [kTb])
            op("act", lambda e, pt=pt, tt=tt: e.activation(out=Vp[:, tt, 0:128], in_=pt[:, 640:768], func=AF.Copy), reads=[ptb], writes=[Vpb])
            op("dve", lambda e, pt=pt, tt=tt: e.tensor_scalar(wh[:, tt, :], pt[:, 832:848], 0.25 * 0.125, None, ALU.mult), reads=[ptb], writes=[whb])
        qs, qsb = sb(esA, "dqs", [128, 512], F32)
        for tt in range(NT):
            pb, pbb = pbanks.next()
            for rc in range(4):
                mm(pb[:, :], cqT[:, rc, tt * 128:(tt + 1) * 128], wuq[:, rc, :], rc == 0, rc == 3, [cqTb, wuqb], [pbb])
            op("act", lambda e, pb=pb: e.activation(out=qs[:], in_=pb[:, :], func=AF.Copy), reads=[pbb], writes=[qsb])
            op("dve", lambda e: e.tensor_tensor(sq[:], qs[:], qs[:], ALU.mult), reads=[qsb], writes=[sqb])
            op("dve", lambda e: e.tensor_reduce(out=ss[:], in_=_v3(sq[:], 4), axis=AX.X, op=ALU.add), reads=[sqb], writes=[ssb])
            op("act", lambda e: e.activation(out=ss[:], in_=ss[:], func=AF.Sqrt, bias=EPS, scale=1.0 / 128), reads=[ssb], writes=[ssb])
            op("dve", lambda e: e.reciprocal(ss[:], ss[:]), reads=[ssb], writes=[ssb])
            op("dve", lambda e: e.tensor_tensor(_v3(sq[:], 4), _v3(qs[:], 4), _bc_last(ss[:], 128), ALU.mult), reads=[qsb, ssb], writes=[sqb])
            op("dve", lambda e: e.tensor_tensor(_v3(xn[:], 4), _v3(sq[:], 4), _bc_mid(gqd[:], 4), ALU.mult), reads=[sqb, gqdb], writes=[xnb])
            pb2, pb2b = pbanks.next()
            pbv = bf(pb2)
            for h in range(4):
                tr(pbv[:, h * 128:(h + 1) * 128], xn[:, h * 128:(h + 1) * 128], ident_b, [xnb, cstbfb], [pb2b])
            evac_copy(qT[:, tt, :, :].rearrange("p a b -> p (a b)"), pbv[:, 0:512], [pb2b], [qTb])
        for ch in range(8):
            for tg in range(4):
                pb, pbb = pbanks.next()
                for rc in range(4):
                    mm(pb[:, :], wqi[:, rc, ch * 128:(ch + 1) * 128], cqT[:, rc, tg * 512:(tg + 1) * 512], rc == 0, rc == 3, [wqib, cqTb], [pbb])
                evac_copy(qiT[:, ch, tg * 512:(tg + 1) * 512], pb[:, :], [pbb], [qiTb])
        sch.barrier()
        esA.close()
        I4 = [sb(es, f"dI{k}", [128, S], F32, nb=4) for k in range(4)]
        Dg2 = [sb(es, f"dDg{k}", [128, 16, 128], BF16) for k in range(2)]
        lanes = [dict(work=sb(es, f"dwork{k}", [128, S], F32), m8=sb(es, f"dm8{k}", [128, 8], F32),
                      thr=sb(es, f"dthr{k}", [128, 1], F32), selm=sb(es, f"dsel{k}", [128, S], BF16),
                      mT=sb(es, f"dmT{k}", [128, NT, 128], BF16), accs=sb(es, f"daccs{k}", [128, 4, 129], F32),
                      rc=sb(es, f"drc{k}", [128, 4], F32)) for k in range(2)]
        rr = rot(es, "drr", [128, 512], BF16, 3)
        pr = rot(es, "dpr", [128, 4, 128], BF16, 3)
        ibanks = Rot(PSB[0:2])
        iacc = [PSB[2], PSB[3]]
        tbank = PSB[4]
        sbank = PSB[5]
        A0, A1 = PSB[6], PSB[7]

        def acc(h):
            t, b = (A0, A1)[h // 2]
            return t[:, (h % 2) * 256:(h % 2) * 256 + 129], b

        def idx_scores(i):
            I, Ib = I4[i % 4]
            Dg, Dgb = Dg2[i % 2]
            nk = (i + 1) * 128
            ng = (nk + 511) // 512
            op("dve", lambda e: e.tensor_tensor(Dg[:], _bc_mid(cst[:, K_ID:K_ID + 128], 16), _bc_last(wh[:, i, :], 128), ALU.mult),
               reads=[cstb, whb], writes=[Dgb])
            for gp in range(0, ng, 2):
                gs = list(range(gp, min(gp + 2, ng)))
                items = [(hh, g) for hh in range(16) for g in gs]

                def score(k):
                    hh, g = items[k]
                    pp = (hh % 2) * 64
                    ncol = min(512, nk - g * 512)
                    ib, ibb = ibanks.next()
                    mm(ib[:, 0:ncol], qiT[pp:pp + 64, hh // 2, i * 128:(i + 1) * 128], kiT[pp:pp + 64, g * 512:g * 512 + ncol], True, True, [qiTb, kiTb], [ibb])
                    return ib, ibb
                pend = [score(0)]
                for k, (hh, g) in enumerate(items):
                    ncol = min(512, nk - g * 512)
                    ib, ibb = pend.pop(0)
                    r, rb_ = rr.next()
                    op("act", lambda e, r=r, ib=ib, ncol=ncol: e.activation(out=r[:, 0:ncol], in_=ib[:, 0:ncol], func=AF.Relu), reads=[ibb], writes=[rb_])
                    if k + 1 < len(items):
                        pend.append(score(k + 1))
                    ab_, abb_ = iacc[g - gp]
                    last = hh == 15 and g != i // 4
                    mm(ab_[:, 0:ncol], Dg[:, hh, :], r[:, 0:ncol], hh == 0, last, [Dgb, rb_], [abb_])
                    if hh == 15 and g == i // 4:
                        dc0 = i * 128 - g * 512
                        mm(ab_[:, dc0:dc0 + 128], ident_b, cstbf[:, K_MBIG:K_MBIG + 128], False, True, [cstbfb], [abb_])
                for g in gs:
                    ncol = min(512, nk - g * 512)
                    cs = slice(g * 512, g * 512 + ncol)
                    ab_, abb_ = iacc[g - gp]
                    op("act", lambda e, ab_=ab_, ncol=ncol, cs=cs: e.activation(out=I[:, cs], in_=ab_[:, 0:ncol], func=AF.Copy), reads=[abb_], writes=[Ib[g]])

        def gen_select(i, ln):
            I, Ibl = I4[i % 4]
            nk = (i + 1) * 128
            Ib = Ibl[0:(nk + 511) // 512]
            (work, workb), (m8, m8b), (thr, thrb), (selm, selmb), (mT, mTb) = ln["work"], ln["m8"], ln["thr"], ln["selm"], ln["mT"]
            if i >= 2 and "dsa_notopk" not in ctx["dbg"]:
                nround = TOPK // 8
                for rd in range(nround):
                    src = I if rd == 0 else work
                    srcb = Ib if rd == 0 else [workb]
                    op("dve", lambda e, src=src: e.max(out=m8[:], in_=src[:, 0:nk]), reads=srcb, writes=[m8b])
                    yield
                    if rd < nround - 1:
                        op("dve", lambda e, src=src: e.match_replace(out=work[:, 0:nk], in_to_replace=m8[:], in_values=src[:, 0:nk], imm_value=-1e30),
                           reads=srcb + [m8b], writes=[workb])
                        yield
                op("dve", lambda e: e.tensor_reduce(out=thr[:], in_=m8[:], axis=AX.X, op=ALU.min), reads=[m8b], writes=[thrb])
                yield
                th, thb = thr, thrb
            else:
                th, thb = thr0, thr0b
            op("dve", lambda e: e.tensor_scalar(selm[:, 0:nk], I[:, 0:nk], th[:, 0:1], 1.0, ALU.is_ge, ALU.subtract),
               reads=Ib + [thb], writes=[selmb])
            yield
            tb, tbb = tbank
            tbv = bf(tb)
            for j0 in range(0, i + 1, 8):
                nb = min(8, i + 1 - j0)
                for j in range(j0, j0 + nb):
                    tr(tbv[:, (j - j0) * 128:(j - j0 + 1) * 128], selm[:, j * 128:(j + 1) * 128], ident_b, [selmb, cstbfb], [tbb])
                op("act", lambda e, j0=j0, nb=nb: e.activation(out=mT[:, j0:j0 + nb, :].rearrange("p a b -> p (a b)"), in_=tbv[:, 0:nb * 128], func=AF.Copy),
                   reads=[tbb], writes=[mTb])
                yield

        def attend(i, ln):
            (mT, mTb), (accs, accsb) = ln["mT"], ln["accs"]
            sk, skb = sbank
            for j in range(i + 1):
                ty = 0 if j == i else (1 if j == i - 1 else 2)
                mm(sk[:, :], kT[:, j * 128:(j + 1) * 128], qT[:, i, :, :].rearrange("p a b -> p (a b)"), True, False, [kTb, qTb], [skb])
                mm(sk[:, :], ident_b, biasT[:, ty, :, :].rearrange("p a b -> p (a b)"), False, False, [cstbfb, biasTb], [skb])
                for h in range(4):
                    mm(sk[:, h * 128:(h + 1) * 128], id30k[:], mT[:, j, :], False, h == 3, [id30kb, mTb], [skb])
                pT, pTb = pr.next()
                op("act", lambda e, pT=pT: e.activation(out=pT[:].rearrange("p a b -> p (a b)"), in_=sk[:, :], func=AF.Exp),
                   reads=[skb], writes=[pTb])
                for h in range(4):
                    a_ap, a_b = acc(h)
                    mm(a_ap, pT[:, h, :], Vp[:, j, :], j == 0 and h % 2 == 0, j == i, [pTb, Vpb], [a_b])
            for k, (at, ab_) in enumerate((A0, A1)):
                op("act", lambda e, k=k, at=at: e.activation(out=accs[:, 2 * k:2 * k + 2, :], in_=_v3(at[:, 0:512], 2)[:, :, 0:129], func=AF.Copy),
                   reads=[ab_], writes=[accsb])

        def finalize(i, ln):
            (accs, accsb), (rc, rcb) = ln["accs"], ln["rc"]
            op("dve", lambda e: e.reciprocal(rc[:], accs[:, :, 128:129].rearrange("p a b -> p (a b)")), reads=[accsb], writes=[rcb])
            op("dve", lambda e: e.tensor_tensor(_v3(od[:, i, :], 4), accs[:, :, 0:128], _bc_last(rc[:], 128), ALU.mult),
               reads=[accsb, rcb], writes=[odb])

        idx_scores(0)
        idx_scores(1)
        for p in range(0, NT, 2):
            if p + 2 < NT:
                idx_scores(p + 2)
                idx_scores(p + 3)
            _interleave([gen_select(p, lanes[0]), gen_select(p + 1, lanes[1])])
            if p >= 2:
                finalize(p - 2, lanes[0])
                finalize(p - 1, lanes[1])
            attend(p, lanes[0])
            attend(p + 1, lanes[1])
        finalize(NT - 2, lanes[0])
        finalize(NT - 1, lanes[1])
        stgr = rot(es, "dstg", [128, 1024], BF16, 2)
        for tt in range(NT):
            _store_oT(ctx, od[:, tt, :], odb, 4, 1024, tt, PSB[2], stgr)
        sch.barrier()


def mix_ssd(ctx, l, cv, cvb, rv, rvb):
    nc, sch, op, dma, sb, rot, PSB, bf = (ctx[k] for k in ("nc", "sch", "op", "dma", "sb", "rot", "PSB", "bf"))
    mm, tr, evac_copy = ctx["mm"], ctx["tr"], ctx["evac_copy"]
    cst, cstb, cstbf, cstbfb = ctx["cst"], ctx["cstb"], ctx["cstbf"], ctx["cstbfb"]
    ident_b = cstbf[:, K_ID:K_ID + 128]
    tri_f = cst[:, K_TRI:K_TRI + 128]
    su_f = cst[:, K_SU:K_SU + 128]
    ones_f = cst[:, K_ONES:K_ONES + 128]
    PT, PFX = ctx["PT"], ctx["PFX"]
    with contextlib.ExitStack() as es:
        xsT, xsTb = sb(es, "xsT", [128, NT, 1024], BF16)
        BT, BTb = sb(es, "sBT", [128, 2, S], BF16)
        CT, CTb = sb(es, "sCT", [128, 2, S], BF16)
        Btm, Btmb = sb(es, "sBtm", [128, NT, 2, 128], BF16)
        dt, dtb = sb(es, "sdt", [128, NT, 16], F32)
        aa, aab = sb(es, "saa", [128, NT, 16], F32)
        Abc, Abcb = sb(es, "sAbc", [128, 16], F32)
        op("act", lambda e: e.activation(out=Abc[:], in_=rv[:, R_ALOG:R_ALOG + 16], func=AF.Exp), reads=[rvb], writes=[Abcb])
        op("dve", lambda e: e.tensor_scalar(Abc[:], Abc[:], -1.0, None, ALU.mult), reads=[Abcb], writes=[Abcb])
        tbanks = Rot(PSB[0:4])
        with contextlib.ExitStack() as es2:
            xin_ = rot(es2, "sxin", [128, 3 + S], BF16, 2)
            for xin, xinb in xin_.items:
                op("dve", lambda e, xin=xin: e.memset(xin[:, 0:3], 0.0), writes=[xinb])
            acc_ = rot(es2, "sacc", [128, S], F32, 2)
            cvd_ = rot(es2, "scvd", [128, S], BF16, 2)
            for ch in range(12):
                xin, xinb = xin_.next()
                dma("sp", xin[:, 3:3 + S], PFX[ch * 128:(ch + 1) * 128, :], writes=[xinb])
                ac, acb = acc_.next()
                for k in range(4):
                    wcol = cv[:, V_CW + k * 12 + ch:V_CW + k * 12 + ch + 1]
                    if k == 0:
                        op("dve", lambda e, ac=ac, xin=xin, wcol=wcol: e.tensor_scalar(ac[:], xin[:, 0:S], wcol, None, ALU.mult),
                           reads=[xinb, cvb], writes=[acb])
                    else:
                        op("dve", lambda e, ac=ac, xin=xin, wcol=wcol, k=k: e.scalar_tensor_tensor(
                            out=ac[:], in0=xin[:, k:k + S], scalar=wcol, in1=ac[:], op0=ALU.mult, op1=ALU.add),
                           reads=[xinb, cvb, acb], writes=[acb])
                bcol = cv[:, V_CB + ch:V_CB + ch + 1]
                if ch < 8 or ch in (8, 9):
                    cd, cdb = cvd_.next()
                    if ch < 8:
                        op("act", lambda e, cd=cd, ac=ac, bcol=bcol: e.activation(out=cd[:], in_=ac[:], func=AF.Silu, bias=bcol, scale=1.0),
                           reads=[acb, cvb], writes=[cdb])
                        src, srcb = cd[:], cdb
                    else:
                        g = ch - 8
                        op("act", lambda e, g=g, ac=ac, bcol=bcol: e.activation(out=BT[:, g, :], in_=ac[:], func=AF.Silu, bias=bcol, scale=1.0),
                           reads=[acb, cvb], writes=[BTb])
                        src, srcb = BT[:, g, :], BTb
                    for t0 in range(0, NT, 8):
                        pb, pbb = tbanks.next()
                        pbv = bf(pb)
                        for tt in range(t0, t0 + 8):
                            tr(pbv[:, (tt - t0) * 128:(tt - t0 + 1) * 128], src[:, tt * 128:(tt + 1) * 128], ident_b, [srcb, cstbfb], [pbb])
                        if ch < 8:
                            evac_copy(xsT[:, t0:t0 + 8, ch * 128:(ch + 1) * 128], _v3(pbv[:, 0:1024], 8), [pbb], [xsTb])
                        else:
                            evac_copy(Btm[:, t0:t0 + 8, ch - 8, :], _v3(pbv[:, 0:1024], 8), [pbb], [Btmb])
                else:
                    g = ch - 10
                    op("act", lambda e, g=g, ac=ac, bcol=bcol: e.activation(out=CT[:, g, :], in_=ac[:], func=AF.Silu, bias=bcol, scale=1.0),
                       reads=[acb, cvb], writes=[CTb])
            dtT, dtTb = sb(es2, "sdtT", [128, S], BF16)
            dma("sp", dtT[0:16, :], PFX[1536:1552, :], writes=[dtTb])
            zz, zzb = sb(es2, "szz", [128, 16], F32)
            for tt in range(NT):
                pb, pbb = tbanks.next()
                pbv = bf(pb)
                tr(pbv[:, 0:16], dtT[0:16, tt * 128:(tt + 1) * 128], cstbf[0:16, K_ID:K_ID + 16], [dtTb, cstbfb], [pbb])
                op("dve", lambda e, pbv=pbv: e.tensor_tensor(zz[:], pbv[:, 0:16], rv[:, R_DTB:R_DTB + 16], ALU.add), reads=[pbb, rvb], writes=[zzb])
                op("act", lambda e: e.activation(out=zz[:], in_=zz[:], func=AF.Exp), reads=[zzb], writes=[zzb])
                op("act", lambda e, tt=tt: e.activation(out=dt[:, tt, :], in_=zz[:], func=AF.Ln, bias=1.0, scale=1.0), reads=[zzb], writes=[dtb])
            op("dve", lambda e: e.tensor_tensor(aa[:], dt[:], _bc_mid(Abc[:], NT), ALU.mult), reads=[dtb, Abcb], writes=[aab])
            sch.barrier()
        Sf, Sfb = sb(es, "sSf", [128, 2, 512], F32)
        Sb, Sbb = sb(es, "sSb", [128, 2, 512], BF16)
        op("dve", lambda e: e.memset(Sf[:], 0.0), writes=[Sfb])
        op("dve", lambda e: e.memset(Sb[:], 0.0), writes=[Sbb])
        ex3, ex3b = sb(es, "sex3", [128, 48], F32)
        R, Rb = sb(es, "sR", [128, 16, 128], F32)
        Lx, Lxb = sb(es, "sLx", [128, 16, 128], BF16)
        CBm, CBmb = sb(es, "sCBm", [128, 2, 128], BF16)
        MT, MTb = sb(es, "sMT", [128, 16, 128], BF16)
        xdt, xdtb = sb(es, "sxdt", [128, 16, 64], BF16)
        xdd, xddb = sb(es, "sxdd", [128, 16, 64], BF16)
        ysc, yscb = sb(es, "sysc", [128, 16, 64], F32)
        yy, yyb = sb(es, "syy", [128, 16, 64], F32)
        tmp, tmpb = sb(es, "stmp", [128, 16, 64], F32)
        ptz_ = rot(es, "sptz", [128, 1024], BF16, 2)
        sz, szb = sb(es, "ssz", [128, 1024], F32)
        ss, ssb = sb(es, "sss", [128, 2], F32)
        oo, oob = sb(es, "sso", [128, 1024], BF16)
        stgr = rot(es, "sstg", [128, 1024], BF16, 2)
        P = PSB
        for c in range(NT):
            cs = slice(c * 128, (c + 1) * 128)
            ptz, ptzb = ptz_.next()
            dma("sp", ptz[:], PT[cs, C_SZ:C_SZ + 1024], writes=[ptzb])
            p0, p0b = P[0]
            mm(p0[:, 0:16], tri_f, aa[:, c, :], True, True, [cstb, aab], [p0b])
            mm(p0[:, 16:32], su_f, aa[:, c, :], True, True, [cstb, aab], [p0b])
            mm(p0[:, 32:48], ones_f, aa[:, c, :], True, True, [cstb, aab], [p0b])
            op("act", lambda e: e.activation(out=ex3[:], in_=p0[:, 0:48], func=AF.Exp), reads=[p0b], writes=[ex3b])
            ea, edec, etot = ex3[:, 0:16], ex3[:, 16:32], ex3[:, 32:48]
            op("dve", lambda e, c=c: e.tensor_tensor(R[:], _bc_last(aa[:, c, :], 128), _bc_mid(tri_f, 16), ALU.mult), reads=[aab, cstb], writes=[Rb])
            for q in range(4):
                pq, pqb = P[1 + q]
                mm(pq[:, :], su_f, R[:, 4 * q:4 * q + 4, :].rearrange("p a b -> p (a b)"), True, True, [cstb, Rb], [pqb])
                op("act", lambda e, q=q, pq=pq: e.activation(out=Lx[:, 4 * q:4 * q + 4, :].rearrange("p a b -> p (a b)"), in_=pq[:, :], func=AF.Exp),
                   reads=[pqb], writes=[Lxb])
            p5, p5b = P[5]
            for g in range(2):
                mm(p5[:, g * 128:(g + 1) * 128], BT[:, g, cs], CT[:, g, cs], True, True, [BTb, CTb], [p5b])
            op("dve", lambda e: e.tensor_tensor(CBm[:], _v3(p5[:, 0:256], 2), _bc_mid(tri_f, 2), ALU.mult), reads=[p5b, cstb], writes=[CBmb])
            for g in range(2):
                op("dve", lambda e, g=g: e.tensor_tensor(MT[:, 8 * g:8 * g + 8, :], Lx[:, 8 * g:8 * g + 8, :], _bc_mid(CBm[:, g, :], 8), ALU.mult),
                   reads=[Lxb, CBmb], writes=[MTb])
            xs3 = _v3(xsT[:, c, :], 16)
            op("dve", lambda e, xs3=xs3, c=c: e.tensor_tensor(xdt[:], xs3, _bc_last(dt[:, c, :], 64), ALU.mult), reads=[xsTb, dtb], writes=[xdtb])
            op("dve", lambda e: e.tensor_tensor(xdd[:], xdt[:], _bc_last(edec, 64), ALU.mult), reads=[xdtb, ex3b], writes=[xddb])
            for h in range(16):
                py, pyb = P[6 + h // 8]
                mm(py[:, (h % 8) * 64:(h % 8 + 1) * 64], MT[:, h, :], xdt[:, h, :], True, True, [MTb, xdtb], [pyb])
            for g in range(2):
                pq, pqb = P[1 + g]
                mm(pq[:, :], CT[:, g, cs], Sb[:, g, :], True, True, [CTb, Sbb], [pqb])
                op("dve", lambda e, g=g, pq=pq: e.tensor_tensor(ysc[:, 8 * g:8 * g + 8, :], _v3(pq[:, :], 8), _bc_last(ea[:, 8 * g:8 * g + 8], 64), ALU.mult),
                   reads=[pqb, ex3b], writes=[yscb])
                py, pyb = P[6 + g]
                op("dve", lambda e, g=g, py=py: e.tensor_tensor(yy[:, 8 * g:8 * g + 8, :], ysc[:, 8 * g:8 * g + 8, :], _v3(py[:, :], 8), ALU.add),
                   reads=[yscb, pyb], writes=[yyb])
            op("dve", lambda e, xs3=xs3: e.tensor_tensor(tmp[:], xs3, _bc_last(rv[:, R_SD:R_SD + 16], 64), ALU.mult), reads=[xsTb, rvb], writes=[tmpb])
            op("dve", lambda e: e.tensor_tensor(yy[:], yy[:], tmp[:], ALU.add), reads=[yyb, tmpb], writes=[yyb])
            for g in range(2):
                pu, pub = P[3 + g]
                mm(pu[:, :], Btm[:, c, g, :], xdd[:, 8 * g:8 * g + 8, :].rearrange("p a b -> p (a b)"), True, True, [Btmb, xddb], [pub])
                op("dve", lambda e, g=g: e.tensor_tensor(_v3(Sf[:, g, :], 8), _v3(Sf[:, g, :], 8), _bc_last(etot[:, 8 * g:8 * g + 8], 64), ALU.mult),
                   reads=[Sfb, ex3b], writes=[Sfb])
                op("dve", lambda e, g=g, pu=pu: e.tensor_tensor(Sf[:, g, :], Sf[:, g, :], pu[:, :], ALU.add), reads=[Sfb, pub], writes=[Sfb])
            op("act", lambda e: e.activation(out=Sb[:], in_=Sf[:], func=AF.Copy), reads=[Sfb], writes=[Sbb])
            yf = yy[:].rearrange("p a b -> p (a b)")
            op("act", lambda e, ptz=ptz: e.activation(out=sz[:], in_=ptz[:], func=AF.Silu), reads=[ptzb], writes=[szb])
            op("dve", lambda e, yf=yf: e.tensor_tensor(yf, yf, sz[:], ALU.mult), reads=[yyb, szb], writes=[yyb])
            op("dve", lambda e, yf=yf: e.tensor_tensor(sz[:], yf, yf, ALU.mult), reads=[yyb], writes=[szb])
            op("dve", lambda e: e.tensor_reduce(out=ss[:], in_=_v3(sz[:], 2), axis=AX.X, op=ALU.add), reads=[szb], writes=[ssb])
            op("act", lambda e: e.activation(out=ss[:], in_=ss[:], func=AF.Sqrt, bias=EPS, scale=1.0 / 512), reads=[ssb], writes=[ssb])
            op("dve", lambda e: e.reciprocal(ss[:], ss[:]), reads=[ssb], writes=[ssb])
            op("dve", lambda e, yf=yf: e.tensor_tensor(_v3(sz[:], 2), _v3(yf, 2), _bc_last(ss[:], 512), ALU.mult), reads=[yyb, ssb], writes=[szb])
            op("dve", lambda e: e.tensor_tensor(oo[:], sz[:], rv[:, R_SNG:R_SNG + 1024], ALU.mult), reads=[szb, rvb], writes=[oob])
            _store_oT(ctx, oo[:], oob, 8, 1536, c, P[0], stgr)
        sch.barrier()


def _t5_bucket(d):
    d = np.maximum(d, 0)
    max_exact = 16
    lr = np.log(np.maximum(d, 1).astype(np.float32) / max_exact) / math.log(128 / max_exact)
    large = np.minimum(max_exact + (lr * 16).astype(np.int32), 31)
    return np.where(d < max_exact, d, large)


def host_consts():
    i = np.arange(128)
    cst = np.zeros((128, NCST), np.float32)
    cst[:, K_ID:K_ID + 128] = np.eye(128)
    cst[:, K_ONES:K_ONES + 128] = 1.0
    cst[:, K_TRI:K_TRI + 128] = (i[None, :] >= i[:, None])
    cst[:, K_MNEG:K_MNEG + 128] = np.where(i[None, :] >= i[:, None], 0.0, NEG)
    cst[:, K_SU:K_SU + 128] = (i[:, None] > i[None, :])
    lg = np.log1p(-np.exp2(-5.0 - np.arange(4, dtype=np.float64)))
    rel = (i[None, :] - i[:, None]).astype(np.float64)
    for h in range(4):
        cst[:, K_DEC + h * 128:K_DEC + (h + 1) * 128] = np.where(rel >= 0, np.exp(lg[h] * np.maximum(rel, 0)), 0.0)
        cst[:, K_QDEC + h] = np.exp(lg[h] * (i + 1.0))
        cst[:, K_KDEC + h] = np.exp(lg[h] * (127.0 - i))
        cst[:, K_CDEC + h] = np.exp(lg[h] * 128.0)
        cst[h, K_SEL + h * 128:K_SEL + (h + 1) * 128] = 1.0
    cst[:, K_MBIG:K_MBIG + 128] = np.where(i[None, :] <= i[:, None], 0.0, -1e30)
    oh = np.zeros((128, 2, 32, 128), np.float32)
    for ty in range(2):
        dist = i[None, :] - i[:, None] + 128 * ty
        bk = _t5_bucket(dist)
        for b in range(32):
            oh[:, ty, b, :] = ((bk == b) & (dist >= 0))
    oh = oh.reshape(128, 8192)
    pos = np.arange(S, dtype=np.float32)
    inv = (1.0 / (10000.0 ** (np.arange(64, dtype=np.float32) / 64))).astype(np.float32)
    ang = pos[:, None] * inv[None, :]
    rot = np.concatenate([np.cos(ang), np.sin(ang), np.cos(ang) * (128 ** -0.5), np.sin(ang) * (128 ** -0.5)], axis=1).astype(np.float32)
    return cst, oh, rot


def host_params(inp):
    colv = np.zeros((L, NV, 128), np.float32)
    rowv = np.zeros((L, 1, NR), np.float32)
    for l in range(L):
        colv[l, V_N1:V_N1 + 16] = inp["norm1_g"][l].reshape(16, 128)
        colv[l, V_N2:V_N2 + 16] = inp["norm2_g"][l].reshape(16, 128)
        colv[l, V_GB:V_GB + 64] = inp["gate_b"][l].reshape(64, 128)
        colv[l, V_CW:V_CW + 48] = inp["ssd_conv_w"][l].reshape(48, 128)
        colv[l, V_CB:V_CB + 12] = inp["ssd_conv_b"][l].reshape(12, 128)
        r = rowv[l, 0]
        r[R_FB:R_FB + 4] = inp["fox_f_b"][l]
        r[R_FQG:R_FQG + 128] = inp["fox_qn_g"][l]
        r[R_FKG:R_FKG + 128] = inp["fox_kn_g"][l]
        r[R_CQG:R_CQG + 512] = inp["dsa_cq_g"][l]
        r[R_DQG:R_DQG + 128] = inp["dsa_qn_g"][l]
        r[R_DKG:R_DKG + 128] = inp["dsa_kn_g"][l]
        r[R_DTB:R_DTB + 16] = inp["ssd_dt_bias"][l]
        r[R_ALOG:R_ALOG + 16] = inp["ssd_a_log"][l]
        r[R_SD:R_SD + 16] = inp["ssd_d"][l]
        r[R_SNG:R_SNG + 1024] = inp["ssd_norm_g"][l]
    relb = np.ascontiguousarray(inp["rel_bias"].reshape(1, 128)).astype(np.float32)
    return colv, rowv, relb


CORE_OF_BATCH = [0, 1, 4, 5]


def make_in_maps(inp, n_cores=8):
    cst, oh, rot = host_consts()
    colv, rowv, relb = host_params(inp)
    shared = dict(
        w_in=np.ascontiguousarray(inp["w_in"]), w_br=np.ascontiguousarray(inp["w_br"]),
        w_out=np.ascontiguousarray(inp["w_out"]), w_ff1=np.ascontiguousarray(inp["w_ff1"]),
        w_ff2=np.ascontiguousarray(inp["w_ff2"]), w_uq=np.ascontiguousarray(inp["dsa_w_uq"]),
        w_qidx=np.ascontiguousarray(inp["dsa_w_qidx"]), colv=colv, rowv=rowv, relb=relb, cst=cst, oh=oh, rot=rot)
    maps = []
    if n_cores == 8:
        zeros = {k: np.zeros_like(v) for k, v in shared.items()}
        zx = np.zeros((D, S), np.float32)
        for c in range(n_cores):
            if c in CORE_OF_BATCH:
                m = dict(shared)
                m["xT"] = np.ascontiguousarray(inp["x"][CORE_OF_BATCH.index(c)].T)
            else:
                m = dict(zeros)
                m["xT"] = zx
            maps.append(m)
        return maps
    for c in range(n_cores):
        m = dict(shared)
        m["xT"] = np.ascontiguousarray(inp["x"][c % 4].T)
        maps.append(m)
    return maps


def kernel(**inputs):
    inp = {k: np.asarray(v) for k, v in inputs.items()}
    nc = build(L)
    maps = make_in_maps(inp, 8)
    res = run_bass_kernel_spmd(nc, maps, core_ids=list(range(8)))
    out = np.stack([np.ascontiguousarray(res.results[CORE_OF_BATCH[b]]["yT"].T) for b in range(4)], axis=0)
    return out.astype(np.float32)
```

```python
import contextlib
import math
import numpy as np
import concourse.bass as bass
import concourse.mybir as mybir
from concourse.bass_utils import run_bass_kernel_spmd

F32 = mybir.dt.float32
BF16 = mybir.dt.bfloat16
AF = mybir.ActivationFunctionType
ALU = mybir.AluOpType
AX = mybir.AxisListType

D = 2048
S = 2048
L = 4
KC = 16
NT = 16
IN_TOTAL = 15204
EPS = 1e-6
NEG = -30000.0

C_RQ, C_RK, C_RV, C_RG = 0, 512, 1024, 1536
C_FQ, C_FK, C_FV, C_FF = 2048, 2560, 3072, 3584
C_DCQ, C_DK, C_DV, C_IK, C_IW = 3588, 4100, 4228, 4356, 4420
C_SZ, C_SX, C_SDT = 4436, 5460, 6996
C_G = 7012
TM_END = 5460

R_FB, R_FQG, R_FKG, R_CQG, R_DQG, R_DKG, R_DTB, R_ALOG, R_SD, R_SNG = 0, 4, 132, 260, 772, 900, 1028, 1044, 1060, 1076
NR = 2100
V_N1, V_N2, V_GB, V_CW, V_CB = 0, 16, 32, 96, 144
NV = 256

K_ID, K_ONES, K_TRI, K_MNEG, K_SU, K_DEC, K_QDEC, K_KDEC, K_CDEC, K_SEL, K_MBIG = 0, 128, 256, 384, 512, 640, 1152, 1156, 1160, 1164, 1676
NCST = 1804


class Buf:
    __slots__ = ("w", "r", "name")

    def __init__(self, name=""):
        self.w = None
        self.r = {}
        self.name = name


class Sched:
    EPOCH = 60000

    def __init__(self, nc, es, n_dma_sems=32):
        self.nc = nc
        self.es = es
        self.engs = {"pe": nc.tensor, "act": nc.scalar, "dve": nc.vector, "pool": nc.gpsimd, "sp": nc.sync}
        self.sem, self.cnt, self.semkey = {}, {}, {}
        self.nsem = 0
        for e in self.engs:
            self._new_eng_sem(e)
        self.seen = {e: {} for e in self.engs}
        self.dma_sems = []
        for i in range(n_dma_sems):
            s = es.enter_context(nc.semaphore(f"dma{i}"))
            self.dma_sems.append([f"dma{i}", s, 0])
        self.dma_next = 0
        self.n_ops = 0
        self.n_waits = 0

    def _new_eng_sem(self, e):
        self.nsem += 1
        key = f"{e}{self.nsem}"
        self.sem[e] = self.es.enter_context(self.nc.semaphore(key))
        self.semkey[e] = key
        self.cnt[e] = 0

    def _wait(self, eng, tok):
        key, sem, val, teng = tok
        if self.seen[eng].get(key, 0) >= val:
            return
        self.engs[eng].wait_ge(sem, val)
        self.seen[eng][key] = val
        self.n_waits += 1

    def _deps(self, eng, reads, writes, skip_same):
        for b in reads:
            if b.w is not None and not (skip_same and b.w[3] == eng):
                self._wait(eng, b.w)
        for b in writes:
            if b.w is not None and not (skip_same and b.w[3] == eng):
                self._wait(eng, b.w)
            for tok in b.r.values():
                if not (skip_same and tok[3] == eng):
                    self._wait(eng, tok)

    def _commit(self, tok, reads, writes):
        for b in reads:
            b.r[tok[0]] = tok
        for b in writes:
            b.w = tok
            b.r = {}

    def op(self, eng, fn, reads=(), writes=(), skip_same=False):
        self._deps(eng, reads, writes, skip_same)
        ins = fn(self.engs[eng])
        if self.cnt[eng] >= self.EPOCH:
            self._new_eng_sem(eng)
        self.cnt[eng] += 1
        ins.then_inc(self.sem[eng], 1)
        tok = (self.semkey[eng], self.sem[eng], self.cnt[eng], eng)
        self._commit(tok, reads, writes)
        self.n_ops += 1
        return tok

    def dma(self, q, out, in_, reads=(), writes=(), **kw):
        self._deps(q, reads, writes, False)
        ent = self.dma_sems[self.dma_next]
        self.dma_next = (self.dma_next + 1) % len(self.dma_sems)
        key, sem, val = ent
        if val > 0:
            self._wait(q, (key, sem, val, "dma"))
        self.engs[q].dma_start(out=out, in_=in_, **kw).then_inc(sem, 16)
        ent[2] = val + 16
        tok = (key, sem, val + 16, "dma")
        self._commit(tok, reads, writes)
        self.n_ops += 1
        return tok

    def barrier(self):
        toks = []
        for e in self.engs:
            if self.cnt[e] > 0:
                toks.append((self.semkey[e], self.sem[e], self.cnt[e], e))
        for key, sem, val in self.dma_sems:
            if val > 0:
                toks.append((key, sem, val, "dma"))
        for e in self.engs:
            for t in toks:
                self._wait(e, t)

    def final_wait(self, eng="sp"):
        for key, sem, val in self.dma_sems:
            if val > 0:
                self._wait(eng, (key, sem, val, "dma"))


class Rot:
    def __init__(self, items):
        self.items = items
        self.i = 0

    def next(self):
        it = self.items[self.i]
        self.i = (self.i + 1) % len(self.items)
        return it


def build(nl=L, dbg=(), LW=L):
    dbg = set(dbg)
    nc = bass.Bass("TRN2", target_bir_lowering=False)

    def din(name, shape, dt=F32):
        return nc.dram_tensor(name, list(shape), dt, kind="ExternalInput").ap()

    def dscr(name, shape, dt):
        if "dump" in dbg:
            return nc.dram_tensor(name, list(shape), dt, kind="ExternalOutput").ap()
        return nc.dram_tensor(name, list(shape), dt, kind="Internal").ap()

    xT_in = din("xT", [D, S])
    yT = nc.dram_tensor("yT", [D, S], F32, kind="ExternalOutput").ap()
    w_in = din("w_in", [LW, D, IN_TOTAL])
    w_br = din("w_br", [LW, 2560, D])
    w_out = din("w_out", [LW, D, D])
    w_ff1 = din("w_ff1", [LW, D, 4 * D])
    w_ff2 = din("w_ff2", [LW, 4 * D, D])
    w_uq = din("w_uq", [LW, 512, 512])
    w_qidx = din("w_qidx", [LW, 512, 1024])
    colv = din("colv", [LW, NV, 128])
    rowv = din("rowv", [LW, 1, NR])
    relb = din("relb", [1, 128])
    cst_d = din("cst", [128, NCST])
    oh_d = din("oh", [128, 8192])
    rot_d = din("rot", [S, 256])

    PT = dscr("PT", [S, TM_END], BF16)
    PFX = dscr("PFX", [1552, S], BF16)
    PFI = dscr("PFI", [128, S], BF16)
    PFG = dscr("PFG", [4 * D, S], BF16)
    if "ext_ot" in dbg:
        OT = din("OT", [2560, S], BF16)
    else:
        OT = dscr("OT", [2560, S], BF16)
    WB1 = nc.dram_tensor("WB1", [32, 128, KC * 256], BF16, kind="Internal").ap()
    WB2 = nc.dram_tensor("WB2", [32, 128, 32 * 128], BF16, kind="Internal").ap()
    XA = dscr("XA", [D, S], F32)
    XB = dscr("XB", [D, S], F32)

    ges = contextlib.ExitStack()
    with ges:
        sch = Sched(nc, ges)
        op, dma = sch.op, sch.dma

        uid = [0]

        def sb(es, name, shape, dt, nb=1):
            uid[0] += 1
            t = es.enter_context(nc.sbuf_tensor(f"s{uid[0]}_{name}", list(shape), dt))
            if nb == 1:
                return t, Buf(name)
            return t, [Buf(f"{name}{i}") for i in range(nb)]

        def rot(es, name, shape, dt, n):
            return Rot([sb(es, f"{name}{i}", shape, dt) for i in range(n)])

        PSB = []
        for i in range(8):
            t = ges.enter_context(nc.psum_tensor(f"ps{i}", [128, 512], F32))
            PSB.append((t, Buf(f"ps{i}")))

        def bf(ps_t):
            return ps_t[:, :].bitcast(BF16)

        cst, cstb = sb(ges, "cst", [128, NCST], F32)
        cstbf, cstbfb = sb(ges, "cstbf", [128, NCST], BF16)
        dma("sp", cst[:], cst_d, writes=[cstb])
        dma("pool", cstbf[:], cst_d, writes=[cstbfb])
        ident_f = cst[:, K_ID:K_ID + 128]
        ones_f = cst[:, K_ONES:K_ONES + 128]
        tri_f = cst[:, K_TRI:K_TRI + 128]
        su_f = cst[:, K_SU:K_SU + 128]
        ident_b = cstbf[:, K_ID:K_ID + 128]
        mneg_b = cstbf[:, K_MNEG:K_MNEG + 128]
        tri_b = cstbf[:, K_TRI:K_TRI + 128]
        biasT, biasTb = sb(ges, "biasT", [128, 3, 4, 128], BF16)
        with contextlib.ExitStack() as es:
            ohs, ohsb = sb(es, "ohs", [128, 64, 128], BF16)
            rb, rbb = sb(es, "rb", [128, 128], F32)
            acc, accb = sb(es, "bacc", [128, 2, 4, 128], F32)
            dma("pool", ohs[:].rearrange("p a b -> p (a b)"), oh_d, writes=[ohsb])
            dma("sp", rb[:], relb.partition_broadcast(128).rearrange("p a b -> p (a b)"), writes=[rbb])
            for ty in range(2):
                for h in range(4):
                    for b in range(32):
                        col = rb[:, b * 4 + h:b * 4 + h + 1]
                        if b == 0:
                            op("dve", lambda e, ty=ty, h=h, b=b, col=col: e.tensor_scalar(
                                acc[:, ty, h, :], ohs[:, ty * 32 + b, :], col, None, ALU.mult),
                               reads=[ohsb, rbb], writes=[accb])
                        else:
                            op("dve", lambda e, ty=ty, h=h, b=b, col=col: e.scalar_tensor_tensor(
                                out=acc[:, ty, h, :], in0=ohs[:, ty * 32 + b, :], scalar=col, in1=acc[:, ty, h, :],
                                op0=ALU.mult, op1=ALU.add), reads=[ohsb, rbb, accb], writes=[accb])
            op("dve", lambda e: e.tensor_copy(biasT[:, 0:2, :, :], acc[:]), reads=[accb], writes=[biasTb])
            for h in range(4):
                col = rb[:, 31 * 4 + h:31 * 4 + h + 1]
                op("dve", lambda e, h=h, col=col: e.tensor_scalar(
                    biasT[:, 2, h, :], cst[:, K_ONES:K_ONES + 128], col, None, ALU.mult),
                   reads=[cstb, rbb], writes=[biasTb])
            sch.barrier()

        evac_flip = [0]

        def evac_copy(out_ap, in_ap, reads, writes):
            evac_flip[0] ^= 1
            if evac_flip[0]:
                return op("act", lambda e: e.activation(out=out_ap, in_=in_ap, func=AF.Copy), reads=reads, writes=writes)
            return op("dve", lambda e: e.tensor_copy(out_ap, in_ap), reads=reads, writes=writes)

        def mm(out_ap, lhsT, rhs, start, stop, reads, writes):
            return op("pe", lambda e: e.matmul(out_ap, lhsT, rhs, start=start, stop=stop),
                      reads=reads, writes=writes, skip_same=True)

        def tr(out_ap, in_ap, ident, reads, writes):
            return op("pe", lambda e: e.transpose(out_ap, in_ap, ident), reads=reads, writes=writes, skip_same=True)

        def view_kc(ap2d):
            return ap2d.rearrange("(kc p) c -> p kc c", p=128)

        def load_params(es, l):
            cv, cvb = sb(es, "cv", [128, NV], F32)
            rv, rvb = sb(es, "rv", [128, NR], F32)
            with contextlib.ExitStack() as es2:
                raw, rawb = sb(es2, "cvraw", [128, 2, 128], F32)
                dma("sp", raw[:], colv[l].rearrange("(a p) c -> p a c", p=128), writes=[rawb])
                dma("sp", rv[:], rowv[l].partition_broadcast(128).rearrange("p a b -> p (a b)"), writes=[rvb])
                pt, ptb = PSB[0]
                for a in range(2):
                    tr(pt[:, a * 128:(a + 1) * 128], raw[:, a, :], ident_f, [rawb, cstb], [ptb])
                op("dve", lambda e: e.tensor_copy(cv[:], pt[:, 0:256]), reads=[ptb], writes=[cvb])
                op("dve", lambda e: e.tensor_scalar(cv[:, 0:32], cv[:, 0:32], math.sqrt(D), None, ALU.mult),
                   reads=[cvb], writes=[cvb])
                sch.barrier()
            return cv, cvb, rv, rvb

        def norm_group(xsrc, tcols, gcol0, cv, cvb, xt, xtb, hT, hcols, hbuf, sqr, rs, rsb):
            pA, pAb = PSB[7]
            xv = view_kc(xsrc)
            for kc in range(KC):
                dma("sp", xt[:, kc, :], xv[:, kc, tcols], writes=[xtb[kc]])
            for kc in range(KC):
                sq, sqb = sqr.next()
                op("act", lambda e, kc=kc, sq=sq: e.activation(out=sq[:], in_=xt[:, kc, :], func=AF.Square),
                   reads=[xtb[kc]], writes=[sqb])
                mm(pA[:, :], ones_f, sq[:], kc == 0, kc == KC - 1, [sqb, cstb], [pAb])
            op("act", lambda e: e.activation(out=rs[:], in_=pA[:, :], func=AF.Sqrt, bias=float(D * EPS), scale=1.0),
               reads=[pAb], writes=[rsb])
            op("dve", lambda e: e.reciprocal(rs[:], rs[:]), reads=[rsb], writes=[rsb])
            for kc in range(KC):
                op("dve", lambda e, kc=kc: e.scalar_tensor_tensor(
                    out=hT[:, kc, hcols], in0=xt[:, kc, :], scalar=cv[:, gcol0 + kc:gcol0 + kc + 1], in1=rs[:],
                    op0=ALU.mult, op1=ALU.mult), reads=[xtb[kc], cvb, rsb], writes=[hbuf])

        def phase1(l, xsrc, cv, cvb):
            with contextlib.ExitStack() as es:
                hT, hTb = sb(es, "hT", [128, KC, S], BF16, nb=4)
                with contextlib.ExitStack() as es2:
                    xt, xtb = sb(es2, "xt", [128, KC, 512], F32, nb=KC)
                    sqr = rot(es2, "sq", [128, 512], F32, 2)
                    rs, rsb = sb(es2, "rs", [128, 512], F32)
                    for tg in range(4):
                        cols = slice(tg * 512, (tg + 1) * 512)
                        norm_group(xsrc, cols, V_N1, cv, cvb, xt, xtb, hT, cols, hTb[tg], sqr, rs, rsb)
                    sch.barrier()
                wr = rot(es, "w1t", [128, KC, 528], BF16, 3)
                stg = rot(es, "stg", [128, 512], BF16, 4)
                banks = Rot(PSB[0:6])
                wv = view_kc(w_in[l])
                jobs = []
                c = 0
                while c < TM_END:
                    wd = min(512, TM_END - c)
                    jobs.append(("tm", c, wd))
                    c += wd
                jobs.append(("ik", C_IK, 64))
                jobs += [("fx", 5460, 512, 0), ("fx", 5972, 512, 512), ("fx", 6484, 528, 1024)]
                for i in range(16):
                    jobs.append(("fg", C_G + i * 512, 512, i * 512))

                def load(j):
                    wt, wtb = wr.items[j % 3]
                    job = jobs[j]
                    if job[0] == "ik":
                        dma("pool", wt[:, :, 0:64], wv[:, :, C_IK:C_IK + 64], writes=[wtb])
                        dma("pool", wt[:, :, 64:128], wv[:, :, C_IK:C_IK + 64], writes=[wtb])
                    else:
                        dma("pool", wt[:, :, 0:job[2]], wv[:, :, job[1]:job[1] + job[2]], writes=[wtb])

                load(0)
                load(1)
                for j, job in enumerate(jobs):
                    wt, wtb = wr.items[j % 3]
                    if job[0] == "tm":
                        _, c0, wd = job
                        for tt in range(NT):
                            pb, pbb = banks.next()
                            for kc in range(KC):
                                mm(pb[:, 0:wd], hT[:, kc, tt * 128:(tt + 1) * 128], wt[:, kc, 0:wd], kc == 0, kc == KC - 1,
                                   [hTb[tt // 4], wtb], [pbb])
                            st, stb = stg.next()
                            evac_copy(st[:, 0:wd], pb[:, 0:wd], [pbb], [stb])
                            dma("sp", PT[tt * 128:(tt + 1) * 128, c0:c0 + wd], st[:, 0:wd], reads=[stb])
                    else:
                        if job[0] == "ik":
                            slices = [(0, 128, PFI, 0)]
                        elif job[0] == "fx":
                            slices = []
                            s0 = 0
                            while s0 < job[2]:
                                m = min(128, job[2] - s0)
                                slices.append((s0, m, PFX, job[3] + s0))
                                s0 += m
                        else:
                            slices = [(s0, 128, PFG, job[3] + s0) for s0 in range(0, 512, 128)]
                        for (s0, m, dst, r0) in slices:
                            for tg in range(4):
                                pb, pbb = banks.next()
                                for kc in range(KC):
                                    mm(pb[0:m, :], wt[:, kc, s0:s0 + m], hT[:, kc, tg * 512:(tg + 1) * 512], kc == 0, kc == KC - 1,
                                       [hTb[tg], wtb], [pbb])
                                st, stb = stg.next()
                                if job[0] == "fg":
                                    gch = r0 // 128
                                    op("act", lambda e, st=st, pb=pb, gch=gch: e.activation(
                                        out=st[:, :], in_=pb[:, :], func=AF.Sigmoid,
                                        bias=cv[:, V_GB + gch:V_GB + gch + 1], scale=1.0), reads=[pbb, cvb], writes=[stb])
                                else:
                                    evac_copy(st[0:m, :], pb[0:m, :], [pbb], [stb])
                                dma("sp", dst[r0:r0 + m, tg * 512:(tg + 1) * 512], st[0:m, :], reads=[stb])
                    if j + 2 < len(jobs):
                        load(j + 2)
                sch.barrier()

        def phase3(l, xsrc, xdst):
            with contextlib.ExitStack() as es:
                oT, oTb = sb(es, "oT", [128, 20, 1024], BF16)
                mT, mTb = sb(es, "mT", [128, KC, 1024], BF16, nb=KC)
                wr = rot(es, "w3t", [128, 20, 128], BF16, 3)
                gr = rot(es, "g3t", [128, 4, 1024], BF16, 2)
                accl = [sb(es, f"acc3_{k}", [128, 512], F32) for k in range(2)]
                tmpl = [[sb(es, f"tmp3_{k}{m}", [128, 512], F32) for m in range(2)] for k in range(2)]
                xr_ = rot(es, "xr3", [128, 1024], F32, 2)
                xo_ = rot(es, "xo3", [128, 512], F32, 3)
                banks = Rot(PSB[0:6])
                wbv = view_kc(w_br[l])
                wov = view_kc(w_out[l])
                gv = PFG.rearrange("(i dc p) t -> p i dc t", p=128, i=4)
                otv = view_kc(OT)
                xv = view_kc(xsrc)
                xdv = view_kc(xdst)
                brk = [(0, 4), (4, 8), (8, 12), (12, 20)]
                for hf in range(2):
                    hcols = slice(hf * 1024, (hf + 1) * 1024)
                    for kc in range(20):
                        dma("sp", oT[:, kc, :], otv[:, kc, hcols], writes=[oTb])
                    for dc in range(KC):
                        wt, wtb = wr.next()
                        dma("pool", wt[:], wbv[:, :, dc * 128:(dc + 1) * 128], writes=[wtb])
                        gt, gtb = gr.next()
                        dma("sp", gt[:], gv[:, :, dc, hcols], writes=[gtb])
                        def gate_group(t2, lane, wt=wt, wtb=wtb, gt=gt, gtb=gtb, dc=dc):
                            cols = slice(t2 * 512, (t2 + 1) * 512)
                            ac, acb = accl[lane]
                            for i in range(4):
                                pb, pbb = PSB[lane * 4 + i]
                                k0, k1 = brk[i]
                                for kc in range(k0, k1):
                                    mm(pb[:, :], wt[:, kc, :], oT[:, kc, cols], kc == k0, kc == k1 - 1, [wtb, oTb], [pbb])
                                yield
                                if i == 0:
                                    op("dve", lambda e: e.tensor_tensor(ac[:], pb[:, :], gt[:, 0, cols], ALU.mult), reads=[pbb, gtb], writes=[acb])
                                    yield
                                else:
                                    tp, tpb = tmpl[lane][i % 2]
                                    op("dve", lambda e: e.tensor_tensor(tp[:], pb[:, :], gt[:, i, cols], ALU.mult), reads=[pbb, gtb], writes=[tpb])
                                    yield
                                    if i < 3:
                                        op("dve", lambda e: e.tensor_tensor(ac[:], ac[:], tp[:], ALU.add), reads=[acb, tpb], writes=[acb])
                                    else:
                                        op("dve", lambda e: e.tensor_tensor(mT[:, dc, cols], ac[:], tp[:], ALU.add), reads=[acb, tpb], writes=[mTb[dc]])
                                    yield
                        _interleave([gate_group(0, 0), gate_group(1, 1)])
                    for dd in range(KC):
                        wt, wtb = wr.next()
                        dma("pool", wt[:, 0:KC, :], wov[:, :, dd * 128:(dd + 1) * 128], writes=[wtb])
                        xr, xrb = xr_.next()
                        dma("sp", xr[:], xv[:, dd, hcols], writes=[xrb])
                        for t2 in range(2):
                            cols = slice(t2 * 512, (t2 + 1) * 512)
                            pb, pbb = banks.next()
                            for kc in range(KC):
                                mm(pb[:, :], wt[:, kc, :], mT[:, kc, cols], kc == 0, kc == KC - 1, [wtb, mTb[kc]], [pbb])
                            xo, xob = xo_.next()
                            op("dve", lambda e, xo=xo, pb=pb, xr=xr, cols=cols: e.tensor_tensor(
                                xo[:], pb[:, :], xr[:, cols], ALU.add), reads=[pbb, xrb], writes=[xob])
                            dma("sp", xdv[:, dd, hf * 1024 + t2 * 512:hf * 1024 + (t2 + 1) * 512], xo[:], reads=[xob])
                sch.barrier()

        def phase4(l, xsrc, xdst, cv, cvb):
            with contextlib.ExitStack() as es:
                xt, xtb = sb(es, "xt4", [128, KC, 512], F32, nb=KC)
                sqr = rot(es, "sq4", [128, 512], F32, 2)
                rs, rsb = sb(es, "rs4", [128, 512], F32)
                h2, h2b = sb(es, "h2T", [128, KC, 512], BF16)
                aT, aTb = sb(es, "aT", [128, 64, 512], BF16, nb=64)
                w1r = rot(es, "w41", [128, KC, 256], BF16, 3)
                w2r = rot(es, "w42", [128, 32, 128], BF16, 3)
                rl_ = rot(es, "rl4", [128, 512], BF16, 2)
                xo_ = rot(es, "xo4", [128, 512], F32, 2)
                banks = Rot(PSB[0:6])
                w1v = view_kc(w_ff1[l])
                w2v = view_kc(w_ff2[l])
                xdv = view_kc(xdst)
                wb1b = [Buf(f"wb1_{i}") for i in range(32)]
                wb2b = [Buf(f"wb2_{i}") for i in range(32)]
                for tg in range(4):
                    cols = slice(tg * 512, (tg + 1) * 512)
                    norm_group(xsrc, cols, V_N2, cv, cvb, xt, xtb, h2, slice(0, 512), h2b, sqr, rs, rsb)
                    for fg in range(32):
                        wt, wtb = w1r.next()
                        if tg == 0:
                            dma("pool", wt[:], w1v[:, :, fg * 256:(fg + 1) * 256], writes=[wtb])
                            dma("sp", WB1[fg], wt[:].rearrange("p a b -> p (a b)"), reads=[wtb], writes=[wb1b[fg]])
                        else:
                            dma("pool", wt[:].rearrange("p a b -> p (a b)"), WB1[fg], reads=[wb1b[fg]], writes=[wtb])
                        for fs in range(2):
                            fc = fg * 2 + fs
                            pb, pbb = banks.next()
                            for kc in range(KC):
                                mm(pb[:, :], wt[:, kc, fs * 128:(fs + 1) * 128], h2[:, kc, :], kc == 0, kc == KC - 1, [wtb, h2b], [pbb])
                            rl, rlb = rl_.next()
                            op("act", lambda e, rl=rl, pb=pb: e.activation(out=rl[:], in_=pb[:, :], func=AF.Relu),
                               reads=[pbb], writes=[rlb])
                            op("dve", lambda e, rl=rl, fc=fc: e.tensor_tensor(aT[:, fc, :], rl[:], rl[:], ALU.mult),
                               reads=[rlb], writes=[aTb[fc]])
                    for dd in range(KC):
                        pb, pbb = banks.next()
                        for hf in range(2):
                            wt, wtb = w2r.next()
                            wi = dd * 2 + hf
                            if tg == 0:
                                dma("pool", wt[:], w2v[:, hf * 32:(hf + 1) * 32, dd * 128:(dd + 1) * 128], writes=[wtb])
                                dma("sp", WB2[wi], wt[:].rearrange("p a b -> p (a b)"), reads=[wtb], writes=[wb2b[wi]])
                            else:
                                dma("pool", wt[:].rearrange("p a b -> p (a b)"), WB2[wi], reads=[wb2b[wi]], writes=[wtb])
                            for f in range(32):
                                fc = hf * 32 + f
                                mm(pb[:, :], wt[:, f, :], aT[:, fc, :], fc == 0, fc == 63, [wtb, aTb[fc]], [pbb])
                        xo, xob = xo_.next()
                        op("dve", lambda e, xo=xo, pb=pb, dd=dd: e.tensor_tensor(xo[:], pb[:, :], xt[:, dd, :], ALU.add),
                           reads=[pbb, xtb[dd]], writes=[xob])
                        dma("sp", xdv[:, dd, cols], xo[:], reads=[xob])
                sch.barrier()

        MIXERS = {}
        ctx = dict(nc=nc, sch=sch, op=op, dma=dma, sb=sb, rot=rot, PSB=PSB, bf=bf, cst=cst, cstb=cstb, cstbf=cstbf,
                   cstbfb=cstbfb, biasT=biasT, biasTb=biasTb, mm=mm, tr=tr, evac_copy=evac_copy, PT=PT, PFX=PFX, PFI=PFI,
                   OT=OT, rot_d=rot_d, w_uq=w_uq, w_qidx=w_qidx, dbg=dbg)

        xcur = xT_in
        for l in range(nl):
            with contextlib.ExitStack() as les:
                cv, cvb, rv, rvb = load_params(les, l)
                if "skip1" not in dbg:
                    phase1(l, xcur, cv, cvb)
                if "ext_ot" not in dbg:
                    mixers(ctx, l, cv, cvb, rv, rvb)
                if "only_mix" in dbg:
                    continue
                if "no_p3" not in dbg:
                    phase3(l, xcur, XA)
                xdst = yT if l == nl - 1 else XB
                if "no_p4" not in dbg:
                    phase4(l, XA, xdst, cv, cvb)
                xcur = XB
                sch.barrier()
        sch.final_wait("sp")
        print("built: ops", sch.n_ops, "waits", sch.n_waits, "sems", sch.nsem)
    return nc


def mixers(ctx, l, cv, cvb, rv, rvb):
    dbg = ctx["dbg"]
    if "no_ret" not in dbg:
        mix_ret(ctx, l, rv, rvb)
    if "no_fox" not in dbg:
        mix_fox(ctx, l, rv, rvb)
    if "no_dsa" not in dbg:
        mix_dsa(ctx, l, rv, rvb)
    if "no_ssd" not in dbg:
        mix_ssd(ctx, l, cv, cvb, rv, rvb)


def _store_oT(ctx, o_ap, obuf, nch, row0, tt, bank, stgr):
    op, dma, tr, bf, evac_copy = ctx["op"], ctx["dma"], ctx["tr"], ctx["bf"], ctx["evac_copy"]
    ident_b = ctx["cstbf"][:, K_ID:K_ID + 128]
    pb, pbb = bank
    pbv = bf(pb)
    for c in range(nch):
        tr(pbv[:, c * 128:(c + 1) * 128], o_ap[:, c * 128:(c + 1) * 128], ident_b, [obuf, ctx["cstbfb"]], [pbb])
    st, stb = stgr.next()
    evac_copy(st[:, 0:nch * 128], pbv[:, 0:nch * 128], [pbb], [stb])
    dma("sp", ctx["OT"][row0:row0 + nch * 128, tt * 128:(tt + 1) * 128].rearrange("(c p) t -> p c t", p=128),
        st[:, 0:nch * 128].rearrange("p (c t) -> p c t", c=nch), reads=[stb])


def _interleave(gens):
    gens = list(gens)
    while gens:
        for g in list(gens):
            try:
                next(g)
            except StopIteration:
                gens.remove(g)


def _bc_last(ap2, n):
    return ap2.unsqueeze(2).to_broadcast([ap2.shape[0], ap2.shape[1], n])


def _bc_mid(ap2, n):
    return ap2.unsqueeze(1).to_broadcast([ap2.shape[0], n, ap2.shape[1]])


def _v3(ap2, a):
    return ap2.rearrange("p (a b) -> p a b", a=a)


def mix_ret(ctx, l, rv, rvb):
    nc, sch, op, dma, sb, rot, PSB, bf = (ctx[k] for k in ("nc", "sch", "op", "dma", "sb", "rot", "PSB", "bf"))
    mm, tr, evac_copy = ctx["mm"], ctx["tr"], ctx["evac_copy"]
    cst, cstb, cstbf, cstbfb = ctx["cst"], ctx["cstb"], ctx["cstbf"], ctx["cstbfb"]
    ident_b = cstbf[:, K_ID:K_ID + 128]
    PT = ctx["PT"]
    with contextlib.ExitStack() as es:
        rt, rtb = sb(es, "rt", [128, NT, 256], F32)
        dma("sp", rt[:], ctx["rot_d"].rearrange("(t p) c -> p t c", p=128), writes=[rtb])
        Sf, Sfb = sb(es, "Sf", [128, 4, 128], F32)
        Sb, Sbb = sb(es, "Sb", [128, 4, 128], BF16)
        op("dve", lambda e: e.memset(Sf[:], 0.0), writes=[Sfb])
        op("dve", lambda e: e.memset(Sb[:], 0.0), writes=[Sbb])
        ptr = rot(es, "rpt", [128, 2048], BF16, 2)
        tmp = [sb(es, f"rtmp{i}", [128, 4, 64], F32) for i in range(4)]
        qr, qrb = sb(es, "qr", [128, 4, 128], BF16)
        kr, krb = sb(es, "kr", [128, 4, 128], BF16)
        qd, qdb = sb(es, "qd", [128, 4, 128], BF16)
        vd, vdb = sb(es, "vd", [128, 4, 128], BF16)
        qkT, qkTb = sb(es, "qkT", [128, 8, 128], BF16)
        qdT, qdTb = sb(es, "qdT", [128, 4, 128], BF16)
        sm, smb = sb(es, "sm", [128, 512], BF16)
        ysb, ysbb = sb(es, "ysb", [128, 4, 128], F32)
        ysq, ysqb = sb(es, "ysq", [128, 4, 128], F32)
        st4 = [sb(es, f"rst{i}", [128, 4], F32) for i in range(5)]
        sg, sgb = sb(es, "sg", [128, 512], F32)
        o, ob = sb(es, "oret", [128, 512], BF16)
        stgr = rot(es, "rstg", [128, 1024], BF16, 2)
        for c in range(NT):
            pt, ptb = ptr.next()
            dma("sp", pt[:], PT[c * 128:(c + 1) * 128, 0:2048], writes=[ptb])
            for (c0, ct, dst, dstb) in ((0, 0, qr, qrb), (512, 128, kr, krb)):
                x = _v3(pt[:, c0:c0 + 512], 4)
                x1, x2 = x[:, :, 0:64], x[:, :, 64:128]
                cosb = _bc_mid(rt[:, c, ct:ct + 64], 4)
                sinb = _bc_mid(rt[:, c, ct + 64:ct + 128], 4)
                (t0, t0b), (t1, t1b), (t2, t2b), (t3, t3b) = tmp
                op("dve", lambda e, t0=t0, x1=x1, cosb=cosb: e.tensor_tensor(t0[:], x1, cosb, ALU.mult), reads=[ptb, rtb], writes=[t0b])
                op("dve", lambda e, t1=t1, x2=x2, sinb=sinb: e.tensor_tensor(t1[:], x2, sinb, ALU.mult), reads=[ptb, rtb], writes=[t1b])
                op("dve", lambda e, t2=t2, x1=x1, sinb=sinb: e.tensor_tensor(t2[:], x1, sinb, ALU.mult), reads=[ptb, rtb], writes=[t2b])
                op("dve", lambda e, t3=t3, x2=x2, cosb=cosb: e.tensor_tensor(t3[:], x2, cosb, ALU.mult), reads=[ptb, rtb], writes=[t3b])
                op("dve", lambda e, dst=dst, t0=t0, t1=t1: e.tensor_tensor(dst[:, :, 0:64], t0[:], t1[:], ALU.subtract), reads=[t0b, t1b], writes=[dstb])
                op("dve", lambda e, dst=dst, t2=t2, t3=t3: e.tensor_tensor(dst[:, :, 64:128], t2[:], t3[:], ALU.add), reads=[t2b, t3b], writes=[dstb])
            op("dve", lambda e: e.tensor_tensor(qd[:], qr[:], _bc_last(cst[:, K_QDEC:K_QDEC + 4], 128), ALU.mult),
               reads=[qrb, cstb], writes=[qdb])
            op("dve", lambda e, pt=pt: e.tensor_tensor(vd[:], _v3(pt[:, 1024:1536], 4), _bc_last(cst[:, K_KDEC:K_KDEC + 4], 128), ALU.mult),
               reads=[ptb, cstb], writes=[vdb])
            pA, pAb = PSB[0]
            pB, pBb = PSB[1]
            pAv, pBv = bf(pA), bf(pB)
            for h in range(4):
                tr(pAv[:, h * 128:(h + 1) * 128], qr[:, h, :], ident_b, [qrb, cstbfb], [pAb])
                tr(pAv[:, (4 + h) * 128:(5 + h) * 128], kr[:, h, :], ident_b, [krb, cstbfb], [pAb])
                tr(pBv[:, h * 128:(h + 1) * 128], qd[:, h, :], ident_b, [qdb, cstbfb], [pBb])
            op("act", lambda e: e.activation(out=qkT[:].rearrange("p a b -> p (a b)"), in_=pAv[:, :], func=AF.Copy),
               reads=[pAb], writes=[qkTb])
            op("dve", lambda e: e.tensor_copy(qdT[:].rearrange("p a b -> p (a b)"), pBv[:, 0:512]), reads=[pBb], writes=[qdTb])
            pC, pCb = PSB[2]
            for h in range(4):
                mm(pC[:, h * 128:(h + 1) * 128], qkT[:, 4 + h, :], qkT[:, h, :], True, True, [qkTb], [pCb])
            op("dve", lambda e: e.tensor_tensor(sm[:], pC[:, :], cst[:, K_DEC:K_DEC + 512], ALU.mult), reads=[pCb, cstb], writes=[smb])
            pD, pDb = PSB[3]
            for h in range(4):
                hs = slice(h * 128, (h + 1) * 128)
                mm(pD[:, hs], sm[:, hs], pt[:, 1024 + h * 128:1024 + (h + 1) * 128], True, False, [smb, ptb], [pDb])
                mm(pD[:, hs], qdT[:, h, :], Sb[:, h, :], False, True, [qdTb, Sbb], [pDb])
            pE, pEb = PSB[4]
            for h in range(4):
                mm(pE[:, h * 128:(h + 1) * 128], kr[:, h, :], vd[:, h, :], True, True, [krb, vdb], [pEb])
            op("dve", lambda e: e.tensor_tensor(Sf[:], Sf[:], _bc_last(cst[:, K_CDEC:K_CDEC + 4], 128), ALU.mult), reads=[Sfb, cstb], writes=[Sfb])
            op("dve", lambda e: e.tensor_tensor(Sf[:], Sf[:], _v3(pE[:, :], 4), ALU.add), reads=[Sfb, pEb], writes=[Sfb])
            op("act", lambda e: e.activation(out=Sb[:], in_=Sf[:], func=AF.Copy), reads=[Sfb], writes=[Sbb])
            (s1, s1b), (s2, s2b), (mean, meanb), (msq, msqb), (rstd, rstdb) = st4
            op("act", lambda e: e.activation(out=ysb[:], in_=_v3(pD[:, :], 4), func=AF.Copy), reads=[pDb], writes=[ysbb])
            op("dve", lambda e: e.tensor_reduce(out=s1[:], in_=ysb[:], axis=AX.X, op=ALU.add), reads=[ysbb], writes=[s1b])
            op("dve", lambda e: e.tensor_tensor(ysq[:], ysb[:], ysb[:], ALU.mult), reads=[ysbb], writes=[ysqb])
            op("dve", lambda e: e.tensor_reduce(out=s2[:], in_=ysq[:], axis=AX.X, op=ALU.add), reads=[ysqb], writes=[s2b])
            op("dve", lambda e: e.tensor_scalar(mean[:], s1[:], 1.0 / 128, None, ALU.mult), reads=[s1b], writes=[meanb])
            op("dve", lambda e: e.tensor_tensor(msq[:], mean[:], mean[:], ALU.mult), reads=[meanb], writes=[msqb])
            op("dve", lambda e: e.scalar_tensor_tensor(out=rstd[:], in0=s2[:], scalar=1.0 / 128, in1=msq[:], op0=ALU.mult, op1=ALU.subtract),
               reads=[s2b, msqb], writes=[rstdb])
            op("act", lambda e: e.activation(out=rstd[:], in_=rstd[:], func=AF.Sqrt, bias=EPS, scale=1.0), reads=[rstdb], writes=[rstdb])
            op("dve", lambda e: e.reciprocal(rstd[:], rstd[:]), reads=[rstdb], writes=[rstdb])
            op("dve", lambda e: e.tensor_tensor(ysb[:], ysb[:], _bc_last(mean[:], 128), ALU.subtract), reads=[ysbb, meanb], writes=[ysbb])
            op("dve", lambda e: e.tensor_tensor(ysb[:], ysb[:], _bc_last(rstd[:], 128), ALU.mult), reads=[ysbb, rstdb], writes=[ysbb])
            op("act", lambda e, pt=pt: e.activation(out=sg[:], in_=pt[:, 1536:2048], func=AF.Silu), reads=[ptb], writes=[sgb])
            op("dve", lambda e: e.tensor_tensor(o[:], ysb[:].rearrange("p a b -> p (a b)"), sg[:], ALU.mult), reads=[ysbb, sgb], writes=[ob])
            _store_oT(ctx, o[:], ob, 4, 0, c, PSB[5], stgr)
        sch.barrier()


def mix_fox(ctx, l, rv, rvb):
    nc, sch, op, dma, sb, rot, PSB, bf = (ctx[k] for k in ("nc", "sch", "op", "dma", "sb", "rot", "PSB", "bf"))
    mm, tr, evac_copy = ctx["mm"], ctx["tr"], ctx["evac_copy"]
    cst, cstb, cstbf, cstbfb = ctx["cst"], ctx["cstb"], ctx["cstbf"], ctx["cstbfb"]
    ident_b = cstbf[:, K_ID:K_ID + 128]
    ident_f = cst[:, K_ID:K_ID + 128]
    mneg_b = cstbf[:, K_MNEG:K_MNEG + 128]
    tri_f = cst[:, K_TRI:K_TRI + 128]
    ones_f = cst[:, K_ONES:K_ONES + 128]
    PT = ctx["PT"]
    with contextlib.ExitStack() as es:
        qT, qTb = sb(es, "fqT", [128, 4, S], BF16)
        kT, kTb = sb(es, "fkT", [128, 4, S], BF16)
        Vp, Vpb = sb(es, "fVp", [128, NT, 4, 129], BF16)
        nlf, nlfb = sb(es, "nlf", [128, NT, 4], F32)
        G, Gb = sb(es, "fG", [128, NT, 4], F32)
        nGT, nGTb = sb(es, "nGT", [128, S], F32)
        of, ofb = sb(es, "ofox", [128, NT, 512], BF16)
        gq, gqb = sb(es, "fgq", [128, 128], F32)
        carry, carryb = sb(es, "fcarry", [128, 4], F32)
        op("dve", lambda e: e.tensor_scalar(gq[:], rv[:, R_FQG:R_FQG + 128], 128 ** -0.5, None, ALU.mult), reads=[rvb], writes=[gqb])
        op("dve", lambda e: e.memset(Vp[:, :, :, 128:129], 1.0), writes=[Vpb])
        op("dve", lambda e: e.memset(carry[:], 0.0), writes=[carryb])
        W = 4
        lanes = [dict(pt=sb(es, f"fpt{k}", [128, 1540], BF16), sq=sb(es, f"fsq{k}", [128, 512], F32),
                      xn=sb(es, f"fxn{k}", [128, 512], BF16), ss=sb(es, f"fss{k}", [128, 4], F32),
                      z=sb(es, f"fz{k}", [128, 4], F32), pb=PSB[k]) for k in range(W)]
        pbanks = Rot(PSB[0:2])

        def stepA(tt, ln):
            (pt, ptb), (sq, sqb), (xn, xnb), (ss, ssb), (z, zb), (pb, pbb) = ln["pt"], ln["sq"], ln["xn"], ln["ss"], ln["z"], ln["pb"]
            dma("sp", pt[:], PT[tt * 128:(tt + 1) * 128, C_FQ:C_FQ + 1540], writes=[ptb])
            yield
            for (c0, gain, gainb, dstT, dstTb) in ((0, gq[:], gqb, qT, qTb), (512, rv[:, R_FKG:R_FKG + 128], rvb, kT, kTb)):
                x = pt[:, c0:c0 + 512]
                op("dve", lambda e: e.tensor_tensor(sq[:], x, x, ALU.mult), reads=[ptb], writes=[sqb])
                yield
                op("dve", lambda e: e.tensor_reduce(out=ss[:], in_=_v3(sq[:], 4), axis=AX.X, op=ALU.add), reads=[sqb], writes=[ssb])
                yield
                op("act", lambda e: e.activation(out=ss[:], in_=ss[:], func=AF.Sqrt, bias=EPS, scale=1.0 / 128), reads=[ssb], writes=[ssb])
                yield
                op("dve", lambda e: e.reciprocal(ss[:], ss[:]), reads=[ssb], writes=[ssb])
                yield
                op("dve", lambda e: e.tensor_tensor(_v3(sq[:], 4), _v3(x, 4), _bc_last(ss[:], 128), ALU.mult), reads=[ptb, ssb], writes=[sqb])
                yield
                op("dve", lambda e: e.tensor_tensor(_v3(xn[:], 4), _v3(sq[:], 4), _bc_mid(gain, 4), ALU.mult),
                   reads=[sqb, gainb], writes=[xnb])
                yield
                pbv = bf(pb)
                for h in range(4):
                    tr(pbv[:, h * 128:(h + 1) * 128], xn[:, h * 128:(h + 1) * 128], ident_b, [xnb, cstbfb], [pbb])
                yield
                evac_copy(dstT[:, :, tt * 128:(tt + 1) * 128], _v3(pbv[:, 0:512], 4), [pbb], [dstTb])
                yield
            op("act", lambda e: e.activation(out=Vp[:, tt, :, 0:128], in_=_v3(pt[:, 1024:1536], 4), func=AF.Copy),
               reads=[ptb], writes=[Vpb])
            yield
            op("dve", lambda e: e.tensor_tensor(z[:], pt[:, 1536:1540], rv[:, R_FB:R_FB + 4], ALU.add), reads=[ptb, rvb], writes=[zb])
            yield
            op("act", lambda e: e.activation(out=z[:], in_=z[:], func=AF.Exp, scale=-1.0), reads=[zb], writes=[zb])
            yield
            op("act", lambda e: e.activation(out=nlf[:, tt, :], in_=z[:], func=AF.Ln, bias=1.0, scale=1.0), reads=[zb], writes=[nlfb])
            yield
        for t0 in range(0, NT, W):
            _interleave([stepA(t0 + k, lanes[k]) for k in range(W)])
        for tt in range(NT):
            pb, pbb = pbanks.next()
            mm(pb[:, 0:4], tri_f, nlf[:, tt, :], True, True, [cstb, nlfb], [pbb])
            mm(pb[:, 4:8], ones_f, nlf[:, tt, :], True, True, [cstb, nlfb], [pbb])
            op("dve", lambda e, pb=pb, tt=tt: e.tensor_tensor(G[:, tt, :], pb[:, 0:4], carry[:], ALU.add), reads=[pbb, carryb], writes=[Gb])
            op("dve", lambda e, pb=pb: e.tensor_tensor(carry[:], pb[:, 4:8], carry[:], ALU.add), reads=[pbb, carryb], writes=[carryb])
            pb2, pb2b = pbanks.next()
            tr(pb2[0:4, 0:128], G[:, tt, :], ident_f, [Gb, cstb], [pb2b])
            op("dve", lambda e, pb2=pb2, tt=tt: e.tensor_scalar(nGT[0:4, tt * 128:(tt + 1) * 128], pb2[0:4, 0:128], -1.0, None, ALU.mult),
               reads=[pb2b], writes=[nGTb])
        sbanks = Rot(PSB[0:3])
        abanks = Rot([(PSB[3], PSB[4]), (PSB[5], PSB[6])])
        pTr = rot(es, "fpT", [128, 512], BF16, 3)
        rc_ = rot(es, "frc", [128, 1], F32, 2)
        for h in range(4):
            selh = cst[0:4, K_SEL + h * 128:K_SEL + (h + 1) * 128]
            for qg in range(4):
                ab = abanks.next()
                nj = 4 * qg + 4

                def acc(r):
                    t, b = ab[r // 2]
                    return t[:, (r % 2) * 256:(r % 2) * 256 + 129], b
                for j in range(nj):
                    r0 = max(0, j - 4 * qg)
                    ncol = (4 - r0) * 128
                    qc0 = qg * 512 + r0 * 128
                    sk, skb = sbanks.next()
                    mm(sk[:, 0:ncol], kT[:, h, j * 128:(j + 1) * 128], qT[:, h, qc0:qc0 + ncol], True, False, [kTb, qTb], [skb])
                    if j >= 4 * qg:
                        mm(sk[:, 0:128], ident_b, mneg_b, False, False, [cstbfb], [skb])
                    mm(sk[:, 0:ncol], selh, nGT[0:4, qc0:qc0 + ncol], False, True, [cstb, nGTb], [skb])
                    pT, pTb = pTr.next()
                    op("act", lambda e, pT=pT, sk=sk, ncol=ncol, j=j, h=h: e.activation(
                        out=pT[:, 0:ncol], in_=sk[:, 0:ncol], func=AF.Exp, bias=G[:, j, h:h + 1], scale=1.0),
                       reads=[skb, Gb], writes=[pTb])
                    for r in range(r0, 4):
                        i = 4 * qg + r
                        a_ap, a_b = acc(r)
                        mm(a_ap, pT[:, (r - r0) * 128:(r - r0 + 1) * 128], Vp[:, j, h, :], j == 0 and r % 2 == 0, j == i, [pTb, Vpb], [a_b])
                for r in range(4):
                    i = 4 * qg + r
                    a_ap, a_b = acc(r)
                    rc, rcb = rc_.next()
                    op("dve", lambda e, rc=rc, a_ap=a_ap: e.reciprocal(rc[:], a_ap[:, 128:129]), reads=[a_b], writes=[rcb])
                    op("dve", lambda e, rc=rc, a_ap=a_ap, i=i, h=h: e.tensor_scalar(
                        of[:, i, h * 128:(h + 1) * 128], a_ap[:, 0:128], rc[:, 0:1], None, ALU.mult), reads=[a_b, rcb], writes=[ofb])
        stgr = rot(es, "fstg", [128, 1024], BF16, 2)
        for tt in range(NT):
            _store_oT(ctx, of[:, tt, :], ofb, 4, 512, tt, PSB[7], stgr)
        sch.barrier()


def mix_dsa(ctx, l, rv, rvb):
    nc, sch, op, dma, sb, rot, PSB, bf = (ctx[k] for k in ("nc", "sch", "op", "dma", "sb", "rot", "PSB", "bf"))
    mm, tr, evac_copy = ctx["mm"], ctx["tr"], ctx["evac_copy"]
    cst, cstb, cstbf, cstbfb = ctx["cst"], ctx["cstb"], ctx["cstbf"], ctx["cstbfb"]
    biasT, biasTb = ctx["biasT"], ctx["biasTb"]
    ident_b = cstbf[:, K_ID:K_ID + 128]
    PT = ctx["PT"]
    TOPK = 256
    with contextlib.ExitStack() as es:
        kT, kTb = sb(es, "dkT", [128, S], BF16)
        Vp, Vpb = sb(es, "dVp", [128, NT, 129], BF16)
        wh, whb = sb(es, "dwh", [128, NT, 16], F32)
        qT, qTb = sb(es, "dqT", [128, NT, 4, 128], BF16)
        qiT, qiTb = sb(es, "qiT", [128, 8, S], BF16)
        kiT, kiTb = sb(es, "kiT", [128, S], BF16)
        od, odb = sb(es, "odsa", [128, NT, 512], BF16)
        gqd, gqdb = sb(es, "dgq", [128, 128], F32)
        thr0, thr0b = sb(es, "thr0", [128, 1], F32)
        id30k, id30kb = sb(es, "id30k", [128, 128], BF16)
        op("dve", lambda e: e.tensor_scalar(id30k[:], cst[:, K_ID:K_ID + 128], 30000.0, None, ALU.mult), reads=[cstb], writes=[id30kb])
        esA = contextlib.ExitStack()
        cqT, cqTb = sb(esA, "cqT", [128, 4, S], BF16)
        wuq, wuqb = sb(esA, "wuq", [128, 4, 512], BF16)
        wqi, wqib = sb(esA, "wqi", [128, 4, 1024], BF16)
        dma("pool", wuq[:], ctx["w_uq"][l].rearrange("(rc p) c -> p rc c", p=128), writes=[wuqb])
        dma("pool", wqi[:], ctx["w_qidx"][l].rearrange("(rc p) c -> p rc c", p=128), writes=[wqib])
        dma("sp", kiT[:], ctx["PFI"], writes=[kiTb])
        op("dve", lambda e: e.tensor_scalar(gqd[:], rv[:, R_DQG:R_DQG + 128], 128 ** -0.5, None, ALU.mult), reads=[rvb], writes=[gqdb])
        op("dve", lambda e: e.memset(Vp[:, :, 128:129], 1.0), writes=[Vpb])
        op("dve", lambda e: e.memset(thr0[:], -1e29), writes=[thr0b])
        ptr = rot(esA, "dpt", [128, 848], BF16, 2)
        sq, sqb = sb(esA, "dsq", [128, 512], F32)
        xn, xnb = sb(esA, "dxn", [128, 512], BF16)
        ss, ssb = sb(esA, "dss", [128, 4], F32)
        pbanks = Rot(PSB[0:4])
        for tt in range(NT):
            pt, ptb = ptr.next()
            dma("sp", pt[:], PT[tt * 128:(tt + 1) * 128, C_DCQ:C_DCQ + 848], writes=[ptb])
            for (c0, w, gain, dstf) in ((0, 512, rv[:, R_CQG:R_CQG + 512], "cq"), (512, 128, rv[:, R_DKG:R_DKG + 128], "k")):
                x = pt[:, c0:c0 + w]
                op("dve", lambda e, x=x, w=w: e.tensor_tensor(sq[:, 0:w], x, x, ALU.mult), reads=[ptb], writes=[sqb])
                op("dve", lambda e, w=w: e.tensor_reduce(out=ss[:, 0:1], in_=sq[:, 0:w], axis=AX.X, op=ALU.add), reads=[sqb], writes=[ssb])
                op("act", lambda e, w=w: e.activation(out=ss[:, 0:1], in_=ss[:, 0:1], func=AF.Sqrt, bias=EPS, scale=1.0 / w), reads=[ssb], writes=[ssb])
                op("dve", lambda e: e.reciprocal(ss[:, 0:1], ss[:, 0:1]), reads=[ssb], writes=[ssb])
                op("dve", lambda e, x=x, w=w: e.tensor_scalar(sq[:, 0:w], x, ss[:, 0:1], None, ALU.mult), reads=[ptb, ssb], writes=[sqb])
                op("dve", lambda e, w=w, gain=gain: e.tensor_tensor(xn[:, 0:w], sq[:, 0:w], gain, ALU.mult), reads=[sqb, rvb], writes=[xnb])
                pb, pbb = pbanks.next()
                pbv = bf(pb)
                nch = w // 128
                for c in range(nch):
                    tr(pbv[:, c * 128:(c + 1) * 128], xn[:, c * 128:(c + 1) * 128], ident_b, [xnb, cstbfb], [pbb])
                if dstf == "cq":
                    evac_copy(cqT[:, :, tt * 128:(tt + 1) * 128], _v3(pbv[:, 0:512], 4), [pbb], [cqTb])
                else:
                    evac_copy(kT[:, tt * 128:(tt + 1) * 128], pbv[:, 0:128], [pbb], [kTb])
            op("act", lambda e, pt=pt, tt=tt: e.activation(out=Vp[:, tt, 0:128], in_=pt[:, 640:768], func=AF.Copy), reads=[ptb], writes=[Vpb])
            op("dve", lambda e, pt=pt, tt=tt: e.tensor_scalar(wh[:, tt, :], pt[:, 832:848], 0.25 * 0.125, None, ALU.mult), reads=[ptb], writes=[whb])
        qs, qsb = sb(esA, "dqs", [128, 512], F32)
        for tt in range(NT):
            pb, pbb = pbanks.next()
            for rc in range(4):
                mm(pb[:, :], cqT[:, rc, tt * 128:(tt + 1) * 128], wuq[:, rc, :], rc == 0, rc == 3, [cqTb, wuqb], [pbb])
            op("act", lambda e, pb=pb: e.activation(out=qs[:], in_=pb[:, :], func=AF.Copy), reads=[pbb], writes=[qsb])
            op("dve", lambda e: e.tensor_tensor(sq[:], qs[:], qs[:], ALU.mult), reads=[qsb], writes=[sqb])
            op("dve", lambda e: e.tensor_reduce(out=ss[:], in_=_v3(sq[:], 4), axis=AX.X, op=ALU.add), reads=[sqb], writes=[ssb])
            op("act", lambda e: e.activation(out=ss[:], in_=ss[:], func=AF.Sqrt, bias=EPS, scale=1.0 / 128), reads=[ssb], writes=[ssb])
            op("dve", lambda e: e.reciprocal(ss[:], ss[:]), reads=[ssb], writes=[ssb])
            op("dve", lambda e: e.tensor_tensor(_v3(sq[:], 4), _v3(qs[:], 4), _bc_last(ss[:], 128), ALU.mult), reads=[qsb, ssb], writes=[sqb])
            op("dve", lambda e: e.tensor_tensor(_v3(xn[:], 4), _v3(sq[:], 4), _bc_mid(gqd[:], 4), ALU.mult), reads=[sqb, gqdb], writes=[xnb])
            pb2, pb2b = pbanks.next()
            pbv = bf(pb2)
            for h in range(4):
                tr(pbv[:, h * 128:(h + 1) * 128], xn[:, h * 128:(h + 1) * 128], ident_b, [xnb, cstbfb], [pb2b])
            evac_copy(qT[:, tt, :, :].rearrange("p a b -> p (a b)"), pbv[:, 0:512], [pb2b], [qTb])
        for ch in range(8):
            for tg in range(4):
                pb, pbb = pbanks.next()
                for rc in range(4):
                    mm(pb[:, :], wqi[:, rc, ch * 128:(ch + 1) * 128], cqT[:, rc, tg * 512:(tg + 1) * 512], rc == 0, rc == 3, [wqib, cqTb], [pbb])
                evac_copy(qiT[:, ch, tg * 512:(tg + 1) * 512], pb[:, :], [pbb], [qiTb])
        sch.barrier()
        esA.close()
        I4 = [sb(es, f"dI{k}", [128, S], F32, nb=4) for k in range(4)]
        Dg2 = [sb(es, f"dDg{k}", [128, 16, 128], BF16) for k in range(2)]
        lanes = [dict(work=sb(es, f"dwork{k}", [128, S], F32), m8=sb(es, f"dm8{k}", [128, 8], F32),
                      thr=sb(es, f"dthr{k}", [128, 1], F32), selm=sb(es, f"dsel{k}", [128, S], BF16),
                      mT=sb(es, f"dmT{k}", [128, NT, 128], BF16), accs=sb(es, f"daccs{k}", [128, 4, 129], F32),
                      rc=sb(es, f"drc{k}", [128, 4], F32)) for k in range(2)]
        rr = rot(es, "drr", [128, 512], BF16, 3)
        pr = rot(es, "dpr", [128, 4, 128], BF16, 3)
        ibanks = Rot(PSB[0:2])
        iacc = [PSB[2], PSB[3]]
        tbank = PSB[4]
        sbank = PSB[5]
        A0, A1 = PSB[6], PSB[7]

        def acc(h):
            t, b = (A0, A1)[h // 2]
            return t[:, (h % 2) * 256:(h % 2) * 256 + 129], b

        def idx_scores(i):
            I, Ib = I4[i % 4]
            Dg, Dgb = Dg2[i % 2]
            nk = (i + 1) * 128
            ng = (nk + 511) // 512
            op("dve", lambda e: e.tensor_tensor(Dg[:], _bc_mid(cst[:, K_ID:K_ID + 128], 16), _bc_last(wh[:, i, :], 128), ALU.mult),
               reads=[cstb, whb], writes=[Dgb])
            for gp in range(0, ng, 2):
                gs = list(range(gp, min(gp + 2, ng)))
                items = [(hh, g) for hh in range(16) for g in gs]

                def score(k):
                    hh, g = items[k]
                    pp = (hh % 2) * 64
                    ncol = min(512, nk - g * 512)
                    ib, ibb = ibanks.next()
                    mm(ib[:, 0:ncol], qiT[pp:pp + 64, hh // 2, i * 128:(i + 1) * 128], kiT[pp:pp + 64, g * 512:g * 512 + ncol], True, True, [qiTb, kiTb], [ibb])
                    return ib, ibb
                pend = [score(0)]
                for k, (hh, g) in enumerate(items):
                    ncol = min(512, nk - g * 512)
                    ib, ibb = pend.pop(0)
                    r, rb_ = rr.next()
                    op("act", lambda e, r=r, ib=ib, ncol=ncol: e.activation(out=r[:, 0:ncol], in_=ib[:, 0:ncol], func=AF.Relu), reads=[ibb], writes=[rb_])
                    if k + 1 < len(items):
                        pend.append(score(k + 1))
                    ab_, abb_ = iacc[g - gp]
                    last = hh == 15 and g != i // 4
                    mm(ab_[:, 0:ncol], Dg[:, hh, :], r[:, 0:ncol], hh == 0, last, [Dgb, rb_], [abb_])
                    if hh == 15 and g == i // 4:
                        dc0 = i * 128 - g * 512
                        mm(ab_[:, dc0:dc0 + 128], ident_b, cstbf[:, K_MBIG:K_MBIG + 128], False, True, [cstbfb], [abb_])
                for g in gs:
                    ncol = min(512, nk - g * 512)
                    cs = slice(g * 512, g * 512 + ncol)
                    ab_, abb_ = iacc[g - gp]
                    op("act", lambda e, ab_=ab_, ncol=ncol, cs=cs: e.activation(out=I[:, cs], in_=ab_[:, 0:ncol], func=AF.Copy), reads=[abb_], writes=[Ib[g]])

        def gen_select(i, ln):
            I, Ibl = I4[i % 4]
            nk = (i + 1) * 128
            Ib = Ibl[0:(nk + 511) // 512]
            (work, workb), (m8, m8b), (thr, thrb), (selm, selmb), (mT, mTb) = ln["work"], ln["m8"], ln["thr"], ln["selm"], ln["mT"]
            if i >= 2 and "dsa_notopk" not in ctx["dbg"]:
                nround = TOPK // 8
                for rd in range(nround):
                    src = I if rd == 0 else work
                    srcb = Ib if rd == 0 else [workb]
                    op("dve", lambda e, src=src: e.max(out=m8[:], in_=src[:, 0:nk]), reads=srcb, writes=[m8b])
                    yield
                    if rd < nround - 1:
                        op("dve", lambda e, src=src: e.match_replace(out=work[:, 0:nk], in_to_replace=m8[:], in_values=src[:, 0:nk], imm_value=-1e30),
                           reads=srcb + [m8b], writes=[workb])
                        yield
                op("dve", lambda e: e.tensor_reduce(out=thr[:], in_=m8[:], axis=AX.X, op=ALU.min), reads=[m8b], writes=[thrb])
                yield
                th, thb = thr, thrb
            else:
                th, thb = thr0, thr0b
            op("dve", lambda e: e.tensor_scalar(selm[:, 0:nk], I[:, 0:nk], th[:, 0:1], 1.0, ALU.is_ge, ALU.subtract),
               reads=Ib + [thb], writes=[selmb])
            yield
            tb, tbb = tbank
            tbv = bf(tb)
            for j0 in range(0, i + 1, 8):
                nb = min(8, i + 1 - j0)
                for j in range(j0, j0 + nb):
                    tr(tbv[:, (j - j0) * 128:(j - j0 + 1) * 128], selm[:, j * 128:(j + 1) * 128], ident_b, [selmb, cstbfb], [tbb])
                op("act", lambda e, j0=j0, nb=nb: e.activation(out=mT[:, j0:j0 + nb, :].rearrange("p a b -> p (a b)"), in_=tbv[:, 0:nb * 128], func=AF.Copy),
                   reads=[tbb], writes=[mTb])
                yield

        def attend(i, ln):
            (mT, mTb), (accs, accsb) = ln["mT"], ln["accs"]
            sk, skb = sbank
            for j in range(i + 1):
                ty = 0 if j == i else (1 if j == i - 1 else 2)
                mm(sk[:, :], kT[:, j * 128:(j + 1) * 128], qT[:, i, :, :].rearrange("p a b -> p (a b)"), True, False, [kTb, qTb], [skb])
                mm(sk[:, :], ident_b, biasT[:, ty, :, :].rearrange("p a b -> p (a b)"), False, False, [cstbfb, biasTb], [skb])
                for h in range(4):
                    mm(sk[:, h * 128:(h + 1) * 128], id30k[:], mT[:, j, :], False, h == 3, [id30kb, mTb], [skb])
                pT, pTb = pr.next()
                op("act", lambda e, pT=pT: e.activation(out=pT[:].rearrange("p a b -> p (a b)"), in_=sk[:, :], func=AF.Exp),
                   reads=[skb], writes=[pTb])
                for h in range(4):
                    a_ap, a_b = acc(h)
                    mm(a_ap, pT[:, h, :], Vp[:, j, :], j == 0 and h % 2 == 0, j == i, [pTb, Vpb], [a_b])
            for k, (at, ab_) in enumerate((A0, A1)):
                op("act", lambda e, k=k, at=at: e.activation(out=accs[:, 2 * k:2 * k + 2, :], in_=_v3(at[:, 0:512], 2)[:, :, 0:129], func=AF.Copy),
                   reads=[ab_], writes=[accsb])

        def finalize(i, ln):
            (accs, accsb), (rc, rcb) = ln["accs"], ln["rc"]
            op("dve", lambda e: e.reciprocal(rc[:], accs[:, :, 128:129].rearrange("p a b -> p (a b)")), reads=[accsb], writes=[rcb])
            op("dve", lambda e: e.tensor_tensor(_v3(od[:, i, :], 4), accs[:, :, 0:128], _bc_last(rc[:], 128), ALU.mult),
               reads=[accsb, rcb], writes=[odb])

        idx_scores(0)
        idx_scores(1)
        for p in range(0, NT, 2):
            if p + 2 < NT:
                idx_scores(p + 2)
                idx_scores(p + 3)
            _interleave([gen_select(p, lanes[0]), gen_select(p + 1, lanes[1])])
            if p >= 2:
                finalize(p - 2, lanes[0])
                finalize(p - 1, lanes[1])
            attend(p, lanes[0])
            attend(p + 1, lanes[1])
        finalize(NT - 2, lanes[0])
        finalize(NT - 1, lanes[1])
        stgr = rot(es, "dstg", [128, 1024], BF16, 2)
        for tt in range(NT):
            _store_oT(ctx, od[:, tt, :], odb, 4, 1024, tt, PSB[2], stgr)
        sch.barrier()


def mix_ssd(ctx, l, cv, cvb, rv, rvb):
    nc, sch, op, dma, sb, rot, PSB, bf = (ctx[k] for k in ("nc", "sch", "op", "dma", "sb", "rot", "PSB", "bf"))
    mm, tr, evac_copy = ctx["mm"], ctx["tr"], ctx["evac_copy"]
    cst, cstb, cstbf, cstbfb = ctx["cst"], ctx["cstb"], ctx["cstbf"], ctx["cstbfb"]
    ident_b = cstbf[:, K_ID:K_ID + 128]
    tri_f = cst[:, K_TRI:K_TRI + 128]
    su_f = cst[:, K_SU:K_SU + 128]
    ones_f = cst[:, K_ONES:K_ONES + 128]
    PT, PFX = ctx["PT"], ctx["PFX"]
    with contextlib.ExitStack() as es:
        xsT, xsTb = sb(es, "xsT", [128, NT, 1024], BF16)
        BT, BTb = sb(es, "sBT", [128, 2, S], BF16)
        CT, CTb = sb(es, "sCT", [128, 2, S], BF16)
        Btm, Btmb = sb(es, "sBtm", [128, NT, 2, 128], BF16)
        dt, dtb = sb(es, "sdt", [128, NT, 16], F32)
        aa, aab = sb(es, "saa", [128, NT, 16], F32)
        Abc, Abcb = sb(es, "sAbc", [128, 16], F32)
        op("act", lambda e: e.activation(out=Abc[:], in_=rv[:, R_ALOG:R_ALOG + 16], func=AF.Exp), reads=[rvb], writes=[Abcb])
        op("dve", lambda e: e.tensor_scalar(Abc[:], Abc[:], -1.0, None, ALU.mult), reads=[Abcb], writes=[Abcb])
        tbanks = Rot(PSB[0:4])
        with contextlib.ExitStack() as es2:
            xin_ = rot(es2, "sxin", [128, 3 + S], BF16, 2)
            for xin, xinb in xin_.items:
                op("dve", lambda e, xin=xin: e.memset(xin[:, 0:3], 0.0), writes=[xinb])
            acc_ = rot(es2, "sacc", [128, S], F32, 2)
            cvd_ = rot(es2, "scvd", [128, S], BF16, 2)
            for ch in range(12):
                xin, xinb = xin_.next()
                dma("sp", xin[:, 3:3 + S], PFX[ch * 128:(ch + 1) * 128, :], writes=[xinb])
                ac, acb = acc_.next()
                for k in range(4):
                    wcol = cv[:, V_CW + k * 12 + ch:V_CW + k * 12 + ch + 1]
                    if k == 0:
                        op("dve", lambda e, ac=ac, xin=xin, wcol=wcol: e.tensor_scalar(ac[:], xin[:, 0:S], wcol, None, ALU.mult),
                           reads=[xinb, cvb], writes=[acb])
                    else:
                        op("dve", lambda e, ac=ac, xin=xin, wcol=wcol, k=k: e.scalar_tensor_tensor(
                            out=ac[:], in0=xin[:, k:k + S], scalar=wcol, in1=ac[:], op0=ALU.mult, op1=ALU.add),
                           reads=[xinb, cvb, acb], writes=[acb])
                bcol = cv[:, V_CB + ch:V_CB + ch + 1]
                if ch < 8 or ch in (8, 9):
                    cd, cdb = cvd_.next()
                    if ch < 8:
                        op("act", lambda e, cd=cd, ac=ac, bcol=bcol: e.activation(out=cd[:], in_=ac[:], func=AF.Silu, bias=bcol, scale=1.0),
                           reads=[acb, cvb], writes=[cdb])
                        src, srcb = cd[:], cdb
                    else:
                        g = ch - 8
                        op("act", lambda e, g=g, ac=ac, bcol=bcol: e.activation(out=BT[:, g, :], in_=ac[:], func=AF.Silu, bias=bcol, scale=1.0),
                           reads=[acb, cvb], writes=[BTb])
                        src, srcb = BT[:, g, :], BTb
                    for t0 in range(0, NT, 8):
                        pb, pbb = tbanks.next()
                        pbv = bf(pb)
                        for tt in range(t0, t0 + 8):
                            tr(pbv[:, (tt - t0) * 128:(tt - t0 + 1) * 128], src[:, tt * 128:(tt + 1) * 128], ident_b, [srcb, cstbfb], [pbb])
                        if ch < 8:
                            evac_copy(xsT[:, t0:t0 + 8, ch * 128:(ch + 1) * 128], _v3(pbv[:, 0:1024], 8), [pbb], [xsTb])
                        else:
                            evac_copy(Btm[:, t0:t0 + 8, ch - 8, :], _v3(pbv[:, 0:1024], 8), [pbb], [Btmb])
                else:
                    g = ch - 10
                    op("act", lambda e, g=g, ac=ac, bcol=bcol: e.activation(out=CT[:, g, :], in_=ac[:], func=AF.Silu, bias=bcol, scale=1.0),
                       reads=[acb, cvb], writes=[CTb])
            dtT, dtTb = sb(es2, "sdtT", [128, S], BF16)
            dma("sp", dtT[0:16, :], PFX[1536:1552, :], writes=[dtTb])
            zz, zzb = sb(es2, "szz", [128, 16], F32)
            for tt in range(NT):
                pb, pbb = tbanks.next()
                pbv = bf(pb)
                tr(pbv[:, 0:16], dtT[0:16, tt * 128:(tt + 1) * 128], cstbf[0:16, K_ID:K_ID + 16], [dtTb, cstbfb], [pbb])
                op("dve", lambda e, pbv=pbv: e.tensor_tensor(zz[:], pbv[:, 0:16], rv[:, R_DTB:R_DTB + 16], ALU.add), reads=[pbb, rvb], writes=[zzb])
                op("act", lambda e: e.activation(out=zz[:], in_=zz[:], func=AF.Exp), reads=[zzb], writes=[zzb])
                op("act", lambda e, tt=tt: e.activation(out=dt[:, tt, :], in_=zz[:], func=AF.Ln, bias=1.0, scale=1.0), reads=[zzb], writes=[dtb])
            op("dve", lambda e: e.tensor_tensor(aa[:], dt[:], _bc_mid(Abc[:], NT), ALU.mult), reads=[dtb, Abcb], writes=[aab])
            sch.barrier()
        Sf, Sfb = sb(es, "sSf", [128, 2, 512], F32)
        Sb, Sbb = sb(es, "sSb", [128, 2, 512], BF16)
        op("dve", lambda e: e.memset(Sf[:], 0.0), writes=[Sfb])
        op("dve", lambda e: e.memset(Sb[:], 0.0), writes=[Sbb])
        ex3, ex3b = sb(es, "sex3", [128, 48], F32)
        R, Rb = sb(es, "sR", [128, 16, 128], F32)
        Lx, Lxb = sb(es, "sLx", [128, 16, 128], BF16)
        CBm, CBmb = sb(es, "sCBm", [128, 2, 128], BF16)
        MT, MTb = sb(es, "sMT", [128, 16, 128], BF16)
        xdt, xdtb = sb(es, "sxdt", [128, 16, 64], BF16)
        xdd, xddb = sb(es, "sxdd", [128, 16, 64], BF16)
        ysc, yscb = sb(es, "sysc", [128, 16, 64], F32)
        yy, yyb = sb(es, "syy", [128, 16, 64], F32)
        tmp, tmpb = sb(es, "stmp", [128, 16, 64], F32)
        ptz_ = rot(es, "sptz", [128, 1024], BF16, 2)
        sz, szb = sb(es, "ssz", [128, 1024], F32)
        ss, ssb = sb(es, "sss", [128, 2], F32)
        oo, oob = sb(es, "sso", [128, 1024], BF16)
        stgr = rot(es, "sstg", [128, 1024], BF16, 2)
        P = PSB
        for c in range(NT):
            cs = slice(c * 128, (c + 1) * 128)
            ptz, ptzb = ptz_.next()
            dma("sp", ptz[:], PT[cs, C_SZ:C_SZ + 1024], writes=[ptzb])
            p0, p0b = P[0]
            mm(p0[:, 0:16], tri_f, aa[:, c, :], True, True, [cstb, aab], [p0b])
            mm(p0[:, 16:32], su_f, aa[:, c, :], True, True, [cstb, aab], [p0b])
            mm(p0[:, 32:48], ones_f, aa[:, c, :], True, True, [cstb, aab], [p0b])
            op("act", lambda e: e.activation(out=ex3[:], in_=p0[:, 0:48], func=AF.Exp), reads=[p0b], writes=[ex3b])
            ea, edec, etot = ex3[:, 0:16], ex3[:, 16:32], ex3[:, 32:48]
            op("dve", lambda e, c=c: e.tensor_tensor(R[:], _bc_last(aa[:, c, :], 128), _bc_mid(tri_f, 16), ALU.mult), reads=[aab, cstb], writes=[Rb])
            for q in range(4):
                pq, pqb = P[1 + q]
                mm(pq[:, :], su_f, R[:, 4 * q:4 * q + 4, :].rearrange("p a b -> p (a b)"), True, True, [cstb, Rb], [pqb])
                op("act", lambda e, q=q, pq=pq: e.activation(out=Lx[:, 4 * q:4 * q + 4, :].rearrange("p a b -> p (a b)"), in_=pq[:, :], func=AF.Exp),
                   reads=[pqb], writes=[Lxb])
            p5, p5b = P[5]
            for g in range(2):
                mm(p5[:, g * 128:(g + 1) * 128], BT[:, g, cs], CT[:, g, cs], True, True, [BTb, CTb], [p5b])
            op("dve", lambda e: e.tensor_tensor(CBm[:], _v3(p5[:, 0:256], 2), _bc_mid(tri_f, 2), ALU.mult), reads=[p5b, cstb], writes=[CBmb])
            for g in range(2):
                op("dve", lambda e, g=g: e.tensor_tensor(MT[:, 8 * g:8 * g + 8, :], Lx[:, 8 * g:8 * g + 8, :], _bc_mid(CBm[:, g, :], 8), ALU.mult),
                   reads=[Lxb, CBmb], writes=[MTb])
            xs3 = _v3(xsT[:, c, :], 16)
            op("dve", lambda e, xs3=xs3, c=c: e.tensor_tensor(xdt[:], xs3, _bc_last(dt[:, c, :], 64), ALU.mult), reads=[xsTb, dtb], writes=[xdtb])
            op("dve", lambda e: e.tensor_tensor(xdd[:], xdt[:], _bc_last(edec, 64), ALU.mult), reads=[xdtb, ex3b], writes=[xddb])
            for h in range(16):
                py, pyb = P[6 + h // 8]
                mm(py[:, (h % 8) * 64:(h % 8 + 1) * 64], MT[:, h, :], xdt[:, h, :], True, True, [MTb, xdtb], [pyb])
            for g in range(2):
                pq, pqb = P[1 + g]
                mm(pq[:, :], CT[:, g, cs], Sb[:, g, :], True, True, [CTb, Sbb], [pqb])
                op("dve", lambda e, g=g, pq=pq: e.tensor_tensor(ysc[:, 8 * g:8 * g + 8, :], _v3(pq[:, :], 8), _bc_last(ea[:, 8 * g:8 * g + 8], 64), ALU.mult),
                   reads=[pqb, ex3b], writes=[yscb])
                py, pyb = P[6 + g]
                op("dve", lambda e, g=g, py=py: e.tensor_tensor(yy[:, 8 * g:8 * g + 8, :], ysc[:, 8 * g:8 * g + 8, :], _v3(py[:, :], 8), ALU.add),
                   reads=[yscb, pyb], writes=[yyb])
            op("dve", lambda e, xs3=xs3: e.tensor_tensor(tmp[:], xs3, _bc_last(rv[:, R_SD:R_SD + 16], 64), ALU.mult), reads=[xsTb, rvb], writes=[tmpb])
            op("dve", lambda e: e.tensor_tensor(yy[:], yy[:], tmp[:], ALU.add), reads=[yyb, tmpb], writes=[yyb])
            for g in range(2):
                pu, pub = P[3 + g]
                mm(pu[:, :], Btm[:, c, g, :], xdd[:, 8 * g:8 * g + 8, :].rearrange("p a b -> p (a b)"), True, True, [Btmb, xddb], [pub])
                op("dve", lambda e, g=g: e.tensor_tensor(_v3(Sf[:, g, :], 8), _v3(Sf[:, g, :], 8), _bc_last(etot[:, 8 * g:8 * g + 8], 64), ALU.mult),
                   reads=[Sfb, ex3b], writes=[Sfb])
                op("dve", lambda e, g=g, pu=pu: e.tensor_tensor(Sf[:, g, :], Sf[:, g, :], pu[:, :], ALU.add), reads=[Sfb, pub], writes=[Sfb])
            op("act", lambda e: e.activation(out=Sb[:], in_=Sf[:], func=AF.Copy), reads=[Sfb], writes=[Sbb])
            yf = yy[:].rearrange("p a b -> p (a b)")
            op("act", lambda e, ptz=ptz: e.activation(out=sz[:], in_=ptz[:], func=AF.Silu), reads=[ptzb], writes=[szb])
            op("dve", lambda e, yf=yf: e.tensor_tensor(yf, yf, sz[:], ALU.mult), reads=[yyb, szb], writes=[yyb])
            op("dve", lambda e, yf=yf: e.tensor_tensor(sz[:], yf, yf, ALU.mult), reads=[yyb], writes=[szb])
            op("dve", lambda e: e.tensor_reduce(out=ss[:], in_=_v3(sz[:], 2), axis=AX.X, op=ALU.add), reads=[szb], writes=[ssb])
            op("act", lambda e: e.activation(out=ss[:], in_=ss[:], func=AF.Sqrt, bias=EPS, scale=1.0 / 512), reads=[ssb], writes=[ssb])
            op("dve", lambda e: e.reciprocal(ss[:], ss[:]), reads=[ssb], writes=[ssb])
            op("dve", lambda e, yf=yf: e.tensor_tensor(_v3(sz[:], 2), _v3(yf, 2), _bc_last(ss[:], 512), ALU.mult), reads=[yyb, ssb], writes=[szb])
            op("dve", lambda e: e.tensor_tensor(oo[:], sz[:], rv[:, R_SNG:R_SNG + 1024], ALU.mult), reads=[szb, rvb], writes=[oob])
            _store_oT(ctx, oo[:], oob, 8, 1536, c, P[0], stgr)
        sch.barrier()


def _t5_bucket(d):
    d = np.maximum(d, 0)
    max_exact = 16
    lr = np.log(np.maximum(d, 1).astype(np.float32) / max_exact) / math.log(128 / max_exact)
    large = np.minimum(max_exact + (lr * 16).astype(np.int32), 31)
    return np.where(d < max_exact, d, large)


def host_consts():
    i = np.arange(128)
    cst = np.zeros((128, NCST), np.float32)
    cst[:, K_ID:K_ID + 128] = np.eye(128)
    cst[:, K_ONES:K_ONES + 128] = 1.0
    cst[:, K_TRI:K_TRI + 128] = (i[None, :] >= i[:, None])
    cst[:, K_MNEG:K_MNEG + 128] = np.where(i[None, :] >= i[:, None], 0.0, NEG)
    cst[:, K_SU:K_SU + 128] = (i[:, None] > i[None, :])
    lg = np.log1p(-np.exp2(-5.0 - np.arange(4, dtype=np.float64)))
    rel = (i[None, :] - i[:, None]).astype(np.float64)
    for h in range(4):
        cst[:, K_DEC + h * 128:K_DEC + (h + 1) * 128] = np.where(rel >= 0, np.exp(lg[h] * np.maximum(rel, 0)), 0.0)
        cst[:, K_QDEC + h] = np.exp(lg[h] * (i + 1.0))
        cst[:, K_KDEC + h] = np.exp(lg[h] * (127.0 - i))
        cst[:, K_CDEC + h] = np.exp(lg[h] * 128.0)
        cst[h, K_SEL + h * 128:K_SEL + (h + 1) * 128] = 1.0
    cst[:, K_MBIG:K_MBIG + 128] = np.where(i[None, :] <= i[:, None], 0.0, -1e30)
    oh = np.zeros((128, 2, 32, 128), np.float32)
    for ty in range(2):
        dist = i[None, :] - i[:, None] + 128 * ty
        bk = _t5_bucket(dist)
        for b in range(32):
            oh[:, ty, b, :] = ((bk == b) & (dist >= 0))
    oh = oh.reshape(128, 8192)
    pos = np.arange(S, dtype=np.float32)
    inv = (1.0 / (10000.0 ** (np.arange(64, dtype=np.float32) / 64))).astype(np.float32)
    ang = pos[:, None] * inv[None, :]
    rot = np.concatenate([np.cos(ang), np.sin(ang), np.cos(ang) * (128 ** -0.5), np.sin(ang) * (128 ** -0.5)], axis=1).astype(np.float32)
    return cst, oh, rot


def host_params(inp):
    colv = np.zeros((L, NV, 128), np.float32)
    rowv = np.zeros((L, 1, NR), np.float32)
    for l in range(L):
        colv[l, V_N1:V_N1 + 16] = inp["norm1_g"][l].reshape(16, 128)
        colv[l, V_N2:V_N2 + 16] = inp["norm2_g"][l].reshape(16, 128)
        colv[l, V_GB:V_GB + 64] = inp["gate_b"][l].reshape(64, 128)
        colv[l, V_CW:V_CW + 48] = inp["ssd_conv_w"][l].reshape(48, 128)
        colv[l, V_CB:V_CB + 12] = inp["ssd_conv_b"][l].reshape(12, 128)
        r = rowv[l, 0]
        r[R_FB:R_FB + 4] = inp["fox_f_b"][l]
        r[R_FQG:R_FQG + 128] = inp["fox_qn_g"][l]
        r[R_FKG:R_FKG + 128] = inp["fox_kn_g"][l]
        r[R_CQG:R_CQG + 512] = inp["dsa_cq_g"][l]
        r[R_DQG:R_DQG + 128] = inp["dsa_qn_g"][l]
        r[R_DKG:R_DKG + 128] = inp["dsa_kn_g"][l]
        r[R_DTB:R_DTB + 16] = inp["ssd_dt_bias"][l]
        r[R_ALOG:R_ALOG + 16] = inp["ssd_a_log"][l]
        r[R_SD:R_SD + 16] = inp["ssd_d"][l]
        r[R_SNG:R_SNG + 1024] = inp["ssd_norm_g"][l]
    relb = np.ascontiguousarray(inp["rel_bias"].reshape(1, 128)).astype(np.float32)
    return colv, rowv, relb


CORE_OF_BATCH = [0, 1, 4, 5]


def make_in_maps(inp, n_cores=8):
    cst, oh, rot = host_consts()
    colv, rowv, relb = host_params(inp)
    shared = dict(
        w_in=np.ascontiguousarray(inp["w_in"]), w_br=np.ascontiguousarray(inp["w_br"]),
        w_out=np.ascontiguousarray(inp["w_out"]), w_ff1=np.ascontiguousarray(inp["w_ff1"]),
        w_ff2=np.ascontiguousarray(inp["w_ff2"]), w_uq=np.ascontiguousarray(inp["dsa_w_uq"]),
        w_qidx=np.ascontiguousarray(inp["dsa_w_qidx"]), colv=colv, rowv=rowv, relb=relb, cst=cst, oh=oh, rot=rot)
    maps = []
    if n_cores == 8:
        zeros = {k: np.zeros_like(v) for k, v in shared.items()}
        zx = np.zeros((D, S), np.float32)
        for c in range(n_cores):
            if c in CORE_OF_BATCH:
                m = dict(shared)
                m["xT"] = np.ascontiguousarray(inp["x"][CORE_OF_BATCH.index(c)].T)
            else:
                m = dict(zeros)
                m["xT"] = zx
            maps.append(m)
        return maps
    for c in range(n_cores):
        m = dict(shared)
        m["xT"] = np.ascontiguousarray(inp["x"][c % 4].T)
        maps.append(m)
    return maps


def kernel(**inputs):
    inp = {k: np.asarray(v) for k, v in inputs.items()}
    nc = build(L)
    maps = make_in_maps(inp, 8)
    res = run_bass_kernel_spmd(nc, maps, core_ids=list(range(8)))
    out = np.stack([np.ascontiguousarray(res.results[CORE_OF_BATCH[b]]["yT"].T) for b in range(4)], axis=0)
    return out.astype(np.float32)
```
